# Optimizing a Trainium2 kernel written in Bass

```python
import math
import jax, jax.numpy as jnp
from jax import lax
import numpy as np

D_MODEL = 1024
BATCH = 4
SEQ = 8192
DEPTH = 4

D_FF = 2816
PLE_DIM = 256
CONV_WIDTH = 3
CONV_CH = 512
S5_CH = 512
S5_GROUP = 16
S5_GROUPS = S5_CH // S5_GROUP
S5_STATE = 64
AB_IN = 3 * CONV_CH + S5_CH
N_HEADS = 8
HEAD_DIM = 128
N_KV_HEADS = 2
IDX_HEADS = 8
IDX_DIM = 64
TOPK_MAX = 256
Q_BLOCK = 128
C_SIZES = (N_HEADS * HEAD_DIM, N_KV_HEADS * HEAD_DIM, N_KV_HEADS * HEAD_DIM,
           IDX_HEADS * IDX_DIM, IDX_DIM, IDX_HEADS)
C_SPLITS = tuple(int(v) for v in np.cumsum(C_SIZES)[:-1])
C_IN = int(sum(C_SIZES))
ROPE_THETA = 500000.0
ROT_FRAC = 4
LN_EPS = 1e-5
DN_ALPHA = (2 * DEPTH) ** 0.25
DN_BETA = (8 * DEPTH) ** -0.25
N_EVEN = (DEPTH + 1) // 2
N_ODD = DEPTH // 2

kernel_name = 'hybrid_conv_s5_dsa_macaron_deepnorm'


def layer_norm(x, g, b):
    xf = x.astype(jnp.float32)
    mu = jnp.mean(xf, axis=-1, keepdims=True)
    var = jnp.mean(jnp.square(xf - mu), axis=-1, keepdims=True)
    return ((xf - mu) * lax.rsqrt(var + LN_EPS) * g.astype(jnp.float32) + b.astype(jnp.float32)).astype(x.dtype)


def swiglu(x, w1, w3, w2):
    return (jax.nn.silu(x @ w1) * (x @ w3)) @ w2


def rope_tables(positions, rot_dim):
    inv = ROPE_THETA ** (-jnp.arange(0, rot_dim, 2, dtype=jnp.float32) / rot_dim)
    ang = positions.astype(jnp.float32)[..., None] * inv
    return jnp.cos(ang), jnp.sin(ang)


def partial_rope(t, cos, sin):
    half = cos.shape[-1]
    r = 2 * half
    tr = t[..., :r].astype(jnp.float32)
    t1, t2 = tr[..., :half], tr[..., half:]
    rot = jnp.concatenate([t1 * cos - t2 * sin, t2 * cos + t1 * sin], axis=-1).astype(t.dtype)
    return jnp.concatenate([rot, t[..., r:]], axis=-1)


def short_conv_mixer(h, gb, gc, conv_w):
    u = gc * h
    L = u.shape[1]
    up = jnp.pad(u, ((0, 0), (CONV_WIDTH - 1, 0), (0, 0)))
    v = sum(conv_w[j] * up[:, j:j + L] for j in range(CONV_WIDTH))
    return gb * v


def s5_mixer(u, lam_re, lam_im, log_dt, b_re, b_im, c_re, c_im, d_skip, w_glu, b_glu):
    f32 = jnp.float32
    Bt, L, _ = u.shape
    uf = u.astype(f32)
    ug = uf.reshape(Bt, L, S5_GROUPS, S5_GROUP)
    lr = jnp.minimum(lam_re.astype(f32), -1e-4)
    li = lam_im.astype(f32)
    dt = jnp.exp(log_dt.astype(f32))[:, None]
    mag = jnp.exp(lr * dt)
    ab_re = mag * jnp.cos(li * dt)
    ab_im = mag * jnp.sin(li * dt)
    nr, ni = ab_re - 1.0, ab_im
    den = lr * lr + li * li
    f_re = (nr * lr + ni * li) / den
    f_im = (ni * lr - nr * li) / den
    br, bi = b_re.astype(f32), b_im.astype(f32)
    bb_re = f_re[..., None] * br - f_im[..., None] * bi
    bb_im = f_re[..., None] * bi + f_im[..., None] * br
    bu_re = jnp.einsum('blgc,gpc->blgp', ug, bb_re)
    bu_im = jnp.einsum('blgc,gpc->blgp', ug, bb_im)
    a_re = jnp.broadcast_to(ab_re, bu_re.shape)
    a_im = jnp.broadcast_to(ab_im, bu_im.shape)

    def combine(e1, e2):
        a1r, a1i, b1r, b1i = e1
        a2r, a2i, b2r, b2i = e2
        return (a2r * a1r - a2i * a1i, a2r * a1i + a2i * a1r,
                a2r * b1r - a2i * b1i + b2r, a2r * b1i + a2i * b1r + b2i)

    _, _, s_re, s_im = lax.associative_scan(combine, (a_re, a_im, bu_re, bu_im), axis=1)
    y = (jnp.einsum('blgp,gcp->blgc', s_re, c_re.astype(f32))
         - jnp.einsum('blgp,gcp->blgc', s_im, c_im.astype(f32)))
    y = y.reshape(Bt, L, S5_CH) + d_skip.astype(f32) * uf
    z = jax.nn.gelu(y)
    out = z * jax.nn.sigmoid(z @ w_glu.astype(f32) + b_glu.astype(f32))
    return out.astype(u.dtype)


def dsa_mixer(proj, cos_a, sin_a, cos_i, sin_i):
    f32 = jnp.float32
    Bt, L, _ = proj.shape
    q, k, v, qi, ki, wi = jnp.split(proj, C_SPLITS, axis=-1)
    q = partial_rope(q.reshape(Bt, L, N_HEADS, HEAD_DIM), cos_a[:, :, None], sin_a[:, :, None])
    k = partial_rope(k.reshape(Bt, L, N_KV_HEADS, HEAD_DIM), cos_a[:, :, None], sin_a[:, :, None])
    v = v.reshape(Bt, L, N_KV_HEADS, HEAD_DIM)
    qi = partial_rope(qi.reshape(Bt, L, IDX_HEADS, IDX_DIM), cos_i[:, :, None], sin_i[:, :, None])
    ki = partial_rope(ki, cos_i, sin_i).astype(f32)
    wi = wi.astype(f32) * (IDX_HEADS ** -0.5 * IDX_DIM ** -0.5)
    topk = min(TOPK_MAX, L // 4)
    nb = L // Q_BLOCK

    def to_blocks(t):
        return jnp.moveaxis(t.reshape(Bt, nb, Q_BLOCK, *t.shape[2:]), 1, 0)

    t_pos = jnp.arange(L, dtype=jnp.int32).reshape(nb, Q_BLOCK)
    key_pos = jnp.arange(L, dtype=jnp.int32)
    bidx = jnp.arange(Bt)[:, None, None]
    rep = N_HEADS // N_KV_HEADS

    def block(args):
        qb, qib, wb, tb = args
        logits = jax.nn.relu(jnp.einsum('bthd,bsd->bths', qib.astype(f32), ki))
        score = jnp.einsum('bths,bth->bts', logits, wb)
        causal = key_pos[None, :] <= tb[:, None]
        score = jnp.where(causal[None], score, -jnp.inf)
        _, idx = lax.top_k(score, topk)
        valid = idx <= tb[None, :, None]
        k_sel = k[bidx, idx]
        v_sel = v[bidx, idx]
        qg = qb.reshape(Bt, Q_BLOCK, N_KV_HEADS, rep, HEAD_DIM)
        s = jnp.einsum('btgrd,btjgd->btgrj', qg, k_sel).astype(f32) * (HEAD_DIM ** -0.5)
        s = jnp.where(valid[:, :, None, None, :], s, -jnp.inf)
        pr = jax.nn.softmax(s, axis=-1)
        o = jnp.einsum('btgrj,btjgd->btgrd', pr.astype(v.dtype), v_sel)
        return o.reshape(Bt, Q_BLOCK, N_HEADS * HEAD_DIM)

    out = lax.map(block, (to_blocks(q), to_blocks(qi), to_blocks(wi), t_pos))
    return jnp.moveaxis(out, 0, 1).reshape(Bt, L, N_HEADS * HEAD_DIM)


def setup_inputs(seed: int = 0) -> dict:
    key = jax.random.key(seed)
    ks = jax.random.split(key, 32)
    f32 = jnp.float32

    def nrm(i, shape, scale):
        return jax.random.normal(ks[i], shape, f32) * scale

    x = nrm(0, (BATCH, SEQ, D_MODEL), 1.0)
    p = nrm(1, (DEPTH, BATCH, SEQ, PLE_DIM), 1.0)
    offs = jax.random.randint(ks[2], (BATCH, 1), 0, 1024, dtype=jnp.int32)
    positions = offs + jnp.arange(SEQ, dtype=jnp.int32)[None, :]
    ln_g = 1.0 + nrm(3, (DEPTH, 3, D_MODEL), 0.02)
    ln_b = nrm(4, (DEPTH, 3, D_MODEL), 0.02)
    ffn_w1 = nrm(5, (DEPTH, 2, D_MODEL, D_FF), D_MODEL ** -0.5)
    ffn_w3 = nrm(6, (DEPTH, 2, D_MODEL, D_FF), D_MODEL ** -0.5)
    ffn_w2 = nrm(7, (DEPTH, 2, D_FF, D_MODEL), D_FF ** -0.5 * DN_BETA)
    ple_w_proj = nrm(8, (DEPTH, PLE_DIM, D_MODEL), PLE_DIM ** -0.5)
    ple_w_gate = nrm(9, (DEPTH, D_MODEL, D_MODEL), D_MODEL ** -0.5)
    ab_w_in = nrm(10, (N_EVEN, D_MODEL, AB_IN), D_MODEL ** -0.5)
    ab_w_out = nrm(11, (N_EVEN, CONV_CH + S5_CH, D_MODEL), (CONV_CH + S5_CH) ** -0.5 * DN_BETA)
    conv_w = nrm(12, (N_EVEN, CONV_WIDTH, CONV_CH), CONV_WIDTH ** -0.5)
    s5_lam_re = -0.5 + nrm(13, (N_EVEN, S5_GROUPS, S5_STATE), 0.01)
    s5_lam_im = (math.pi * jnp.arange(S5_STATE, dtype=f32))[None, None, :] + nrm(14, (N_EVEN, S5_GROUPS, S5_STATE), 0.01)
    s5_log_dt = jax.random.uniform(ks[15], (N_EVEN, S5_GROUPS), f32, math.log(1e-3), math.log(1e-1))
    s5_b_re = nrm(16, (N_EVEN, S5_GROUPS, S5_STATE, S5_GROUP), (2 * S5_GROUP) ** -0.5)
    s5_b_im = nrm(17, (N_EVEN, S5_GROUPS, S5_STATE, S5_GROUP), (2 * S5_GROUP) ** -0.5)
    s5_c_re = nrm(18, (N_EVEN, S5_GROUPS, S5_GROUP, S5_STATE), S5_STATE ** -0.5)
    s5_c_im = nrm(19, (N_EVEN, S5_GROUPS, S5_GROUP, S5_STATE), S5_STATE ** -0.5)
    s5_d = nrm(20, (N_EVEN, S5_CH), 1.0)
    s5_w_glu = nrm(21, (N_EVEN, S5_CH, S5_CH), S5_CH ** -0.5)
    s5_b_glu = nrm(22, (N_EVEN, S5_CH), 0.02)
    c_w_in = nrm(23, (N_ODD, D_MODEL, C_IN), D_MODEL ** -0.5)
    c_w_out = nrm(24, (N_ODD, N_HEADS * HEAD_DIM, D_MODEL), (N_HEADS * HEAD_DIM) ** -0.5 * DN_BETA)
    return {'x': x, 'p': p, 'positions': positions, 'ln_g': ln_g, 'ln_b': ln_b,
            'ffn_w1': ffn_w1, 'ffn_w3': ffn_w3, 'ffn_w2': ffn_w2,
            'ple_w_proj': ple_w_proj, 'ple_w_gate': ple_w_gate,
            'ab_w_in': ab_w_in, 'ab_w_out': ab_w_out, 'conv_w': conv_w,
            's5_lam_re': s5_lam_re, 's5_lam_im': s5_lam_im, 's5_log_dt': s5_log_dt,
            's5_b_re': s5_b_re, 's5_b_im': s5_b_im, 's5_c_re': s5_c_re, 's5_c_im': s5_c_im,
            's5_d': s5_d, 's5_w_glu': s5_w_glu, 's5_b_glu': s5_b_glu,
            'c_w_in': c_w_in, 'c_w_out': c_w_out}


def reference(x, p, positions, ln_g, ln_b, ffn_w1, ffn_w3, ffn_w2, ple_w_proj, ple_w_gate,
              ab_w_in, ab_w_out, conv_w, s5_lam_re, s5_lam_im, s5_log_dt, s5_b_re, s5_b_im,
              s5_c_re, s5_c_im, s5_d, s5_w_glu, s5_b_glu, c_w_in, c_w_out):
    cos_a, sin_a = rope_tables(positions, HEAD_DIM // ROT_FRAC)
    cos_i, sin_i = rope_tables(positions, IDX_DIM // ROT_FRAC)
    h = x
    for i in range(DEPTH):
        h = layer_norm(DN_ALPHA * h + 0.5 * swiglu(h, ffn_w1[i, 0], ffn_w3[i, 0], ffn_w2[i, 0]),
                       ln_g[i, 0], ln_b[i, 0])
        j = i // 2
        if i % 2 == 0:
            proj = h @ ab_w_in[j]
            hc, gb, gc, u = jnp.split(proj, [CONV_CH, 2 * CONV_CH, 3 * CONV_CH], axis=-1)
            ya = short_conv_mixer(hc, gb, gc, conv_w[j])
            yb = s5_mixer(u, s5_lam_re[j], s5_lam_im[j], s5_log_dt[j], s5_b_re[j], s5_b_im[j],
                          s5_c_re[j], s5_c_im[j], s5_d[j], s5_w_glu[j], s5_b_glu[j])
            mix = jnp.concatenate([ya, yb], axis=-1) @ ab_w_out[j]
        else:
            proj = h @ c_w_in[j]
            mix = dsa_mixer(proj, cos_a, sin_a, cos_i, sin_i) @ c_w_out[j]
        h = layer_norm(DN_ALPHA * h + mix, ln_g[i, 1], ln_b[i, 1])
        h = layer_norm(DN_ALPHA * h + 0.5 * swiglu(h, ffn_w1[i, 1], ffn_w3[i, 1], ffn_w2[i, 1]),
                       ln_g[i, 2], ln_b[i, 2])
        h = h + (p[i] @ ple_w_proj[i]) * jax.nn.sigmoid(h @ ple_w_gate[i])
    return h
```

```python
import contextlib
import math
import numpy as np
import concourse.bass as bass
import concourse.mybir as mybir
from concourse.bass_utils import run_bass_kernel_spmd

F32 = mybir.dt.float32
BF16 = mybir.dt.bfloat16
I32 = mybir.dt.int32
AF = mybir.ActivationFunctionType
ALU = mybir.AluOpType
AX = mybir.AxisListType

S = 8192
D = 1024
FF = 2816
NFC = FF // 128
T = 512
NT = S // T
DEPTH = 4
ALPHA = (2 * DEPTH) ** 0.25
EPS = 1e-5
NCORES = 8


class _Eng:
    def __init__(self, name, h, sem):
        self.name, self.h, self.sem = name, h, sem
        self.count = 0
        self.seen = {}


class Buf:
    def __init__(self, t, name):
        self.t = t
        self.name = name
        self.w = None
        self.r = {}

    def __getitem__(self, k):
        return self.t[k]


class Ctx:
    def __init__(self, nc, es):
        self.nc, self.es = nc, es
        self.e = {}
        for name in ["tensor", "vector", "scalar", "gpsimd"]:
            sem = es.enter_context(nc.semaphore("s_" + name))
            self.e[name] = _Eng(name, getattr(nc, name), sem)
        self.dq = {}
        for name, n in [("sync", 16), ("gpsimd", 8)]:
            sems = [es.enter_context(nc.semaphore("d_%s%d" % (name, i))) for i in range(n)]
            self.dq[name] = dict(h=getattr(nc, name), sems=sems, cnt=[0] * n, i=0, seen={})
        self.nbuf = 0

    def sb(self, es, shape, dt, name=None):
        self.nbuf += 1
        name = (name or "b") + "_%d" % self.nbuf
        return Buf(es.enter_context(self.nc.sbuf_tensor(name, list(shape), dt)), name)

    def ps(self, es, shape, dt, name=None):
        self.nbuf += 1
        name = (name or "p") + "_%d" % self.nbuf
        return Buf(es.enter_context(self.nc.psum_tensor(name, list(shape), dt)), name)

    def _need(self, seen, h, dep, me, skip_same):
        if dep is None:
            return
        src, val = dep
        if src[0] == "E":
            if src[1] == me and skip_same:
                return
            sem = self.e[src[1]].sem
        else:
            sem = self.dq[src[1]]["sems"][src[2]]
        if seen.get(src, -1) >= val:
            return
        h.wait_ge(sem, val)
        seen[src] = val

    def op(self, eng, fn, reads=(), writes=()):
        E = self.e[eng]
        same_ok = eng == "tensor"
        for b in reads:
            self._need(E.seen, E.h, b.w, eng, same_ok)
        for b in writes:
            self._need(E.seen, E.h, b.w, eng, same_ok)
            for d in list(b.r.items()):
                self._need(E.seen, E.h, d, eng, True)
        ins = fn(E.h)
        E.count += 1
        ins.then_inc(E.sem, 1)
        tag = (("E", eng), E.count)
        for b in reads:
            b.r[tag[0]] = tag[1]
        for b in writes:
            b.w = tag
            b.r = {}
        return ins

    def dma(self, qn, out, in_, reads=(), writes=(), **kw):
        q = self.dq[qn]
        h = q["h"]
        seen = self.e[qn].seen if qn in self.e else q["seen"]
        me = qn if qn in self.e else None
        for b in reads:
            self._need(seen, h, b.w, me, False)
        for b in writes:
            self._need(seen, h, b.w, me, False)
            for d in list(b.r.items()):
                self._need(seen, h, d, me, False)
        i = q["i"]
        q["i"] = (i + 1) % len(q["sems"])
        src = ("D", qn, i)
        if q["cnt"][i] > 0 and seen.get(src, -1) < q["cnt"][i]:
            h.wait_ge(q["sems"][i], q["cnt"][i])
            seen[src] = q["cnt"][i]
        ins = h.dma_start(out=out, in_=in_, **kw)
        q["cnt"][i] += 16
        ins.then_inc(q["sems"][i], 16)
        tag = (src, q["cnt"][i])
        for b in reads:
            b.r[tag[0]] = tag[1]
        for b in writes:
            b.w = tag
            b.r = {}
        return ins

    def barrier(self):
        deps = [(("E", n), E.count) for n, E in self.e.items() if E.count > 0]
        for qn, q in self.dq.items():
            for i, v in enumerate(q["cnt"]):
                if v > 0:
                    deps.append((("D", qn, i), v))
        for n, E in self.e.items():
            for d in deps:
                self._need(E.seen, E.h, d, n, True)
        q = self.dq["sync"]
        for d in deps:
            self._need(q["seen"], q["h"], d, None, False)


def _kmaj(w):
    K, N = w.shape
    return np.ascontiguousarray(w.reshape(K // 128, 128, N).transpose(1, 0, 2))


def _swap_index():
    idx = []
    swA = [d + 16 if d < 16 else (d - 16 if d < 32 else d) for d in range(128)]
    swI = [d + 8 if d < 8 else (d - 8 if d < 16 else d) for d in range(64)]
    for h in range(8):
        idx += [h * 128 + v for v in swA]
    for g in range(2):
        idx += [1024 + g * 128 + v for v in swA]
    for h in range(8):
        idx += [1536 + h * 64 + v for v in swI]
    idx += [2048 + v for v in swI] + [2048 + v for v in swI]
    return np.array(idx, np.int64)


_SW_IDX = _swap_index()
BIG = 3.0e38


class Blob:
    def __init__(self):
        self.parts, self.off, self.n = [], {}, 0

    def add(self, name, arr):
        a = np.ascontiguousarray(arr, dtype=np.float32).reshape(-1)
        self.off[name] = (self.n, tuple(arr.shape))
        self.parts.append(a)
        self.n += a.size
        pad = (-self.n) % 128
        if pad:
            self.parts.append(np.zeros(pad, np.float32))
            self.n += pad

    def finish(self, row=2048 * 128):
        pad = (-self.n) % row
        if pad:
            self.parts.append(np.zeros(pad, np.float32))
            self.n += pad
        return np.concatenate(self.parts).reshape(-1, 2048)


def _layer_blob_layout(L, inputs=None):
    b = Blob()

    def get(name, idx, shape):
        if inputs is None:
            return np.zeros(shape, np.float32)
        a = inputs[name]
        for i in idx:
            a = a[i]
        return np.asarray(a, np.float32)

    for s in range(2):
        w1 = get("ffn_w1", (L, s), (D, FF))
        w3 = get("ffn_w3", (L, s), (D, FF))
        w2 = get("ffn_w2", (L, s), (FF, D))
        a1 = w1.reshape(8, 128, NFC, 128).transpose(2, 1, 0, 3)
        a3 = w3.reshape(8, 128, NFC, 128).transpose(2, 1, 0, 3)
        b.add("w13_%d" % s, np.stack([a1, a3], axis=2))
        b.add("w2_%d" % s, _kmaj(w2))
    b.add("wg", _kmaj(get("ple_w_gate", (L,), (D, D))))
    b.add("wp", _kmaj(get("ple_w_proj", (L,), (256, D))))
    j = L // 2
    if L % 2 == 0:
        b.add("win", _kmaj(get("ab_w_in", (j,), (D, 2048))))
        b.add("wout", _kmaj(get("ab_w_out", (j,), (D, D))))
        b.add("wglu", _kmaj(get("s5_w_glu", (j,), (512, 512))))
    else:
        wc = get("c_w_in", (j,), (D, 2120))
        wpad = np.zeros((D, 2176), np.float32)
        wpad[:, :2120] = wc
        b.add("cin", _kmaj(wpad))
        b.add("cinsw", _kmaj(np.ascontiguousarray(wc[:, _SW_IDX])))
        b.add("cout", _kmaj(get("c_w_out", (j,), (D, D))))
    return b


def _small_params(inputs=None):
    b = Blob()

    def get(name, shape):
        if inputs is None:
            return np.zeros(shape, np.float32)
        return np.asarray(inputs[name], np.float32)

    g = get("ln_g", (DEPTH, 3, D))
    bb = get("ln_b", (DEPTH, 3, D))
    b.add("ln_g", g.reshape(DEPTH, 3, 8, 128).transpose(3, 0, 1, 2))
    b.add("ln_b", bb.reshape(DEPTH, 3, 8, 128).transpose(3, 0, 1, 2))
    sgn = np.ones((128, 1), np.float32)
    sgn[64:] = -1.0
    b.add("sgn", sgn)
    b.add("ident", np.eye(128, dtype=np.float32))
    sw = np.zeros((128, 128), np.float32)
    for k in range(64):
        sw[k, k + 64] = 1.0
        sw[k + 64, k] = 1.0
    b.add("swapP", sw)
    inv_a = (500000.0 ** (-np.arange(0, 32, 2, dtype=np.float32) / np.float32(32))).astype(np.float32)
    inv_i = (500000.0 ** (-np.arange(0, 16, 2, dtype=np.float32) / np.float32(16))).astype(np.float32)
    rc = np.zeros((128, 4), np.float32)
    for r in range(128):
        if r < 32:
            rc[r, 0] = inv_a[r % 16]
            rc[r, 1] = -1.0 if r < 16 else 1.0
        dd = r % 64
        if dd < 16:
            rc[r, 2] = inv_i[dd % 8]
            rc[r, 3] = -1.0 if dd < 8 else 1.0
    b.add("ropec", rc)
    xx = np.arange(896)[None, :]
    qq = np.arange(128)[:, None]
    b.add("Wneg", np.where(xx <= qq + 384, 0.0, -BIG).astype(np.float32))
    b.add("Wpos", np.where(xx <= qq + 384, 0.0, BIG).astype(np.float32))
    for j in range(2):
        lre = get("s5_lam_re", (2, 32, 64))[j]
        lim = get("s5_lam_im", (2, 32, 64))[j]
        ldt = get("s5_log_dt", (2, 32))[j]
        b.add("lamre_A%d" % j, np.concatenate([lre.T, lre.T], 0))
        b.add("lamim_A%d" % j, np.concatenate([lim.T, lim.T], 0))
        b.add("logdt_A%d" % j, np.broadcast_to(ldt[None, :], (128, 32)))
        b.add("lamre_C%d" % j, np.broadcast_to(lre.reshape(1, 2048), (16, 2048)))
        b.add("lamim_C%d" % j, np.broadcast_to(lim.reshape(1, 2048), (16, 2048)))
        b.add("logdt_C%d" % j, np.broadcast_to(np.repeat(ldt, 64)[None, :], (16, 2048)))
        b.add("bre_C%d" % j, get("s5_b_re", (2, 32, 64, 16))[j].transpose(2, 0, 1))
        b.add("bim_C%d" % j, get("s5_b_im", (2, 32, 64, 16))[j].transpose(2, 0, 1))
        cre = get("s5_c_re", (2, 32, 16, 64))[j].transpose(2, 0, 1)
        cim = get("s5_c_im", (2, 32, 16, 64))[j].transpose(2, 0, 1)
        b.add("CA%d" % j, np.concatenate([cre, cim], 0))
        b.add("CB%d" % j, np.concatenate([cim, cre], 0))
        b.add("d_C%d" % j, get("s5_d", (2, 512))[j].reshape(32, 16).T)
        b.add("convw%d" % j, get("conv_w", (2, 3, 512))[j].reshape(3, 4, 128).transpose(2, 1, 0))
        b.add("bglu%d" % j, get("s5_b_glu", (2, 512))[j].reshape(4, 128).T)
    return b


class Prog:
    def __init__(self, stop_after=None, debug=False):
        self.stop_after = stop_after
        self.nqs = None
        self.start_at = None
        dk = "ExternalOutput" if debug else "Internal"
        nc = self.nc = bass.Bass("TRN2", target_bir_lowering=False)
        self.xT = nc.dram_tensor("xT", [D, S], F32, kind="ExternalInput").ap()
        self.pT = nc.dram_tensor("pT", [DEPTH, 256, S], F32, kind="ExternalInput").ap()
        self.lay = [_layer_blob_layout(L) for L in range(DEPTH)]
        self.wl, self.wb = [], []
        for L in range(DEPTH):
            rows = self.lay[L].finish().shape[0]
            self.wl.append(nc.dram_tensor("wl%d" % L, [rows, 2048], F32, kind="ExternalInput").ap())
            self.wb.append(nc.dram_tensor("wb%d" % L, [rows, 2048], BF16, kind="Internal").ap())
        self.splay = _small_params()
        sp_rows = self.splay.finish().shape[0]
        self.sp = nc.dram_tensor("sp", [sp_rows, 2048], F32, kind="ExternalInput").ap()
        self.outT = nc.dram_tensor("outT", [D, S], F32, kind="ExternalOutput").ap()
        self.hTb = nc.dram_tensor("hTb", [D, S], BF16, kind="Internal").ap()
        self.pTb = nc.dram_tensor("pTb", [DEPTH, 256, S], BF16, kind="Internal").ap()
        self.yaTb = nc.dram_tensor("yaTb", [512, S], BF16, kind=dk).ap()
        self.uT32 = nc.dram_tensor("uT32", [512, S], F32, kind=dk).ap()
        self.uTb = nc.dram_tensor("uTb", [512, S], BF16, kind="Internal").ap()
        self.zT32 = nc.dram_tensor("zT32", [512, S], F32, kind=dk).ap()
        self.zTb = nc.dram_tensor("zTb", [512, S], BF16, kind="Internal").ap()
        self.posb = nc.dram_tensor("posb", [128, S], I32, kind="ExternalInput").ap()
        self.tabs = nc.dram_tensor("tabs", [4, 128, S], F32, kind="Internal").ap()
        self.qT = nc.dram_tensor("qT", [1024, S], BF16, kind="Internal").ap()
        self.kT = nc.dram_tensor("kT", [256, S], BF16, kind="Internal").ap()
        self.vtok = nc.dram_tensor("vtok", [S, 256], BF16, kind="Internal").ap()
        self.qiT = nc.dram_tensor("qiT", [512, S], BF16, kind="Internal").ap()
        self.kiT = nc.dram_tensor("kiT", [128, S], BF16, kind="Internal").ap()
        self.witok = nc.dram_tensor("witok", [S, 8], F32, kind="Internal").ap()
        self.attT = nc.dram_tensor("attT", [1024, S], F32, kind=dk).ap()
        self.lden = nc.dram_tensor("lden", [8, S], F32, kind=dk).ap()
        self.B_tabs = Buf(self.tabs, "tabs")
        self.B_qkv = Buf(self.qT, "qkv")
        self.B_att = Buf(self.attT, "att")
        self.B_ya = Buf(self.yaTb, "yaTb")
        self.B_u = Buf(self.uT32, "u")
        self.B_z = Buf(self.zT32, "z")
        self.B_out = Buf(self.outT, "outT")
        self.B_hTb = Buf(self.hTb, "hTb")
        self.B_wb = [Buf(self.wb[L], "wb%d" % L) for L in range(DEPTH)]
        self.B_pTb = Buf(self.pTb, "pTb")

    def wv(self, L, name):
        off, shape = self.lay[L].off[name]
        n = int(np.prod(shape))
        flat = self.wb[L].rearrange("r c -> (r c)")[off:off + n]
        return flat, shape

    def spv(self, name):
        off, shape = self.splay.off[name]
        n = int(np.prod(shape))
        return self.sp.rearrange("r c -> (r c)")[off:off + n], shape

    def cast_weights(self, c):
        for L in range(DEPTH):
            rows = self.wl[L].shape[0]
            for r0 in range(0, rows, 128):
                c.dma("gpsimd", self.wb[L][r0:r0 + 128, :], self.wl[L][r0:r0 + 128, :],
                      writes=[self.B_wb[L]])
        for r0 in range(0, D, 128):
            for c0 in range(0, S, 2048):
                c.dma("gpsimd", self.hTb[r0:r0 + 128, c0:c0 + 2048], self.xT[r0:r0 + 128, c0:c0 + 2048],
                      writes=[self.B_hTb])
        for L in range(DEPTH):
            for r0 in range(0, 256, 128):
                for c0 in range(0, S, 2048):
                    c.dma("gpsimd", self.pTb[L, r0:r0 + 128, c0:c0 + 2048],
                          self.pT[L, r0:r0 + 128, c0:c0 + 2048], writes=[self.B_pTb])
        c.barrier()

    def ln_setup(self, c, es):
        o = type("LN", (), {})()
        o.y = [c.sb(es, [128, T], F32, "y") for _ in range(8)]
        o.ybf = [c.sb(es, [128, T], BF16, "ybf") for _ in range(2)]
        o.ysq = [c.sb(es, [128, T], BF16, "ysq") for _ in range(2)]
        o.ob = [c.sb(es, [128, T], BF16, "ob") for _ in range(8)]
        o.t1 = [c.sb(es, [128, T], F32, "t1") for _ in range(2)]
        o.t2 = [c.sb(es, [128, T], F32, "t2") for _ in range(2)]
        o.mean = c.sb(es, [128, T], F32, "mean")
        o.msq = c.sb(es, [128, T], F32, "msq")
        o.var = c.sb(es, [128, T], F32, "var")
        o.rstd = c.sb(es, [128, T], F32, "rstd")
        o.ones = c.sb(es, [128, 128], BF16, "ones")
        o.lng = c.sb(es, [128, DEPTH * 3 * 8], F32, "lng")
        o.lnb = c.sb(es, [128, DEPTH * 3 * 8], F32, "lnb")
        gf, _ = self.spv("ln_g")
        bf_, _ = self.spv("ln_b")
        c.dma("sync", o.lng[:], gf.rearrange("(p m) -> p m", p=128), writes=[o.lng])
        c.dma("sync", o.lnb[:], bf_.rearrange("(p m) -> p m", p=128), writes=[o.lnb])
        c.op("vector", lambda e: e.memset(o.ones[:], 1.0), writes=[o.ones])
        o.pss = c.ps(es, [128, T], F32, "pss")
        o.psq = c.ps(es, [128, T], F32, "psq")
        return o

    def ln_accum(self, c, o, dc):
        yd = o.y[dc]
        yb_, yq_ = o.ybf[dc % 2], o.ysq[dc % 2]
        c.op("scalar", lambda e: e.activation(out=yb_[:], in_=yd[:], func=AF.Copy), reads=[yd], writes=[yb_])
        c.op("scalar", lambda e: e.activation(out=yq_[:], in_=yd[:], func=AF.Square), reads=[yd], writes=[yq_])
        c.op("tensor", lambda e: e.matmul(o.pss[:], lhsT=o.ones[:], rhs=yb_[:], start=(dc == 0), stop=(dc == 7)),
             reads=[o.ones, yb_], writes=[o.pss])
        c.op("tensor", lambda e: e.matmul(o.psq[:], lhsT=o.ones[:], rhs=yq_[:], start=(dc == 0), stop=(dc == 7)),
             reads=[o.ones, yq_], writes=[o.psq])

    def ln_finish(self, c, o, lcol, tok, store):
        mean_sb, msq, var, rstd = o.mean, o.msq, o.var, o.rstd
        c.op("scalar", lambda e: e.activation(out=mean_sb[:], in_=o.pss[:], func=AF.Copy, scale=1.0 / D),
             reads=[o.pss], writes=[mean_sb])
        c.op("scalar", lambda e: e.activation(out=msq[:], in_=mean_sb[:], func=AF.Square), reads=[mean_sb], writes=[msq])
        c.op("vector", lambda e: e.scalar_tensor_tensor(out=var[:], in0=o.psq[:], scalar=1.0 / D, in1=msq[:],
                                                        op0=ALU.mult, op1=ALU.subtract),
             reads=[o.psq, msq], writes=[var])
        c.op("vector", lambda e: e.tensor_scalar(out=var[:], in0=var[:], scalar1=float(EPS), scalar2=None, op0=ALU.add),
             reads=[var], writes=[var])
        c.op("scalar", lambda e: e.activation(out=msq[:], in_=var[:], func=AF.Sqrt), reads=[var], writes=[msq])
        c.op("vector", lambda e: e.reciprocal(out=rstd[:], in_=msq[:]), reads=[msq], writes=[rstd])
        for dc in range(8):
            yd = o.y[dc]
            a1, a2 = o.t1[dc % 2], o.t2[dc % 2]
            c.op("vector", lambda e: e.tensor_tensor(out=a1[:], in0=yd[:], in1=mean_sb[:], op=ALU.subtract),
                 reads=[yd, mean_sb], writes=[a1])
            c.op("gpsimd", lambda e: e.tensor_tensor(out=a2[:], in0=a1[:], in1=rstd[:], op=ALU.mult),
                 reads=[a1, rstd], writes=[a2])
            c.op("vector", lambda e: e.tensor_scalar(out=yd[:], in0=a2[:], scalar1=o.lng[:, lcol + dc:lcol + dc + 1],
                                                     scalar2=o.lnb[:, lcol + dc:lcol + dc + 1], op0=ALU.mult, op1=ALU.add),
                 reads=[a2, o.lng, o.lnb], writes=[yd])
            o_ = o.ob[dc]
            c.op("scalar", lambda e: e.activation(out=o_[:], in_=yd[:], func=AF.Copy), reads=[yd], writes=[o_])
            if store:
                c.dma("gpsimd", self.outT[dc * 128:(dc + 1) * 128, tok], yd[:], reads=[yd], writes=[self.B_out])
                c.dma("gpsimd", self.hTb[dc * 128:(dc + 1) * 128, tok], o_[:], reads=[o_], writes=[self.B_hTb])

    def ffn(self, c, L, s, src32, B_src32, ple):
        j_ln = 0 if s == 0 else 2
        with contextlib.ExitStack() as es:
            w13f, _ = self.wv(L, "w13_%d" % s)
            w13v = w13f.rearrange("(c p m) -> c p m", c=NFC, p=128)
            w2f, _ = self.wv(L, "w2_%d" % s)
            w2v = w2f.rearrange("(p m) -> p m", p=128)
            w2sb = c.sb(es, [128, NFC * D], BF16, "w2sb")
            for q in range(4):
                m0, m1 = q * (NFC * D // 4), (q + 1) * (NFC * D // 4)
                c.dma("sync", w2sb[:, m0:m1], w2v[:, m0:m1], reads=[self.B_wb[L]], writes=[w2sb])
            w13sb = [c.sb(es, [128, 2 * 8 * 128], BF16, "w13sb") for _ in range(3)]
            hb = [c.sb(es, [128, 8, T], BF16, "hb") for _ in range(2)]
            act = [c.sb(es, [128, T], BF16, "act") for _ in range(NFC)]
            sgt = [c.sb(es, [128, T], F32, "sgt") for _ in range(2)]
            h32c = [c.sb(es, [128, T], F32, "h32c") for _ in range(3)]
            o = self.ln_setup(c, es)
            y, ob, t1, ybf = o.y, o.ob, o.t1, o.ybf
            pg = [c.ps(es, [128, T], F32, "pg") for _ in range(2)]
            pu = [c.ps(es, [128, T], F32, "pu") for _ in range(2)]
            pd = [c.ps(es, [128, T], F32, "pd") for _ in range(2)]
            if ple:
                wgf, _ = self.wv(L, "wg")
                wpf, _ = self.wv(L, "wp")
                wgsb = c.sb(es, [128, 8 * D], BF16, "wgsb")
                wpsb = c.sb(es, [128, 2 * D], BF16, "wpsb")
                c.dma("sync", wgsb[:], wgf.rearrange("(p m) -> p m", p=128), reads=[self.B_wb[L]], writes=[wgsb])
                c.dma("sync", wpsb[:], wpf.rearrange("(p m) -> p m", p=128), reads=[self.B_wb[L]], writes=[wpsb])
                ptb = [c.sb(es, [128, 2, T], BF16, "ptb") for _ in range(2)]
                sig = [c.sb(es, [128, T], F32, "sig") for _ in range(2)]
                et = [c.sb(es, [128, T], F32, "et") for _ in range(2)]
            lcol = (L * 3 + j_ln) * 8
            hTb_v = self.hTb.rearrange("(kc p) t -> p kc t", p=128)
            pTb_v = self.pTb.rearrange("l (kc p) t -> l p kc t", p=128)

            def load(i):
                c.dma("sync", hb[i % 2][:], hTb_v[:, :, i * T:(i + 1) * T], reads=[self.B_hTb], writes=[hb[i % 2]])
                if ple:
                    c.dma("sync", ptb[i % 2][:], pTb_v[L, :, :, i * T:(i + 1) * T], reads=[self.B_pTb],
                          writes=[ptb[i % 2]])

            def compute(i):
                hbi = hb[i % 2]
                tok = slice(i * T, (i + 1) * T)
                for cc in range(NFC):
                    wt = w13sb[cc % 3]
                    c.dma("sync", wt[:], w13v[cc], reads=[self.B_wb[L]], writes=[wt])
                    g_, u_ = pg[cc % 2], pu[cc % 2]
                    for kc in range(8):
                        c.op("tensor", lambda e, kc=kc: e.matmul(g_[:], lhsT=wt[:, kc * 128:(kc + 1) * 128],
                                                                 rhs=hbi[:, kc, :], start=(kc == 0), stop=(kc == 7)),
                             reads=[wt, hbi], writes=[g_])
                    for kc in range(8):
                        c.op("tensor", lambda e, kc=kc: e.matmul(u_[:], lhsT=wt[:, (8 + kc) * 128:(9 + kc) * 128],
                                                                 rhs=hbi[:, kc, :], start=(kc == 0), stop=(kc == 7)),
                             reads=[wt, hbi], writes=[u_])
                    sg = sgt[cc % 2]
                    c.op("scalar", lambda e: e.activation(out=sg[:], in_=g_[:], func=AF.Silu), reads=[g_], writes=[sg])
                    a_ = act[cc]
                    c.op("vector", lambda e: e.scalar_tensor_tensor(out=a_[:], in0=u_[:], scalar=0.5, in1=sg[:],
                                                                    op0=ALU.mult, op1=ALU.mult),
                         reads=[u_, sg], writes=[a_])
                for dc in range(8):
                    hr = h32c[dc % 3]
                    c.dma("sync", hr[:], src32[dc * 128:(dc + 1) * 128, tok], reads=[B_src32], writes=[hr])
                    p_ = pd[dc % 2]
                    for cc in range(NFC):
                        c.op("tensor", lambda e, cc=cc: e.matmul(
                            p_[:], lhsT=w2sb[:, cc * D + dc * 128: cc * D + (dc + 1) * 128], rhs=act[cc][:],
                            start=(cc == 0), stop=(cc == NFC - 1)), reads=[w2sb, act[cc]], writes=[p_])
                    yd = y[dc]
                    c.op("vector", lambda e: e.scalar_tensor_tensor(out=yd[:], in0=hr[:], scalar=float(ALPHA), in1=p_[:],
                                                                    op0=ALU.mult, op1=ALU.add),
                         reads=[hr, p_], writes=[yd])
                    self.ln_accum(c, o, dc)
                self.ln_finish(c, o, lcol, tok, store=not ple)
                if ple:
                    pti = ptb[i % 2]
                    for dc in range(8):
                        g_, u_ = pg[dc % 2], pu[dc % 2]
                        for kc in range(8):
                            c.op("tensor", lambda e, kc=kc: e.matmul(
                                g_[:], lhsT=wgsb[:, kc * D + dc * 128: kc * D + (dc + 1) * 128], rhs=ob[kc][:],
                                start=(kc == 0), stop=(kc == 7)), reads=[wgsb, ob[kc]], writes=[g_])
                        for kc in range(2):
                            c.op("tensor", lambda e, kc=kc: e.matmul(
                                u_[:], lhsT=wpsb[:, kc * D + dc * 128: kc * D + (dc + 1) * 128], rhs=pti[:, kc, :],
                                start=(kc == 0), stop=(kc == 1)), reads=[wpsb, pti], writes=[u_])
                        sg, e_ = sig[dc % 2], et[dc % 2]
                        c.op("scalar", lambda e: e.activation(out=sg[:], in_=g_[:], func=AF.Sigmoid), reads=[g_], writes=[sg])
                        c.op("vector", lambda e: e.tensor_tensor(out=e_[:], in0=u_[:], in1=sg[:], op=ALU.mult),
                             reads=[u_, sg], writes=[e_])
                        yd = y[dc]
                        a1 = t1[dc % 2]
                        c.op("gpsimd", lambda e: e.tensor_tensor(out=a1[:], in0=yd[:], in1=e_[:], op=ALU.add),
                             reads=[yd, e_], writes=[a1])
                        a2 = ybf[dc % 2]
                        c.op("scalar", lambda e: e.activation(out=a2[:], in_=a1[:], func=AF.Copy), reads=[a1], writes=[a2])
                        c.dma("gpsimd", self.outT[dc * 128:(dc + 1) * 128, tok], a1[:], reads=[a1], writes=[self.B_out])
                        c.dma("gpsimd", self.hTb[dc * 128:(dc + 1) * 128, tok], a2[:], reads=[a2], writes=[self.B_hTb])

            for i in range(NT + 1):
                if i < NT:
                    load(i)
                if i > 0:
                    compute(i - 1)
            c.barrier()

    def spload(self, c, es, name, eng="sync"):
        f, shape = self.spv(name)
        P = shape[0]
        m = int(np.prod(shape)) // P
        t = c.sb(es, [P, m], F32, name)
        c.dma(eng, t[:], f.rearrange("(p m) -> p m", p=P), writes=[t])
        return t

    def sin_tile(self, c, out, x, P, N, shift, tmp):
        a, xs, kf, m, ki = tmp["a"], tmp["xs"], tmp["kf"], tmp["m"], tmp["ki"]
        TWO_PI = 2.0 * math.pi
        C1 = 6.28125
        C2 = TWO_PI - C1
        V = lambda t: t[0:P, 0:N]
        c.op("vector", lambda e: e.tensor_scalar(out=V(a), in0=x, scalar1=float(shift), scalar2=None, op0=ALU.add),
             reads=[tmp["xb"]], writes=[a])
        c.op("vector", lambda e: e.tensor_scalar(out=V(xs), in0=V(a), scalar1=float(1.0 / TWO_PI), scalar2=None, op0=ALU.mult),
             reads=[a], writes=[xs])
        c.op("vector", lambda e: e.tensor_copy(out=V(ki), in_=V(xs)), reads=[xs], writes=[ki])
        c.op("vector", lambda e: e.tensor_copy(out=V(kf), in_=V(ki)), reads=[ki], writes=[kf])
        c.op("vector", lambda e: e.scalar_tensor_tensor(out=V(a), in0=V(kf), scalar=float(-C1), in1=V(a), op0=ALU.mult, op1=ALU.add),
             reads=[kf, a], writes=[a])
        c.op("vector", lambda e: e.scalar_tensor_tensor(out=V(a), in0=V(kf), scalar=float(-C2), in1=V(a), op0=ALU.mult, op1=ALU.add),
             reads=[kf, a], writes=[a])
        c.op("vector", lambda e: e.tensor_scalar(out=V(m), in0=V(a), scalar1=float(math.pi), scalar2=float(-TWO_PI),
                                                 op0=ALU.is_gt, op1=ALU.mult), reads=[a], writes=[m])
        c.op("vector", lambda e: e.tensor_tensor(out=V(a), in0=V(a), in1=V(m), op=ALU.add), reads=[a, m], writes=[a])
        c.op("vector", lambda e: e.tensor_scalar(out=V(m), in0=V(a), scalar1=float(-math.pi), scalar2=float(TWO_PI),
                                                 op0=ALU.is_lt, op1=ALU.mult), reads=[a], writes=[m])
        c.op("vector", lambda e: e.tensor_tensor(out=V(a), in0=V(a), in1=V(m), op=ALU.add), reads=[a, m], writes=[a])
        c.op("vector", lambda e: e.tensor_scalar(out=V(a), in0=V(a), scalar1=-3.1415925, scalar2=3.1415925,
                                                 op0=ALU.max, op1=ALU.min), reads=[a], writes=[a])
        c.op("scalar", lambda e: e.activation(out=out, in_=V(a), func=AF.Sin), reads=[a], writes=[tmp["ob"]])

    def even_in(self, c, L):
        j = L // 2
        with contextlib.ExitStack() as es:
            winf, _ = self.wv(L, "win")
            wsb = c.sb(es, [128, 8 * 2048], BF16, "winsb")
            wv_ = winf.rearrange("(p m) -> p m", p=128)
            for q in range(4):
                c.dma("sync", wsb[:, q * 4096:(q + 1) * 4096], wv_[:, q * 4096:(q + 1) * 4096], reads=[self.B_wb[L]], writes=[wsb])
            cw = self.spload(c, es, "convw%d" % j)
            hb = [c.sb(es, [128, 8, T], BF16, "hb") for _ in range(2)]
            pp = [c.ps(es, [128, T], F32, "pp") for _ in range(6)]
            hcs = [c.sb(es, [128, T], F32, "hcs") for _ in range(2)]
            ucx = [c.sb(es, [128, T + 2], F32, "ucx") for _ in range(4)]
            vt = [c.sb(es, [128, T], F32, "vt") for _ in range(2)]
            yab = [c.sb(es, [128, T], BF16, "yab") for _ in range(2)]
            u32 = [c.sb(es, [128, T], F32, "u32") for _ in range(2)]
            ubf = [c.sb(es, [128, T], BF16, "ubf") for _ in range(2)]
            for t_ in ucx:
                c.op("vector", lambda e: e.memset(t_[:], 0.0), writes=[t_])
            hTb_v = self.hTb.rearrange("(kc p) t -> p kc t", p=128)
            npp = [0]

            def proj(hbi, oc):
                p_ = pp[npp[0] % 6]
                npp[0] += 1
                for kc in range(8):
                    c.op("tensor", lambda e, kc=kc: e.matmul(p_[:], lhsT=wsb[:, kc * 2048 + oc * 128: kc * 2048 + (oc + 1) * 128],
                                                             rhs=hbi[:, kc, :], start=(kc == 0), stop=(kc == 7)),
                         reads=[wsb, hbi], writes=[p_])
                return p_

            def load(i):
                c.dma("sync", hb[i % 2][:], hTb_v[:, :, i * T:(i + 1) * T], reads=[self.B_hTb], writes=[hb[i % 2]])

            def compute(i):
                hbi = hb[i % 2]
                tok = slice(i * T, (i + 1) * T)
                for ch in range(4):
                    p_h = proj(hbi, ch)
                    p_c = proj(hbi, 8 + ch)
                    p_b = proj(hbi, 4 + ch)
                    hs = hcs[ch % 2]
                    c.op("scalar", lambda e: e.activation(out=hs[:], in_=p_h[:], func=AF.Copy), reads=[p_h], writes=[hs])
                    ux = ucx[ch]
                    c.op("vector", lambda e: e.tensor_tensor(out=ux[:, 2:T + 2], in0=hs[:], in1=p_c[:], op=ALU.mult),
                         reads=[hs, p_c], writes=[ux])
                    v_ = vt[ch % 2]
                    c.op("vector", lambda e: e.tensor_scalar(out=v_[:], in0=ux[:, 2:T + 2], scalar1=cw[:, ch * 3 + 2: ch * 3 + 3],
                                                             scalar2=None, op0=ALU.mult), reads=[ux, cw], writes=[v_])
                    c.op("vector", lambda e: e.scalar_tensor_tensor(out=v_[:], in0=ux[:, 1:T + 1], scalar=cw[:, ch * 3 + 1: ch * 3 + 2],
                                                                    in1=v_[:], op0=ALU.mult, op1=ALU.add), reads=[ux, cw, v_], writes=[v_])
                    c.op("vector", lambda e: e.scalar_tensor_tensor(out=v_[:], in0=ux[:, 0:T], scalar=cw[:, ch * 3: ch * 3 + 1],
                                                                    in1=v_[:], op0=ALU.mult, op1=ALU.add), reads=[ux, cw, v_], writes=[v_])
                    ya_ = yab[ch % 2]
                    c.op("vector", lambda e: e.tensor_tensor(out=ya_[:], in0=v_[:], in1=p_b[:], op=ALU.mult),
                         reads=[v_, p_b], writes=[ya_])
                    c.op("vector", lambda e: e.tensor_copy(out=ux[:, 0:2], in_=ux[:, T:T + 2]), reads=[ux], writes=[ux])
                    c.dma("gpsimd", self.yaTb[ch * 128:(ch + 1) * 128, tok], ya_[:], reads=[ya_], writes=[self.B_ya])
                for ch in range(4):
                    p_u = proj(hbi, 12 + ch)
                    a_, b_ = u32[ch % 2], ubf[ch % 2]
                    c.op("scalar", lambda e: e.activation(out=a_[:], in_=p_u[:], func=AF.Copy), reads=[p_u], writes=[a_])
                    c.op("scalar", lambda e: e.activation(out=b_[:], in_=p_u[:], func=AF.Copy), reads=[p_u], writes=[b_])
                    c.dma("gpsimd", self.uT32[ch * 128:(ch + 1) * 128, tok], a_[:], reads=[a_], writes=[self.B_u])
                    c.dma("gpsimd", self.uTb[ch * 128:(ch + 1) * 128, tok], b_[:], reads=[b_], writes=[self.B_u])

            for i in range(NT + 1):
                if i < NT:
                    load(i)
                if i > 0:
                    compute(i - 1)
            c.barrier()

    def even_s5(self, c, L):
        j = L // 2
        NJ = T + 1
        GK = math.sqrt(2.0 / math.pi)
        with contextlib.ExitStack() as es:
            sgn = self.spload(c, es, "sgn")
            ident = self.spload(c, es, "ident")
            swapP = self.spload(c, es, "swapP")
            dC = self.spload(c, es, "d_C%d" % j)
            magA = c.sb(es, [128, 32], F32, "magA")
            thA = c.sb(es, [128, 32], F32, "thA")
            LB1 = c.sb(es, [16, 32 * 128], BF16, "LB1")
            LB2 = c.sb(es, [16, 32 * 128], BF16, "LB2")
            LC1 = c.sb(es, [128, 512], BF16, "LC1")
            LC2 = c.sb(es, [128, 512], BF16, "LC2")
            Jf = c.sb(es, [128, NJ], F32, "Jf")
            with contextlib.ExitStack() as es2:
                lreA = self.spload(c, es2, "lamre_A%d" % j)
                limA = self.spload(c, es2, "lamim_A%d" % j)
                ldtA = self.spload(c, es2, "logdt_A%d" % j)
                dtA = c.sb(es2, [128, 32], F32, "dtA")
                c.op("scalar", lambda e: e.activation(out=dtA[:], in_=ldtA[:], func=AF.Exp), reads=[ldtA], writes=[dtA])
                c.op("vector", lambda e: e.tensor_scalar(out=lreA[:], in0=lreA[:], scalar1=-1e-4, scalar2=None, op0=ALU.min),
                     reads=[lreA], writes=[lreA])
                c.op("vector", lambda e: e.tensor_tensor(out=lreA[:], in0=lreA[:], in1=dtA[:], op=ALU.mult), reads=[lreA, dtA], writes=[lreA])
                c.op("scalar", lambda e: e.activation(out=magA[:], in_=lreA[:], func=AF.Exp), reads=[lreA], writes=[magA])
                c.op("vector", lambda e: e.tensor_tensor(out=thA[:], in0=limA[:], in1=dtA[:], op=ALU.mult), reads=[limA, dtA], writes=[thA])
                Ji = c.sb(es2, [128, NJ], I32, "Ji")
                c.op("gpsimd", lambda e: e.iota(Ji[:], pattern=[[1, NJ]], base=0, channel_multiplier=0), writes=[Ji])
                c.op("vector", lambda e: e.tensor_copy(out=Jf[:], in_=Ji[:]), reads=[Ji], writes=[Jf])
                lre = self.spload(c, es2, "lamre_C%d" % j)
                lim = self.spload(c, es2, "lamim_C%d" % j)
                ldt = self.spload(c, es2, "logdt_C%d" % j)
                br = self.spload(c, es2, "bre_C%d" % j)
                bi = self.spload(c, es2, "bim_C%d" % j)
                N = 512
                mk = lambda nm, dt=F32: c.sb(es2, [16, N], dt, nm)
                dt_, mag, ang, cs, sn = mk("dt"), mk("mag"), mk("ang"), mk("cs"), mk("sn")
                tmp = dict(a=mk("ta"), xs=mk("txs"), kf=mk("tkf"), m=mk("tm"), ki=mk("tki", I32))
                nr, ni, den, w1_, w2_, fre, fim = mk("nr"), mk("ni"), mk("den"), mk("w1"), mk("w2"), mk("fre"), mk("fim")
                bbre, bbim, lrq = mk("bbre"), mk("bbim"), mk("lrq")
                tt = lambda o_, a_, b_, op: c.op("vector", lambda e: e.tensor_tensor(out=o_[:], in0=a_[:], in1=b_[:], op=op),
                                                 reads=[a_, b_], writes=[o_])
                for gq in range(4):
                    sl = slice(gq * N, (gq + 1) * N)
                    c.op("scalar", lambda e: e.activation(out=dt_[:], in_=ldt[:, sl], func=AF.Exp), reads=[ldt], writes=[dt_])
                    c.op("vector", lambda e: e.tensor_scalar(out=lrq[:], in0=lre[:, sl], scalar1=-1e-4, scalar2=None, op0=ALU.min),
                         reads=[lre], writes=[lrq])
                    tt(mag, lrq, dt_, ALU.mult)
                    c.op("scalar", lambda e: e.activation(out=mag[:], in_=mag[:], func=AF.Exp), reads=[mag], writes=[mag])
                    c.op("vector", lambda e: e.tensor_tensor(out=ang[:], in0=lim[:, sl], in1=dt_[:], op=ALU.mult), reads=[lim, dt_], writes=[ang])
                    tmp["xb"] = ang
                    tmp["ob"] = cs
                    self.sin_tile(c, cs[:], ang[:], 16, N, math.pi / 2, tmp)
                    tmp["ob"] = sn
                    self.sin_tile(c, sn[:], ang[:], 16, N, 0.0, tmp)
                    tt(nr, mag, cs, ALU.mult)
                    c.op("vector", lambda e: e.tensor_scalar(out=nr[:], in0=nr[:], scalar1=-1.0, scalar2=None, op0=ALU.add), reads=[nr], writes=[nr])
                    tt(ni, mag, sn, ALU.mult)
                    tt(den, lrq, lrq, ALU.mult)
                    c.op("vector", lambda e: e.tensor_tensor(out=w1_[:], in0=lim[:, sl], in1=lim[:, sl], op=ALU.mult), reads=[lim], writes=[w1_])
                    tt(den, den, w1_, ALU.add)
                    c.op("vector", lambda e: e.reciprocal(out=den[:], in_=den[:]), reads=[den], writes=[den])
                    tt(w1_, nr, lrq, ALU.mult)
                    c.op("vector", lambda e: e.tensor_tensor(out=w2_[:], in0=ni[:], in1=lim[:, sl], op=ALU.mult), reads=[ni, lim], writes=[w2_])
                    tt(fre, w1_, w2_, ALU.add)
                    tt(fre, fre, den, ALU.mult)
                    tt(w1_, ni, lrq, ALU.mult)
                    c.op("vector", lambda e: e.tensor_tensor(out=w2_[:], in0=nr[:], in1=lim[:, sl], op=ALU.mult), reads=[nr, lim], writes=[w2_])
                    tt(fim, w1_, w2_, ALU.subtract)
                    tt(fim, fim, den, ALU.mult)
                    c.op("vector", lambda e: e.tensor_tensor(out=w1_[:], in0=fre[:], in1=br[:, sl], op=ALU.mult), reads=[fre, br], writes=[w1_])
                    c.op("vector", lambda e: e.tensor_tensor(out=w2_[:], in0=fim[:], in1=bi[:, sl], op=ALU.mult), reads=[fim, bi], writes=[w2_])
                    tt(bbre, w1_, w2_, ALU.subtract)
                    c.op("vector", lambda e: e.tensor_tensor(out=w1_[:], in0=fre[:], in1=bi[:, sl], op=ALU.mult), reads=[fre, bi], writes=[w1_])
                    c.op("vector", lambda e: e.tensor_tensor(out=w2_[:], in0=fim[:], in1=br[:, sl], op=ALU.mult), reads=[fim, br], writes=[w2_])
                    tt(bbim, w1_, w2_, ALU.add)
                    v3 = lambda t_: t_[:].rearrange("c (g p) -> c g p", g=8)
                    l3 = lambda t_, h: t_[:].rearrange("c (g m) -> c g m", g=32)[:, gq * 8:(gq + 1) * 8, h * 64:(h + 1) * 64]
                    c.op("vector", lambda e: e.tensor_copy(out=l3(LB1, 0), in_=v3(bbre)), reads=[bbre], writes=[LB1])
                    c.op("vector", lambda e: e.tensor_copy(out=l3(LB1, 1), in_=v3(bbim)), reads=[bbim], writes=[LB1])
                    c.op("vector", lambda e: e.tensor_copy(out=l3(LB2, 0), in_=v3(bbim)), reads=[bbim], writes=[LB2])
                    c.op("vector", lambda e: e.tensor_scalar(out=l3(LB2, 1), in0=v3(bbre), scalar1=-1.0, scalar2=None, op0=ALU.mult),
                         reads=[bbre], writes=[LB2])
                CA = self.spload(c, es2, "CA%d" % j)
                CB = self.spload(c, es2, "CB%d" % j)
                c.op("vector", lambda e: e.tensor_copy(out=LC1[0:64, :], in_=CA[0:64, :]), reads=[CA], writes=[LC1])
                c.op("vector", lambda e: e.tensor_scalar(out=LC1[64:128, :], in0=CA[64:128, :], scalar1=-1.0, scalar2=None, op0=ALU.mult),
                     reads=[CA], writes=[LC1])
                c.op("vector", lambda e: e.tensor_scalar(out=LC2[:], in0=CB[:], scalar1=-1.0, scalar2=None, op0=ALU.mult),
                     reads=[CB], writes=[LC2])
                c.barrier()
            COS = [c.sb(es, [128, NJ], F32, "COS") for _ in range(2)]
            SIN = [c.sb(es, [128, NJ], F32, "SIN") for _ in range(2)]
            ANG = [c.sb(es, [128, NJ], F32, "ANG") for _ in range(2)]
            mkA = lambda nm, dt=F32: c.sb(es, [128, NJ], dt, nm)
            tmpA = dict(a=mkA("ta"), xs=mkA("txs"), kf=mkA("tkf"), m=mkA("tm"), ki=mkA("tki", I32))
            MAGT = [c.sb(es, [128, T], F32, "MAGT") for _ in range(2)]
            onesF = c.sb(es, [128, T], F32, "onesF")
            c.op("vector", lambda e: e.memset(onesF[:], 1.0), writes=[onesF])
            Rm = [c.sb(es, [128, 128], F32, "Rm") for _ in range(2)]
            ss = [c.sb(es, [128, 1], F32, "ss") for _ in range(2)]
            ub = [c.sb(es, [16, T], BF16, "ub") for _ in range(3)]
            u32 = [c.sb(es, [16, T], F32, "u32") for _ in range(3)]
            P1 = [c.ps(es, [128, T], F32, "P1") for _ in range(2)]
            P2 = [c.ps(es, [128, T], F32, "P2") for _ in range(2)]
            py = [c.ps(es, [16, T], F32, "py") for _ in range(2)]
            pc = [c.ps(es, [128, 2], F32, "pc") for _ in range(2)]
            m1 = [c.sb(es, [128, T], F32, "m1") for _ in range(2)]
            m2 = [c.sb(es, [128, T], F32, "m2") for _ in range(2)]
            vv = [c.sb(es, [128, T], F32, "vv") for _ in range(2)]
            zz = [c.sb(es, [128, T], F32, "zz") for _ in range(2)]
            q1 = [c.sb(es, [128, T], BF16, "q1") for _ in range(2)]
            q2 = [c.sb(es, [128, T], BF16, "q2") for _ in range(2)]
            init = [c.sb(es, [128, 1], F32, "init") for _ in range(2)]
            yy = [c.sb(es, [16, T], F32, "yy") for _ in range(2)]
            g1 = [c.sb(es, [16, T], F32, "g1") for _ in range(2)]
            g2 = [c.sb(es, [16, T], F32, "g2") for _ in range(2)]
            zo = [c.sb(es, [16, T], F32, "zo") for _ in range(2)]
            zb = [c.sb(es, [16, T], BF16, "zb") for _ in range(2)]
            n = 0
            for g in range(32):
                cs_, sn_, an_ = COS[g % 2], SIN[g % 2], ANG[g % 2]
                c.op("vector", lambda e: e.tensor_scalar(out=an_[:], in0=Jf[:], scalar1=thA[:, g:g + 1], scalar2=None, op0=ALU.mult),
                     reads=[Jf, thA], writes=[an_])
                tmpA["xb"] = an_
                tmpA["ob"] = cs_
                self.sin_tile(c, cs_[:], an_[:], 128, NJ, math.pi / 2, tmpA)
                tmpA["ob"] = sn_
                self.sin_tile(c, sn_[:], an_[:], 128, NJ, 0.0, tmpA)
                mg = MAGT[g % 2]
                c.op("vector", lambda e: e.tensor_scalar(out=mg[:], in0=onesF[:], scalar1=magA[:, g:g + 1], scalar2=None, op0=ALU.mult),
                     reads=[onesF, magA], writes=[mg])
                R_, s_ = Rm[g % 2], ss[g % 2]
                c.op("vector", lambda e: e.tensor_tensor(out=s_[:], in0=sn_[:, T:T + 1], in1=sgn[:], op=ALU.mult), reads=[sn_, sgn], writes=[s_])
                c.op("vector", lambda e: e.tensor_scalar(out=R_[:], in0=ident[:], scalar1=cs_[:, T:T + 1], scalar2=None, op0=ALU.mult),
                     reads=[ident, cs_], writes=[R_])
                c.op("vector", lambda e: e.scalar_tensor_tensor(out=R_[:], in0=swapP[:], scalar=s_[:, 0:1], in1=R_[:], op0=ALU.mult, op1=ALU.add),
                     reads=[swapP, s_, R_], writes=[R_])
                for i in range(NT):
                    tok = slice(i * T, (i + 1) * T)
                    ub_, u32_ = ub[n % 3], u32[n % 3]
                    c.dma("sync", ub_[:], self.uTb[g * 16:(g + 1) * 16, tok], reads=[self.B_u], writes=[ub_])
                    c.dma("sync", u32_[:], self.uT32[g * 16:(g + 1) * 16, tok], reads=[self.B_u], writes=[u32_])
                    p1, p2 = P1[n % 2], P2[n % 2]
                    c.op("tensor", lambda e: e.matmul(p1[:], lhsT=LB1[:, g * 128:(g + 1) * 128], rhs=ub_[:], start=True, stop=True),
                         reads=[LB1, ub_], writes=[p1])
                    c.op("tensor", lambda e: e.matmul(p2[:], lhsT=LB2[:, g * 128:(g + 1) * 128], rhs=ub_[:], start=True, stop=True),
                         reads=[LB2, ub_], writes=[p2])
                    a_, b_, v_, z_ = m1[n % 2], m2[n % 2], vv[n % 2], zz[n % 2]
                    c.op("vector", lambda e: e.tensor_tensor(out=a_[:], in0=cs_[:, 0:T], in1=p1[:], op=ALU.mult), reads=[cs_, p1], writes=[a_])
                    c.op("vector", lambda e: e.tensor_tensor(out=b_[:], in0=sn_[:, 0:T], in1=p2[:], op=ALU.mult), reads=[sn_, p2], writes=[b_])
                    c.op("gpsimd", lambda e: e.tensor_tensor(out=v_[:], in0=a_[:], in1=b_[:], op=ALU.add), reads=[a_, b_], writes=[v_])
                    if i == 0:
                        c.op("vector", lambda e: e.tensor_tensor_scan(out=z_[:], data0=mg[:], data1=v_[:], initial=0.0,
                                                                      op0=ALU.mult, op1=ALU.add), reads=[mg, v_], writes=[z_])
                    else:
                        ini = init[i % 2]
                        c.op("vector", lambda e: e.tensor_tensor_scan(out=z_[:], data0=mg[:], data1=v_[:], initial=ini[:, 0:1],
                                                                      op0=ALU.mult, op1=ALU.add), reads=[mg, v_, ini], writes=[z_])
                    if i < NT - 1:
                        pc_ = pc[i % 2]
                        nini = init[(i + 1) % 2]
                        c.op("tensor", lambda e: e.matmul(pc_[:], lhsT=R_[:], rhs=z_[:, T - 2:T], start=True, stop=True),
                             reads=[R_, z_], writes=[pc_])
                        c.op("scalar", lambda e: e.activation(out=nini[:], in_=pc_[:, 1:2], func=AF.Copy), reads=[pc_], writes=[nini])
                    qa, qb = q1[n % 2], q2[n % 2]
                    c.op("gpsimd", lambda e: e.tensor_tensor(out=qa[:], in0=cs_[:, 0:T], in1=z_[:], op=ALU.mult), reads=[cs_, z_], writes=[qa])
                    c.op("gpsimd", lambda e: e.tensor_tensor(out=qb[:], in0=sn_[:, 0:T], in1=z_[:], op=ALU.mult), reads=[sn_, z_], writes=[qb])
                    py_ = py[n % 2]
                    c.op("tensor", lambda e: e.matmul(py_[:], lhsT=LC1[:, g * 16:(g + 1) * 16], rhs=qa[:], start=True, stop=False),
                         reads=[LC1, qa], writes=[py_])
                    c.op("tensor", lambda e: e.matmul(py_[:], lhsT=LC2[:, g * 16:(g + 1) * 16], rhs=qb[:], start=False, stop=True),
                         reads=[LC2, qb], writes=[py_])
                    y_, ga, gb_, zo_, zb_ = yy[n % 2], g1[n % 2], g2[n % 2], zo[n % 2], zb[n % 2]
                    c.op("vector", lambda e: e.scalar_tensor_tensor(out=y_[:], in0=u32_[:], scalar=dC[:, g:g + 1], in1=py_[:],
                                                                    op0=ALU.mult, op1=ALU.add), reads=[u32_, dC, py_], writes=[y_])
                    c.op("scalar", lambda e: e.activation(out=ga[:], in_=y_[:], func=AF.Square), reads=[y_], writes=[ga])
                    c.op("vector", lambda e: e.tensor_scalar(out=ga[:], in0=ga[:], scalar1=0.044715, scalar2=1.0, op0=ALU.mult, op1=ALU.add),
                         reads=[ga], writes=[ga])
                    c.op("vector", lambda e: e.tensor_tensor(out=gb_[:], in0=ga[:], in1=y_[:], op=ALU.mult), reads=[ga, y_], writes=[gb_])
                    c.op("scalar", lambda e: e.activation(out=gb_[:], in_=gb_[:], func=AF.Sigmoid, scale=float(2.0 * GK)), reads=[gb_], writes=[gb_])
                    c.op("vector", lambda e: e.tensor_tensor(out=zo_[:], in0=gb_[:], in1=y_[:], op=ALU.mult), reads=[gb_, y_], writes=[zo_])
                    c.op("scalar", lambda e: e.activation(out=zb_[:], in_=zo_[:], func=AF.Copy), reads=[zo_], writes=[zb_])
                    c.dma("gpsimd", self.zT32[g * 16:(g + 1) * 16, tok], zo_[:], reads=[zo_], writes=[self.B_z])
                    c.dma("gpsimd", self.zTb[g * 16:(g + 1) * 16, tok], zb_[:], reads=[zb_], writes=[self.B_z])
                    n += 1
            c.barrier()

    def even_out(self, c, L):
        j = L // 2
        with contextlib.ExitStack() as es:
            woutf, _ = self.wv(L, "wout")
            wgluf, _ = self.wv(L, "wglu")
            wout = c.sb(es, [128, 8 * D], BF16, "wout")
            wglu = c.sb(es, [128, 4 * 512], BF16, "wglu")
            c.dma("sync", wout[:], woutf.rearrange("(p m) -> p m", p=128), reads=[self.B_wb[L]], writes=[wout])
            c.dma("sync", wglu[:], wgluf.rearrange("(p m) -> p m", p=128), reads=[self.B_wb[L]], writes=[wglu])
            bglu = self.spload(c, es, "bglu%d" % j)
            o = self.ln_setup(c, es)
            zb = [c.sb(es, [128, 4, T], BF16, "zb") for _ in range(2)]
            z32 = [c.sb(es, [128, 4, T], F32, "z32") for _ in range(2)]
            yab = [c.sb(es, [128, 4, T], BF16, "yab") for _ in range(2)]
            ybb = [c.sb(es, [128, T], BF16, "ybb") for _ in range(4)]
            sig = [c.sb(es, [128, T], F32, "sig") for _ in range(2)]
            h32c = [c.sb(es, [128, T], F32, "h32c") for _ in range(3)]
            pg = [c.ps(es, [128, T], F32, "pg") for _ in range(2)]
            pd = [c.ps(es, [128, T], F32, "pd") for _ in range(2)]
            lcol = (L * 3 + 1) * 8
            v4 = lambda ap: ap.rearrange("(kc p) t -> p kc t", p=128)

            def load(i):
                tok = slice(i * T, (i + 1) * T)
                c.dma("sync", zb[i % 2][:], v4(self.zTb)[:, :, tok], reads=[self.B_z], writes=[zb[i % 2]])
                c.dma("sync", z32[i % 2][:], v4(self.zT32)[:, :, tok], reads=[self.B_z], writes=[z32[i % 2]])
                c.dma("sync", yab[i % 2][:], v4(self.yaTb)[:, :, tok], reads=[self.B_ya], writes=[yab[i % 2]])

            def compute(i):
                tok = slice(i * T, (i + 1) * T)
                zbi, z32i, yai = zb[i % 2], z32[i % 2], yab[i % 2]
                for oc in range(4):
                    g_ = pg[oc % 2]
                    for kc in range(4):
                        c.op("tensor", lambda e, kc=kc: e.matmul(g_[:], lhsT=wglu[:, kc * 512 + oc * 128: kc * 512 + (oc + 1) * 128],
                                                                 rhs=zbi[:, kc, :], start=(kc == 0), stop=(kc == 3)),
                             reads=[wglu, zbi], writes=[g_])
                    sg = sig[oc % 2]
                    c.op("scalar", lambda e: e.activation(out=sg[:], in_=g_[:], func=AF.Sigmoid, bias=bglu[:, oc:oc + 1]),
                         reads=[g_, bglu], writes=[sg])
                    yb_ = ybb[oc]
                    c.op("vector", lambda e: e.tensor_tensor(out=yb_[:], in0=z32i[:, oc, :], in1=sg[:], op=ALU.mult),
                         reads=[z32i, sg], writes=[yb_])
                for dc in range(8):
                    hr = h32c[dc % 3]
                    c.dma("sync", hr[:], self.outT[dc * 128:(dc + 1) * 128, tok], reads=[self.B_out], writes=[hr])
                    p_ = pd[dc % 2]
                    for kc in range(8):
                        rhs_b = yai if kc < 4 else ybb[kc - 4]
                        rhs = yai[:, kc, :] if kc < 4 else ybb[kc - 4][:]
                        c.op("tensor", lambda e, kc=kc, rhs=rhs: e.matmul(
                            p_[:], lhsT=wout[:, kc * D + dc * 128: kc * D + (dc + 1) * 128], rhs=rhs,
                            start=(kc == 0), stop=(kc == 7)), reads=[wout, rhs_b], writes=[p_])
                    yd = o.y[dc]
                    c.op("vector", lambda e: e.scalar_tensor_tensor(out=yd[:], in0=hr[:], scalar=float(ALPHA), in1=p_[:],
                                                                    op0=ALU.mult, op1=ALU.add), reads=[hr, p_], writes=[yd])
                    self.ln_accum(c, o, dc)
                self.ln_finish(c, o, lcol, tok, store=True)

            for i in range(NT + 1):
                if i < NT:
                    load(i)
                if i > 0:
                    compute(i - 1)
            c.barrier()

    def rope_tables(self, c):
        CH = 512
        with contextlib.ExitStack() as es:
            rc = self.spload(c, es, "ropec")
            posi = [c.sb(es, [128, CH], I32, "posi") for _ in range(2)]
            posf = [c.sb(es, [128, CH], F32, "posf") for _ in range(2)]
            ang = [c.sb(es, [128, CH], F32, "ang") for _ in range(2)]
            outt = [c.sb(es, [128, CH], F32, "outt") for _ in range(4)]
            mk = lambda nm, dt=F32: c.sb(es, [128, CH], dt, nm)
            tmp = dict(a=mk("ta"), xs=mk("txs"), kf=mk("tkf"), m=mk("tm"), ki=mk("tki", I32))
            n = 0
            for i in range(S // CH):
                sl = slice(i * CH, (i + 1) * CH)
                pi_, pf_ = posi[i % 2], posf[i % 2]
                c.dma("sync", pi_[:], self.posb[:, sl], writes=[pi_])
                c.op("vector", lambda e: e.tensor_copy(out=pf_[:], in_=pi_[:]), reads=[pi_], writes=[pf_])
                for ty in range(2):
                    an_ = ang[ty]
                    c.op("vector", lambda e: e.tensor_scalar(out=an_[:], in0=pf_[:], scalar1=rc[:, 2 * ty:2 * ty + 1], scalar2=None,
                                                             op0=ALU.mult), reads=[pf_, rc], writes=[an_])
                    tmp["xb"] = an_
                    oc_ = outt[n % 4]
                    n += 1
                    tmp["ob"] = oc_
                    self.sin_tile(c, oc_[:], an_[:], 128, CH, math.pi / 2, tmp)
                    c.dma("gpsimd", self.tabs[2 * ty, :, sl], oc_[:], reads=[oc_], writes=[self.B_tabs])
                    os_ = outt[n % 4]
                    n += 1
                    tmp["ob"] = os_
                    self.sin_tile(c, os_[:], an_[:], 128, CH, 0.0, tmp)
                    c.op("vector", lambda e: e.tensor_scalar(out=os_[:], in0=os_[:], scalar1=rc[:, 2 * ty + 1:2 * ty + 2], scalar2=None,
                                                             op0=ALU.mult), reads=[os_, rc], writes=[os_])
                    c.dma("gpsimd", self.tabs[2 * ty + 1, :, sl], os_[:], reads=[os_], writes=[self.B_tabs])
            c.barrier()

    def odd_in(self, c, L):
        with contextlib.ExitStack() as es:
            cinf, _ = self.wv(L, "cin")
            cswf, _ = self.wv(L, "cinsw")
            NM, NS = 2176, 1920
            wm = c.sb(es, [128, 8 * NM], BF16, "wm")
            ws = c.sb(es, [128, 8 * NS], BF16, "ws")
            wmv = cinf.rearrange("(p m) -> p m", p=128)
            wsv = cswf.rearrange("(p m) -> p m", p=128)
            for q in range(4):
                c.dma("sync", wm[:, q * 2 * NM:(q + 1) * 2 * NM], wmv[:, q * 2 * NM:(q + 1) * 2 * NM], reads=[self.B_wb[L]], writes=[wm])
                c.dma("sync", ws[:, q * 2 * NS:(q + 1) * 2 * NS], wsv[:, q * 2 * NS:(q + 1) * 2 * NS], reads=[self.B_wb[L]], writes=[ws])
            hb = [c.sb(es, [128, 8, T], BF16, "hb") for _ in range(2)]
            tb = [c.sb(es, [128, 4, T], F32, "tb") for _ in range(2)]
            pm = [c.ps(es, [128, T], F32, "pm") for _ in range(3)]
            psw = [c.ps(es, [128, T], F32, "psw") for _ in range(3)]
            pvw = [c.ps(es, [128, 512], F32, "pvw") for _ in range(2)]
            ta = [c.sb(es, [128, T], F32, "ta") for _ in range(2)]
            tb2 = [c.sb(es, [128, T], F32, "tb2") for _ in range(2)]
            ro = [c.sb(es, [128, T], BF16, "ro") for _ in range(3)]
            vo = [c.sb(es, [128, 256], BF16, "vo") for _ in range(2)]
            wo = [c.sb(es, [128, 8], F32, "wo") for _ in range(2)]
            hTb_v = self.hTb.rearrange("(kc p) t -> p kc t", p=128)
            tabs_v = self.tabs.rearrange("f p t -> p f t")
            chunks = []
            for h in range(8):
                chunks.append((self.qT[h * 128:(h + 1) * 128], h * 128, h, 0, 128))
            for g in range(2):
                chunks.append((self.kT[g * 128:(g + 1) * 128], 1024 + g * 128, 8 + g, 0, 128))
            for q in range(4):
                chunks.append((self.qiT[q * 128:(q + 1) * 128], 1536 + q * 128, 10 + q, 1, 128))
            chunks.append((None, 2048, 14, 1, 64))
            cnt = [0]

            def load(i):
                tok = slice(i * T, (i + 1) * T)
                c.dma("sync", hb[i % 2][:], hTb_v[:, :, tok], reads=[self.B_hTb], writes=[hb[i % 2]])
                c.dma("sync", tb[i % 2][:], tabs_v[:, :, tok], reads=[self.B_tabs], writes=[tb[i % 2]])

            def compute(i):
                tok = slice(i * T, (i + 1) * T)
                hbi, tbi = hb[i % 2], tb[i % 2]
                for (dst, mo, sc, ty, rows) in chunks:
                    k = cnt[0]
                    cnt[0] += 1
                    p_m, p_s = pm[k % 3], psw[k % 3]
                    for kc in range(8):
                        c.op("tensor", lambda e, kc=kc: e.matmul(p_m[0:rows, :], lhsT=wm[:, kc * NM + mo: kc * NM + mo + rows],
                                                                 rhs=hbi[:, kc, :], start=(kc == 0), stop=(kc == 7)),
                             reads=[wm, hbi], writes=[p_m])
                    for kc in range(8):
                        c.op("tensor", lambda e, kc=kc: e.matmul(p_s[0:rows, :], lhsT=ws[:, kc * NS + sc * 128: kc * NS + sc * 128 + rows],
                                                                 rhs=hbi[:, kc, :], start=(kc == 0), stop=(kc == 7)),
                             reads=[ws, hbi], writes=[p_s])
                    a_, b_, r_ = ta[k % 2], tb2[k % 2], ro[k % 3]
                    c.op("vector", lambda e: e.tensor_tensor(out=a_[0:rows, :], in0=tbi[0:rows, 2 * ty, :], in1=p_m[0:rows, :], op=ALU.mult),
                         reads=[tbi, p_m], writes=[a_])
                    c.op("vector", lambda e: e.tensor_tensor(out=b_[0:rows, :], in0=tbi[0:rows, 2 * ty + 1, :], in1=p_s[0:rows, :], op=ALU.mult),
                         reads=[tbi, p_s], writes=[b_])
                    c.op("gpsimd", lambda e: e.tensor_tensor(out=r_[0:rows, :], in0=a_[0:rows, :], in1=b_[0:rows, :], op=ALU.add),
                         reads=[a_, b_], writes=[r_])
                    if dst is not None:
                        c.dma("gpsimd", dst[:, tok], r_[:], reads=[r_], writes=[self.B_qkv])
                    else:
                        c.dma("gpsimd", self.kiT[0:64, tok], r_[0:64, :], reads=[r_], writes=[self.B_qkv])
                        c.dma("gpsimd", self.kiT[64:128, tok], r_[0:64, :], reads=[r_], writes=[self.B_qkv])
                for sub in range(4):
                    p_vw = pvw[sub % 2]
                    for kc in range(8):
                        c.op("tensor", lambda e, kc=kc: e.matmul(p_vw[:, 0:256], lhsT=hbi[:, kc, sub * 128:(sub + 1) * 128],
                                                                 rhs=wm[:, kc * NM + 1280: kc * NM + 1536], start=(kc == 0), stop=(kc == 7)),
                             reads=[wm, hbi], writes=[p_vw])
                    for kc in range(8):
                        c.op("tensor", lambda e, kc=kc: e.matmul(p_vw[:, 256:264], lhsT=hbi[:, kc, sub * 128:(sub + 1) * 128],
                                                                 rhs=wm[:, kc * NM + 2112: kc * NM + 2120], start=(kc == 0), stop=(kc == 7)),
                             reads=[wm, hbi], writes=[p_vw])
                    v_, w_ = vo[sub % 2], wo[sub % 2]
                    c.op("scalar", lambda e: e.activation(out=v_[:], in_=p_vw[:, 0:256], func=AF.Copy), reads=[p_vw], writes=[v_])
                    c.op("scalar", lambda e: e.activation(out=w_[:], in_=p_vw[:, 256:264], func=AF.Copy, scale=float(512 ** -0.5)), reads=[p_vw], writes=[w_])
                    t0 = i * T + sub * 128
                    c.dma("gpsimd", self.vtok[t0:t0 + 128, :], v_[:], reads=[v_], writes=[self.B_qkv])
                    c.dma("gpsimd", self.witok[t0:t0 + 128, :], w_[:], reads=[w_], writes=[self.B_qkv])

            for i in range(NT + 1):
                if i < NT:
                    load(i)
                if i > 0:
                    compute(i - 1)
            c.barrier()

    def odd_attn(self, c, L, nqs=None):
        QS = 256
        NQS = S // QS
        NIT = 16
        with contextlib.ExitStack() as es:
            KT = c.sb(es, [128, 2, S], BF16, "KT")
            V = c.sb(es, [128, 64, 256], BF16, "V")
            kT_v = self.kT.rearrange("(g p) t -> p g t", p=128)
            v_v = self.vtok.rearrange("(kb p) c -> p kb c", p=128)
            for q in range(4):
                c.dma("sync", KT[:, :, q * 2048:(q + 1) * 2048], kT_v[:, :, q * 2048:(q + 1) * 2048], reads=[self.B_qkv], writes=[KT])
                c.dma("sync", V[:, q * 16:(q + 1) * 16, :], v_v[:, q * 16:(q + 1) * 16, :], reads=[self.B_qkv], writes=[V])
            identf = self.spload(c, es, "ident")
            Wneg = self.spload(c, es, "Wneg")
            Wpos = self.spload(c, es, "Wpos")
            identb = c.sb(es, [128, 128], BF16, "identb")
            onesb = c.sb(es, [128, 128], BF16, "onesb")
            c.op("vector", lambda e: e.tensor_copy(out=identb[:], in_=identf[:]), reads=[identf], writes=[identb])
            c.op("vector", lambda e: e.memset(onesb[:], 1.0), writes=[onesb])
            row = c.sb(es, [128, S], F32, "row")
            msk = c.sb(es, [128, S], BF16, "msk")
            maskT = c.sb(es, [128, 64, QS], BF16, "maskT")
            kib = [c.sb(es, [128, 512], BF16, "kib") for _ in range(3)]
            qib = [c.sb(es, [128, 4, 128], BF16, "qib") for _ in range(2)]
            wib = [c.sb(es, [128, 8], F32, "wib") for _ in range(2)]
            dg = [c.sb(es, [128, 128], BF16, "dg") for _ in range(8)]
            rl = [c.sb(es, [128, 512], BF16, "rl") for _ in range(3)]
            tmpd = c.sb(es, [128, 512], F32, "tmpd")
            st8 = c.sb(es, [128, 8], F32, "st8")
            sc = {k: c.sb(es, [128, 1], F32, k) for k in ["lo", "hi", "mid", "cnt", "ge", "d1", "d2", "dmin", "omin"]}
            qTb = [c.sb(es, [128, 8, QS], BF16, "qTb") for _ in range(2)]
            ex = [c.sb(es, [128, QS], BF16, "ex") for _ in range(3)]
            pb = [c.sb(es, [128, QS], BF16, "pb") for _ in range(3)]
            oev = [c.sb(es, [128, QS], F32, "oev") for _ in range(2)]
            lev = [c.sb(es, [1, QS], F32, "lev") for _ in range(2)]
            plg = [c.ps(es, [128, 512], F32, "plg") for _ in range(2)]
            psc = [c.ps(es, [128, 512], F32, "psc") for _ in range(2)]
            ptm = c.ps(es, [128, 512], BF16, "ptm")
            def halves(nm):
                c.nbuf += 1
                t_ = es.enter_context(self.nc.psum_tensor("%s_%d" % (nm, c.nbuf), [128, 2 * QS], F32))
                return [Buf(t_[:, 0:QS], nm + "0"), Buf(t_[:, QS:2 * QS], nm + "1")]
            pst = halves("pst")
            po = halves("po")
            pl = halves("pl")
            qiT_v = self.qiT.rearrange("(q p) t -> p q t", p=128)
            qT_v = self.qT.rearrange("(h p) t -> p h t", p=128)
            nk = [0]
            nh = [0]
            for Qs in range(NQS if nqs is None else nqs):
                q0 = Qs * QS
                qt = qTb[Qs % 2]
                c.dma("sync", qt[:], qT_v[:, :, q0:q0 + QS], reads=[self.B_qkv], writes=[qt])
                c.op("gpsimd", lambda e: e.memset(maskT[:, 2 * Qs + 1, 0:128], 0.0), writes=[maskT])
                for b in range(2):
                    i = 2 * Qs + b
                    t0 = i * 128
                    nkb = i // 4 + 1
                    n = nkb * 512
                    qi_, wi_ = qib[i % 2], wib[i % 2]
                    c.dma("sync", qi_[:], qiT_v[:, :, t0:t0 + 128], reads=[self.B_qkv], writes=[qi_])
                    c.dma("sync", wi_[:], self.witok[t0:t0 + 128, :], reads=[self.B_qkv], writes=[wi_])
                    for h in range(8):
                        c.op("vector", lambda e, h=h: e.tensor_scalar(out=dg[h][:], in0=identb[:], scalar1=wi_[:, h:h + 1], scalar2=None,
                                                                      op0=ALU.mult), reads=[identb, wi_], writes=[dg[h]])
                    for kb in range(nkb):
                        ki_ = kib[nk[0] % 3]
                        c.dma("sync", ki_[:], self.kiT[:, kb * 512:(kb + 1) * 512], reads=[self.B_qkv], writes=[ki_])
                        ps_ = psc[nk[0] % 2]
                        nk[0] += 1
                        for h in range(8):
                            pl_ = plg[h % 2]
                            r0 = (h % 2) * 64
                            c.op("tensor", lambda e: e.matmul(pl_[:], lhsT=qi_[r0:r0 + 64, h // 2, :], rhs=ki_[r0:r0 + 64, :],
                                                              start=True, stop=True), reads=[qi_, ki_], writes=[pl_])
                            r_ = rl[h % 3]
                            c.op("scalar", lambda e: e.activation(out=r_[:], in_=pl_[:], func=AF.Relu), reads=[pl_], writes=[r_])
                            c.op("tensor", lambda e: e.matmul(ps_[:], lhsT=dg[h][:], rhs=r_[:], start=(h == 0), stop=(h == 7)),
                                 reads=[dg[h], r_], writes=[ps_])
                        ksl = slice(kb * 512, (kb + 1) * 512)
                        if kb == nkb - 1:
                            v = i % 4
                            wsl = slice((3 - v) * 128, (3 - v) * 128 + 512)
                            c.op("vector", lambda e: e.tensor_tensor(out=row[:, ksl], in0=Wneg[:, wsl], in1=ps_[:], op=ALU.add),
                                 reads=[Wneg, ps_], writes=[row])
                            c.op("vector", lambda e: e.tensor_tensor(out=tmpd[:], in0=Wpos[:, wsl], in1=ps_[:], op=ALU.add),
                                 reads=[Wpos, ps_], writes=[tmpd])
                            c.op("vector", lambda e: e.tensor_reduce(out=sc["dmin"][:], in_=tmpd[:], axis=AX.X, op=ALU.min),
                                 reads=[tmpd], writes=[sc["dmin"]])
                        else:
                            c.op("vector", lambda e: e.tensor_copy(out=row[:, ksl], in_=ps_[:]), reads=[ps_], writes=[row])
                    c.op("vector", lambda e: e.max(out=st8[:], in_=row[:, 0:n]), reads=[row], writes=[st8])
                    c.op("vector", lambda e: e.tensor_copy(out=sc["hi"][:], in_=st8[:, 0:1]), reads=[st8], writes=[sc["hi"]])
                    if nkb > 1:
                        c.op("vector", lambda e: e.tensor_reduce(out=sc["omin"][:], in_=row[:, 0:n - 512], axis=AX.X, op=ALU.min),
                             reads=[row], writes=[sc["omin"]])
                        c.op("vector", lambda e: e.tensor_tensor(out=sc["lo"][:], in0=sc["dmin"][:], in1=sc["omin"][:], op=ALU.min),
                             reads=[sc["dmin"], sc["omin"]], writes=[sc["lo"]])
                    else:
                        c.op("vector", lambda e: e.tensor_copy(out=sc["lo"][:], in_=sc["dmin"][:]), reads=[sc["dmin"]], writes=[sc["lo"]])
                    if i >= 2:
                        for it in range(NIT):
                            lo, hi, mid, cnt, ge, d1, d2 = (sc[k] for k in ["lo", "hi", "mid", "cnt", "ge", "d1", "d2"])
                            c.op("vector", lambda e: e.tensor_scalar(out=mid[:], in0=lo[:], scalar1=hi[:, 0:1], scalar2=0.5,
                                                                     op0=ALU.add, op1=ALU.mult), reads=[lo, hi], writes=[mid])
                            c.op("vector", lambda e: e.tensor_scalar(out=msk[:, 0:n], in0=row[:, 0:n], scalar1=mid[:, 0:1], scalar2=None,
                                                                     op0=ALU.is_ge, op1=ALU.add, accum_out=cnt[:]),
                                 reads=[row, mid], writes=[msk, cnt])
                            c.op("vector", lambda e: e.tensor_scalar(out=ge[:], in0=cnt[:], scalar1=255.5, scalar2=None, op0=ALU.is_ge),
                                 reads=[cnt], writes=[ge])
                            c.op("vector", lambda e: e.tensor_tensor(out=d1[:], in0=mid[:], in1=lo[:], op=ALU.subtract), reads=[mid, lo], writes=[d1])
                            c.op("vector", lambda e: e.tensor_tensor(out=d2[:], in0=hi[:], in1=mid[:], op=ALU.subtract), reads=[hi, mid], writes=[d2])
                            c.op("vector", lambda e: e.scalar_tensor_tensor(out=lo[:], in0=d1[:], scalar=ge[:, 0:1], in1=lo[:],
                                                                            op0=ALU.mult, op1=ALU.add), reads=[d1, ge, lo], writes=[lo])
                            c.op("vector", lambda e: e.scalar_tensor_tensor(out=hi[:], in0=d2[:], scalar=ge[:, 0:1], in1=mid[:],
                                                                            op0=ALU.mult, op1=ALU.add), reads=[d2, ge, mid], writes=[hi])
                    nv = (i + 1) * 128
                    c.op("vector", lambda e: e.tensor_scalar(out=msk[:, 0:nv], in0=row[:, 0:nv], scalar1=sc["lo"][:, 0:1], scalar2=None,
                                                             op0=ALU.is_ge), reads=[row, sc["lo"]], writes=[msk])
                    for k4 in range(0, i + 1, 4):
                        m4 = min(4, i + 1 - k4)
                        for kk in range(m4):
                            c.op("tensor", lambda e, kk=kk: e.transpose(out=ptm[:, kk * 128:(kk + 1) * 128],
                                                                        in_=msk[:, (k4 + kk) * 128:(k4 + kk + 1) * 128], identity=identb[:]),
                                 reads=[msk, identb], writes=[ptm])
                        c.op("scalar", lambda e: e.activation(
                            out=maskT[:, k4:k4 + m4, b * 128:(b + 1) * 128],
                            in_=ptm[:, 0:m4 * 128].rearrange("p (k q) -> p k q", k=m4), func=AF.Copy), reads=[ptm], writes=[maskT])
                nkk = 2 * Qs + 2
                for h in range(8):
                    g = h // 4
                    po_, pl_ = po[nh[0] % 2], pl[nh[0] % 2]
                    for kk in range(nkk):
                        st_ = pst[kk % 2]
                        c.op("tensor", lambda e: e.matmul(st_[:], lhsT=KT[:, g, kk * 128:(kk + 1) * 128], rhs=qt[:, h, :], start=True, stop=True),
                             reads=[KT, qt], writes=[st_])
                        e_, p_ = ex[kk % 3], pb[kk % 3]
                        c.op("scalar", lambda e: e.activation(out=e_[:], in_=st_[:], func=AF.Exp, scale=float(128 ** -0.5)), reads=[st_], writes=[e_])
                        c.op("gpsimd", lambda e: e.tensor_tensor(out=p_[:], in0=e_[:], in1=maskT[:, kk, :], op=ALU.mult),
                             reads=[e_, maskT], writes=[p_])
                        c.op("tensor", lambda e: e.matmul(po_[:], lhsT=V[:, kk, g * 128:(g + 1) * 128], rhs=p_[:], start=(kk == 0), stop=(kk == nkk - 1)),
                             reads=[V, p_], writes=[po_])
                        c.op("tensor", lambda e: e.matmul(pl_[:], lhsT=onesb[:], rhs=p_[:], start=(kk == 0), stop=(kk == nkk - 1)),
                             reads=[onesb, p_], writes=[pl_])
                    o_, l_ = oev[nh[0] % 2], lev[nh[0] % 2]
                    nh[0] += 1
                    c.op("scalar", lambda e: e.activation(out=o_[:], in_=po_[:], func=AF.Copy), reads=[po_], writes=[o_])
                    c.op("scalar", lambda e: e.activation(out=l_[:], in_=pl_[0:1, :], func=AF.Copy), reads=[pl_], writes=[l_])
                    c.dma("gpsimd", self.attT[h * 128:(h + 1) * 128, q0:q0 + QS], o_[:], reads=[o_], writes=[self.B_att])
                    c.dma("gpsimd", self.lden[h:h + 1, q0:q0 + QS], l_[:], reads=[l_], writes=[self.B_att])
            c.barrier()

    def odd_out(self, c, L):
        with contextlib.ExitStack() as es:
            coutf, _ = self.wv(L, "cout")
            wout = c.sb(es, [128, 8 * D], BF16, "wout")
            c.dma("sync", wout[:], coutf.rearrange("(p m) -> p m", p=128), reads=[self.B_wb[L]], writes=[wout])
            o = self.ln_setup(c, es)
            at = [c.sb(es, [128, 8, T], F32, "at") for _ in range(2)]
            ld = [c.sb(es, [128, 8, T], F32, "ld") for _ in range(2)]
            ab = [c.sb(es, [128, T], BF16, "ab") for _ in range(8)]
            h32c = [c.sb(es, [128, T], F32, "h32c") for _ in range(3)]
            pd = [c.ps(es, [128, T], F32, "pd") for _ in range(2)]
            lcol = (L * 3 + 1) * 8
            at_v = self.attT.rearrange("(kc p) t -> p kc t", p=128)

            def load(i):
                tok = slice(i * T, (i + 1) * T)
                c.dma("sync", at[i % 2][:], at_v[:, :, tok], reads=[self.B_att], writes=[at[i % 2]])
                for h in range(8):
                    c.dma("sync", ld[i % 2][:, h, :], self.lden[h:h + 1, tok].partition_broadcast(128), reads=[self.B_att], writes=[ld[i % 2]])

            def compute(i):
                tok = slice(i * T, (i + 1) * T)
                ati, ldi = at[i % 2], ld[i % 2]
                c.op("vector", lambda e: e.reciprocal(out=ldi[:], in_=ldi[:]), reads=[ldi], writes=[ldi])
                for h in range(8):
                    eng = "vector" if h % 2 == 0 else "gpsimd"
                    c.op(eng, lambda e: e.tensor_tensor(out=ab[h][:], in0=ati[:, h, :], in1=ldi[:, h, :], op=ALU.mult),
                         reads=[ati, ldi], writes=[ab[h]])
                for dc in range(8):
                    hr = h32c[dc % 3]
                    c.dma("sync", hr[:], self.outT[dc * 128:(dc + 1) * 128, tok], reads=[self.B_out], writes=[hr])
                    p_ = pd[dc % 2]
                    for kc in range(8):
                        c.op("tensor", lambda e, kc=kc: e.matmul(p_[:], lhsT=wout[:, kc * D + dc * 128: kc * D + (dc + 1) * 128], rhs=ab[kc][:],
                                                                 start=(kc == 0), stop=(kc == 7)), reads=[wout, ab[kc]], writes=[p_])
                    yd = o.y[dc]
                    c.op("vector", lambda e: e.scalar_tensor_tensor(out=yd[:], in0=hr[:], scalar=float(ALPHA), in1=p_[:],
                                                                    op0=ALU.mult, op1=ALU.add), reads=[hr, p_], writes=[yd])
                    self.ln_accum(c, o, dc)
                self.ln_finish(c, o, lcol, tok, store=True)

            for i in range(NT + 1):
                if i < NT:
                    load(i)
                if i > 0:
                    compute(i - 1)
            c.barrier()

    def build(self):
        nc = self.nc
        with contextlib.ExitStack() as es:
            c = Ctx(nc, es)
            self.cast_weights(c)
            self.rope_tables(c)
            stages = []
            for L in range(DEPTH):
                stages.append(("f0", L))
                stages.append(("mix", L))
                stages.append(("f1", L))
            first = True
            if self.start_at is not None:
                stages = stages[stages.index(self.start_at):]
                first = False
                for r0 in range(0, D, 128):
                    c.dma("sync", self.outT[r0:r0 + 128, :], self.xT[r0:r0 + 128, :], writes=[self.B_out])
                c.barrier()
            for kind, L in stages:
                if kind == "f0":
                    src, bsrc = (self.xT, Buf(self.xT, "xT")) if first else (self.outT, self.B_out)
                    self.ffn(c, L, 0, src, bsrc, ple=False)
                    first = False
                elif kind == "f1":
                    self.ffn(c, L, 1, self.outT, self.B_out, ple=True)
                elif kind == "mix" and L % 2 == 0:
                    self.even_in(c, L)
                    if self.stop_after == ("ein", L):
                        break
                    self.even_s5(c, L)
                    if self.stop_after == ("es5", L):
                        break
                    self.even_out(c, L)
                elif kind == "mix":
                    self.odd_in(c, L)
                    if self.stop_after == ("oin", L):
                        break
                    self.odd_attn(c, L, nqs=self.nqs)
                    if self.stop_after == ("oat", L):
                        break
                    self.odd_out(c, L)
                if self.stop_after == (kind, L):
                    break
            c.barrier()
        return nc


def _prep_inputs(inputs, b):
    m = {}
    m["xT"] = np.ascontiguousarray(np.asarray(inputs["x"][b], np.float32).T)
    m["pT"] = np.ascontiguousarray(np.asarray(inputs["p"][:, b], np.float32).transpose(0, 2, 1))
    m["posb"] = np.ascontiguousarray(np.broadcast_to(np.asarray(inputs["positions"][b], np.int32)[None, :], (128, S)))
    return m


def _shared_inputs(inputs):
    m = {}
    for L in range(DEPTH):
        m["wl%d" % L] = _layer_blob_layout(L, inputs).finish()
    m["sp"] = _small_params(inputs).finish()
    return m


def kernel(**inputs):
    prog = Prog()
    nc = prog.build()
    shared = _shared_inputs(inputs)
    in_maps = []
    for core in range(NCORES):
        m = dict(shared)
        m.update(_prep_inputs(inputs, core % 4))
        in_maps.append(m)
    res = run_bass_kernel_spmd(nc, in_maps, core_ids=list(range(NCORES)))
    out = np.stack([np.ascontiguousarray(res.results[b]["outT"].T) for b in range(4)], axis=0)
    return out.astype(np.float32)
```

```python
import contextlib
import math
import numpy as np
import concourse.bass as bass
import concourse.mybir as mybir
from concourse.bass_utils import run_bass_kernel_spmd

F32 = mybir.dt.float32
BF16 = mybir.dt.bfloat16
I32 = mybir.dt.int32
AF = mybir.ActivationFunctionType
ALU = mybir.AluOpType
AX = mybir.AxisListType

S = 8192
D = 1024
FF = 2816
NFC = FF // 128
T = 512
NT = S // T
DEPTH = 4
ALPHA = (2 * DEPTH) ** 0.25
EPS = 1e-5
NCORES = 8
import os
LA_IDX = int(os.environ.get('LA_IDX', '1'))
LA_ATT = int(os.environ.get('LA_ATT', '1'))


class _Eng:
    def __init__(self, name, h, sem):
        self.name, self.h, self.sem = name, h, sem
        self.count = 0
        self.seen = {}


class Buf:
    def __init__(self, t, name):
        self.t = t
        self.name = name
        self.w = None
        self.r = {}

    def __getitem__(self, k):
        return self.t[k]


class Ctx:
    def __init__(self, nc, es):
        self.nc, self.es = nc, es
        self.e = {}
        for name in ["tensor", "vector", "scalar", "gpsimd"]:
            sem = es.enter_context(nc.semaphore("s_" + name))
            self.e[name] = _Eng(name, getattr(nc, name), sem)
        self.dq = {}
        for name, n in [("sync", 16), ("gpsimd", 8)]:
            sems = [es.enter_context(nc.semaphore("d_%s%d" % (name, i))) for i in range(n)]
            self.dq[name] = dict(h=getattr(nc, name), sems=sems, cnt=[0] * n, i=0, seen={})
        self.nbuf = 0

    def sb(self, es, shape, dt, name=None):
        self.nbuf += 1
        name = (name or "b") + "_%d" % self.nbuf
        return Buf(es.enter_context(self.nc.sbuf_tensor(name, list(shape), dt)), name)

    def ps(self, es, shape, dt, name=None):
        self.nbuf += 1
        name = (name or "p") + "_%d" % self.nbuf
        return Buf(es.enter_context(self.nc.psum_tensor(name, list(shape), dt)), name)

    def _need(self, seen, h, dep, me, skip_same):
        if dep is None:
            return
        src, val = dep
        if src[0] == "E":
            if src[1] == me and skip_same:
                return
            sem = self.e[src[1]].sem
        else:
            sem = self.dq[src[1]]["sems"][src[2]]
        if seen.get(src, -1) >= val:
            return
        h.wait_ge(sem, val)
        seen[src] = val

    def op(self, eng, fn, reads=(), writes=()):
        E = self.e[eng]
        same_ok = eng == "tensor"
        for b in reads:
            self._need(E.seen, E.h, b.w, eng, same_ok)
        for b in writes:
            self._need(E.seen, E.h, b.w, eng, same_ok)
            for d in list(b.r.items()):
                self._need(E.seen, E.h, d, eng, True)
        ins = fn(E.h)
        E.count += 1
        ins.then_inc(E.sem, 1)
        tag = (("E", eng), E.count)
        for b in reads:
            b.r[tag[0]] = tag[1]
        for b in writes:
            b.w = tag
            b.r = {}
        return ins

    def dma(self, qn, out, in_, reads=(), writes=(), **kw):
        q = self.dq[qn]
        h = q["h"]
        seen = self.e[qn].seen if qn in self.e else q["seen"]
        me = qn if qn in self.e else None
        for b in reads:
            self._need(seen, h, b.w, me, False)
        for b in writes:
            self._need(seen, h, b.w, me, False)
            for d in list(b.r.items()):
                self._need(seen, h, d, me, False)
        i = q["i"]
        q["i"] = (i + 1) % len(q["sems"])
        src = ("D", qn, i)
        if q["cnt"][i] > 0 and seen.get(src, -1) < q["cnt"][i]:
            h.wait_ge(q["sems"][i], q["cnt"][i])
            seen[src] = q["cnt"][i]
        ins = h.dma_start(out=out, in_=in_, **kw)
        q["cnt"][i] += 16
        ins.then_inc(q["sems"][i], 16)
        tag = (src, q["cnt"][i])
        for b in reads:
            b.r[tag[0]] = tag[1]
        for b in writes:
            b.w = tag
            b.r = {}
        return ins

    def barrier(self):
        deps = [(("E", n), E.count) for n, E in self.e.items() if E.count > 0]
        for qn, q in self.dq.items():
            for i, v in enumerate(q["cnt"]):
                if v > 0:
                    deps.append((("D", qn, i), v))
        for n, E in self.e.items():
            for d in deps:
                self._need(E.seen, E.h, d, n, True)
        q = self.dq["sync"]
        for d in deps:
            self._need(q["seen"], q["h"], d, None, False)


def _kmaj(w):
    K, N = w.shape
    return np.ascontiguousarray(w.reshape(K // 128, 128, N).transpose(1, 0, 2))


def _swap_index():
    idx = []
    swA = [d + 16 if d < 16 else (d - 16 if d < 32 else d) for d in range(128)]
    swI = [d + 8 if d < 8 else (d - 8 if d < 16 else d) for d in range(64)]
    for h in range(8):
        idx += [h * 128 + v for v in swA]
    for g in range(2):
        idx += [1024 + g * 128 + v for v in swA]
    for h in range(8):
        idx += [1536 + h * 64 + v for v in swI]
    idx += [2048 + v for v in swI] + [2048 + v for v in swI]
    return np.array(idx, np.int64)


_SW_IDX = _swap_index()
BIG = 3.0e38


class Blob:
    def __init__(self):
        self.parts, self.off, self.n = [], {}, 0

    def add(self, name, arr):
        a = np.ascontiguousarray(arr, dtype=np.float32).reshape(-1)
        self.off[name] = (self.n, tuple(arr.shape))
        self.parts.append(a)
        self.n += a.size
        pad = (-self.n) % 128
        if pad:
            self.parts.append(np.zeros(pad, np.float32))
            self.n += pad

    def finish(self, row=2048 * 128):
        pad = (-self.n) % row
        if pad:
            self.parts.append(np.zeros(pad, np.float32))
            self.n += pad
        return np.concatenate(self.parts).reshape(-1, 2048)


def _layer_blob_layout(L, inputs=None):
    b = Blob()

    def get(name, idx, shape):
        if inputs is None:
            return np.zeros(shape, np.float32)
        a = inputs[name]
        for i in idx:
            a = a[i]
        return np.asarray(a, np.float32)

    for s in range(2):
        w1 = get("ffn_w1", (L, s), (D, FF))
        w3 = get("ffn_w3", (L, s), (D, FF))
        w2 = get("ffn_w2", (L, s), (FF, D))
        a1 = w1.reshape(8, 128, NFC, 128).transpose(2, 1, 0, 3)
        a3 = w3.reshape(8, 128, NFC, 128).transpose(2, 1, 0, 3)
        b.add("w13_%d" % s, np.stack([a1, a3], axis=2))
        b.add("w2_%d" % s, _kmaj(w2))
    b.add("wg", _kmaj(get("ple_w_gate", (L,), (D, D))))
    b.add("wp", _kmaj(get("ple_w_proj", (L,), (256, D))))
    j = L // 2
    if L % 2 == 0:
        b.add("win", _kmaj(get("ab_w_in", (j,), (D, 2048))))
        b.add("wout", _kmaj(get("ab_w_out", (j,), (D, D))))
        b.add("wglu", _kmaj(get("s5_w_glu", (j,), (512, 512))))
    else:
        wc = get("c_w_in", (j,), (D, 2120))
        wpad = np.zeros((D, 2176), np.float32)
        wpad[:, :2120] = wc
        b.add("cin", _kmaj(wpad))
        b.add("cinsw", _kmaj(np.ascontiguousarray(wc[:, _SW_IDX])))
        b.add("cout", _kmaj(get("c_w_out", (j,), (D, D))))
    return b


def _small_params(inputs=None):
    b = Blob()

    def get(name, shape):
        if inputs is None:
            return np.zeros(shape, np.float32)
        return np.asarray(inputs[name], np.float32)

    g = get("ln_g", (DEPTH, 3, D))
    bb = get("ln_b", (DEPTH, 3, D))
    b.add("ln_g", g.reshape(DEPTH, 3, 8, 128).transpose(3, 0, 1, 2))
    b.add("ln_b", bb.reshape(DEPTH, 3, 8, 128).transpose(3, 0, 1, 2))
    sgn = np.ones((128, 1), np.float32)
    sgn[64:] = -1.0
    b.add("sgn", sgn)
    b.add("ident", np.eye(128, dtype=np.float32))
    sw = np.zeros((128, 128), np.float32)
    for k in range(64):
        sw[k, k + 64] = 1.0
        sw[k + 64, k] = 1.0
    b.add("swapP", sw)
    inv_a = (500000.0 ** (-np.arange(0, 32, 2, dtype=np.float32) / np.float32(32))).astype(np.float32)
    inv_i = (500000.0 ** (-np.arange(0, 16, 2, dtype=np.float32) / np.float32(16))).astype(np.float32)
    rc = np.zeros((128, 4), np.float32)
    for r in range(128):
        if r < 32:
            rc[r, 0] = inv_a[r % 16]
            rc[r, 1] = -1.0 if r < 16 else 1.0
        dd = r % 64
        if dd < 16:
            rc[r, 2] = inv_i[dd % 8]
            rc[r, 3] = -1.0 if dd < 8 else 1.0
    b.add("ropec", rc)
    xx = np.arange(896)[None, :]
    qq = np.arange(128)[:, None]
    b.add("Wneg", np.where(xx <= qq + 384, 0.0, -BIG).astype(np.float32))
    b.add("Wpos", np.where(xx <= qq + 384, 0.0, BIG).astype(np.float32))
    for j in range(2):
        lre = get("s5_lam_re", (2, 32, 64))[j]
        lim = get("s5_lam_im", (2, 32, 64))[j]
        ldt = get("s5_log_dt", (2, 32))[j]
        b.add("lamre_A%d" % j, np.concatenate([lre.T, lre.T], 0))
        b.add("lamim_A%d" % j, np.concatenate([lim.T, lim.T], 0))
        b.add("logdt_A%d" % j, np.broadcast_to(ldt[None, :], (128, 32)))
        b.add("lamre_C%d" % j, np.broadcast_to(lre.reshape(1, 2048), (16, 2048)))
        b.add("lamim_C%d" % j, np.broadcast_to(lim.reshape(1, 2048), (16, 2048)))
        b.add("logdt_C%d" % j, np.broadcast_to(np.repeat(ldt, 64)[None, :], (16, 2048)))
        b.add("bre_C%d" % j, get("s5_b_re", (2, 32, 64, 16))[j].transpose(2, 0, 1))
        b.add("bim_C%d" % j, get("s5_b_im", (2, 32, 64, 16))[j].transpose(2, 0, 1))
        cre = get("s5_c_re", (2, 32, 16, 64))[j].transpose(2, 0, 1)
        cim = get("s5_c_im", (2, 32, 16, 64))[j].transpose(2, 0, 1)
        b.add("CA%d" % j, np.concatenate([cre, cim], 0))
        b.add("CB%d" % j, np.concatenate([cim, cre], 0))
        b.add("d_C%d" % j, get("s5_d", (2, 512))[j].reshape(32, 16).T)
        b.add("convw%d" % j, get("conv_w", (2, 3, 512))[j].reshape(3, 4, 128).transpose(2, 1, 0))
        b.add("bglu%d" % j, get("s5_b_glu", (2, 512))[j].reshape(4, 128).T)
    return b


class Prog:
    def __init__(self, stop_after=None, debug=False):
        self.stop_after = stop_after
        self.nqs = None
        self.start_at = None
        dk = "ExternalOutput" if debug else "Internal"
        nc = self.nc = bass.Bass("TRN2", target_bir_lowering=False)
        self.xT = nc.dram_tensor("xT", [D, S], F32, kind="ExternalInput").ap()
        self.pT = nc.dram_tensor("pT", [DEPTH, 256, S], F32, kind="ExternalInput").ap()
        self.lay = [_layer_blob_layout(L) for L in range(DEPTH)]
        self.wl, self.wb = [], []
        for L in range(DEPTH):
            rows = self.lay[L].finish().shape[0]
            self.wl.append(nc.dram_tensor("wl%d" % L, [rows, 2048], F32, kind="ExternalInput").ap())
            self.wb.append(nc.dram_tensor("wb%d" % L, [rows, 2048], BF16, kind="Internal").ap())
        self.splay = _small_params()
        sp_rows = self.splay.finish().shape[0]
        self.sp = nc.dram_tensor("sp", [sp_rows, 2048], F32, kind="ExternalInput").ap()
        self.outT = nc.dram_tensor("outT", [D, S], F32, kind="ExternalOutput").ap()
        self.hTb = nc.dram_tensor("hTb", [D, S], BF16, kind="Internal").ap()
        self.pTb = nc.dram_tensor("pTb", [DEPTH, 256, S], BF16, kind="Internal").ap()
        self.yaTb = nc.dram_tensor("yaTb", [512, S], BF16, kind=dk).ap()
        self.uT32 = nc.dram_tensor("uT32", [512, S], F32, kind=dk).ap()
        self.uTb = nc.dram_tensor("uTb", [512, S], BF16, kind="Internal").ap()
        self.zT32 = nc.dram_tensor("zT32", [512, S], F32, kind=dk).ap()
        self.zTb = nc.dram_tensor("zTb", [512, S], BF16, kind="Internal").ap()
        self.posb = nc.dram_tensor("posb", [128, S], I32, kind="ExternalInput").ap()
        self.tabs = nc.dram_tensor("tabs", [4, 128, S], F32, kind="Internal").ap()
        self.qT = nc.dram_tensor("qT", [1024, S], BF16, kind="Internal").ap()
        self.kT = nc.dram_tensor("kT", [256, S], BF16, kind="Internal").ap()
        self.vtok = nc.dram_tensor("vtok", [S, 256], BF16, kind="Internal").ap()
        self.qiT = nc.dram_tensor("qiT", [512, S], BF16, kind="Internal").ap()
        self.kiT = nc.dram_tensor("kiT", [128, S], BF16, kind="Internal").ap()
        self.witok = nc.dram_tensor("witok", [S, 8], F32, kind="Internal").ap()
        self.attT = nc.dram_tensor("attT", [1024, S], F32, kind=dk).ap()
        self.lden = nc.dram_tensor("lden", [8, S], F32, kind=dk).ap()
        self.B_tabs = Buf(self.tabs, "tabs")
        self.B_qkv = Buf(self.qT, "qkv")
        self.B_att = Buf(self.attT, "att")
        self.B_ya = Buf(self.yaTb, "yaTb")
        self.B_u = Buf(self.uT32, "u")
        self.B_z = Buf(self.zT32, "z")
        self.B_out = Buf(self.outT, "outT")
        self.B_hTb = Buf(self.hTb, "hTb")
        self.B_wb = [Buf(self.wb[L], "wb%d" % L) for L in range(DEPTH)]
        self.B_pTb = Buf(self.pTb, "pTb")

    def wv(self, L, name):
        off, shape = self.lay[L].off[name]
        n = int(np.prod(shape))
        flat = self.wb[L].rearrange("r c -> (r c)")[off:off + n]
        return flat, shape

    def spv(self, name):
        off, shape = self.splay.off[name]
        n = int(np.prod(shape))
        return self.sp.rearrange("r c -> (r c)")[off:off + n], shape

    def cast_weights(self, c):
        for L in range(DEPTH):
            rows = self.wl[L].shape[0]
            for r0 in range(0, rows, 128):
                c.dma("gpsimd", self.wb[L][r0:r0 + 128, :], self.wl[L][r0:r0 + 128, :],
                      writes=[self.B_wb[L]])
        for r0 in range(0, D, 128):
            for c0 in range(0, S, 2048):
                c.dma("gpsimd", self.hTb[r0:r0 + 128, c0:c0 + 2048], self.xT[r0:r0 + 128, c0:c0 + 2048],
                      writes=[self.B_hTb])
        for L in range(DEPTH):
            for r0 in range(0, 256, 128):
                for c0 in range(0, S, 2048):
                    c.dma("gpsimd", self.pTb[L, r0:r0 + 128, c0:c0 + 2048],
                          self.pT[L, r0:r0 + 128, c0:c0 + 2048], writes=[self.B_pTb])
        c.barrier()

    def ln_setup(self, c, es):
        o = type("LN", (), {})()
        o.y = [c.sb(es, [128, T], F32, "y") for _ in range(8)]
        o.ybf = [c.sb(es, [128, T], BF16, "ybf") for _ in range(2)]
        o.ysq = [c.sb(es, [128, T], BF16, "ysq") for _ in range(2)]
        o.ob = [c.sb(es, [128, T], BF16, "ob") for _ in range(8)]
        o.t1 = [c.sb(es, [128, T], F32, "t1") for _ in range(2)]
        o.t2 = [c.sb(es, [128, T], F32, "t2") for _ in range(2)]
        o.mean = c.sb(es, [128, T], F32, "mean")
        o.msq = c.sb(es, [128, T], F32, "msq")
        o.var = c.sb(es, [128, T], F32, "var")
        o.rstd = c.sb(es, [128, T], F32, "rstd")
        o.ones = c.sb(es, [128, 128], BF16, "ones")
        o.lng = c.sb(es, [128, DEPTH * 3 * 8], F32, "lng")
        o.lnb = c.sb(es, [128, DEPTH * 3 * 8], F32, "lnb")
        gf, _ = self.spv("ln_g")
        bf_, _ = self.spv("ln_b")
        c.dma("sync", o.lng[:], gf.rearrange("(p m) -> p m", p=128), writes=[o.lng])
        c.dma("sync", o.lnb[:], bf_.rearrange("(p m) -> p m", p=128), writes=[o.lnb])
        c.op("vector", lambda e: e.memset(o.ones[:], 1.0), writes=[o.ones])
        o.pss = c.ps(es, [128, T], F32, "pss")
        o.psq = c.ps(es, [128, T], F32, "psq")
        return o

    def ln_accum(self, c, o, dc):
        yd = o.y[dc]
        yb_, yq_ = o.ybf[dc % 2], o.ysq[dc % 2]
        c.op("scalar", lambda e: e.activation(out=yb_[:], in_=yd[:], func=AF.Copy), reads=[yd], writes=[yb_])
        c.op("scalar", lambda e: e.activation(out=yq_[:], in_=yd[:], func=AF.Square), reads=[yd], writes=[yq_])
        c.op("tensor", lambda e: e.matmul(o.pss[:], lhsT=o.ones[:], rhs=yb_[:], start=(dc == 0), stop=(dc == 7)),
             reads=[o.ones, yb_], writes=[o.pss])
        c.op("tensor", lambda e: e.matmul(o.psq[:], lhsT=o.ones[:], rhs=yq_[:], start=(dc == 0), stop=(dc == 7)),
             reads=[o.ones, yq_], writes=[o.psq])

    def ln_finish(self, c, o, lcol, tok, store):
        mean_sb, msq, var, rstd = o.mean, o.msq, o.var, o.rstd
        c.op("scalar", lambda e: e.activation(out=mean_sb[:], in_=o.pss[:], func=AF.Copy, scale=1.0 / D),
             reads=[o.pss], writes=[mean_sb])
        c.op("scalar", lambda e: e.activation(out=msq[:], in_=mean_sb[:], func=AF.Square), reads=[mean_sb], writes=[msq])
        c.op("vector", lambda e: e.scalar_tensor_tensor(out=var[:], in0=o.psq[:], scalar=1.0 / D, in1=msq[:],
                                                        op0=ALU.mult, op1=ALU.subtract),
             reads=[o.psq, msq], writes=[var])
        c.op("vector", lambda e: e.tensor_scalar(out=var[:], in0=var[:], scalar1=float(EPS), scalar2=None, op0=ALU.add),
             reads=[var], writes=[var])
        c.op("scalar", lambda e: e.activation(out=msq[:], in_=var[:], func=AF.Sqrt), reads=[var], writes=[msq])
        c.op("vector", lambda e: e.reciprocal(out=rstd[:], in_=msq[:]), reads=[msq], writes=[rstd])
        for dc in range(8):
            yd = o.y[dc]
            a1, a2 = o.t1[dc % 2], o.t2[dc % 2]
            c.op("vector", lambda e: e.tensor_tensor(out=a1[:], in0=yd[:], in1=mean_sb[:], op=ALU.subtract),
                 reads=[yd, mean_sb], writes=[a1])
            c.op("gpsimd", lambda e: e.tensor_tensor(out=a2[:], in0=a1[:], in1=rstd[:], op=ALU.mult),
                 reads=[a1, rstd], writes=[a2])
            c.op("vector", lambda e: e.tensor_scalar(out=yd[:], in0=a2[:], scalar1=o.lng[:, lcol + dc:lcol + dc + 1],
                                                     scalar2=o.lnb[:, lcol + dc:lcol + dc + 1], op0=ALU.mult, op1=ALU.add),
                 reads=[a2, o.lng, o.lnb], writes=[yd])
            o_ = o.ob[dc]
            c.op("scalar", lambda e: e.activation(out=o_[:], in_=yd[:], func=AF.Copy), reads=[yd], writes=[o_])
            if store:
                c.dma("gpsimd", self.outT[dc * 128:(dc + 1) * 128, tok], yd[:], reads=[yd], writes=[self.B_out])
                c.dma("gpsimd", self.hTb[dc * 128:(dc + 1) * 128, tok], o_[:], reads=[o_], writes=[self.B_hTb])

    def ffn(self, c, L, s, src32, B_src32, ple):
        j_ln = 0 if s == 0 else 2
        with contextlib.ExitStack() as es:
            w13f, _ = self.wv(L, "w13_%d" % s)
            w13v = w13f.rearrange("(c p m) -> c p m", c=NFC, p=128)
            w2f, _ = self.wv(L, "w2_%d" % s)
            w2v = w2f.rearrange("(p m) -> p m", p=128)
            w2sb = c.sb(es, [128, NFC * D], BF16, "w2sb")
            for q in range(4):
                m0, m1 = q * (NFC * D // 4), (q + 1) * (NFC * D // 4)
                c.dma("sync", w2sb[:, m0:m1], w2v[:, m0:m1], reads=[self.B_wb[L]], writes=[w2sb])
            w13sb = [c.sb(es, [128, 2 * 8 * 128], BF16, "w13sb") for _ in range(3)]
            hb = [c.sb(es, [128, 8, T], BF16, "hb") for _ in range(2)]
            act = [c.sb(es, [128, T], BF16, "act") for _ in range(NFC)]
            sgt = [c.sb(es, [128, T], F32, "sgt") for _ in range(2)]
            h32c = [c.sb(es, [128, T], F32, "h32c") for _ in range(3)]
            o = self.ln_setup(c, es)
            y, ob, t1, ybf = o.y, o.ob, o.t1, o.ybf
            pg = [c.ps(es, [128, T], F32, "pg") for _ in range(2)]
            pu = [c.ps(es, [128, T], F32, "pu") for _ in range(2)]
            pd = [c.ps(es, [128, T], F32, "pd") for _ in range(2)]
            if ple:
                wgf, _ = self.wv(L, "wg")
                wpf, _ = self.wv(L, "wp")
                wgsb = c.sb(es, [128, 8 * D], BF16, "wgsb")
                wpsb = c.sb(es, [128, 2 * D], BF16, "wpsb")
                c.dma("sync", wgsb[:], wgf.rearrange("(p m) -> p m", p=128), reads=[self.B_wb[L]], writes=[wgsb])
                c.dma("sync", wpsb[:], wpf.rearrange("(p m) -> p m", p=128), reads=[self.B_wb[L]], writes=[wpsb])
                ptb = [c.sb(es, [128, 2, T], BF16, "ptb") for _ in range(2)]
                sig = [c.sb(es, [128, T], F32, "sig") for _ in range(2)]
                et = [c.sb(es, [128, T], F32, "et") for _ in range(2)]
            lcol = (L * 3 + j_ln) * 8
            hTb_v = self.hTb.rearrange("(kc p) t -> p kc t", p=128)
            pTb_v = self.pTb.rearrange("l (kc p) t -> l p kc t", p=128)

            def load(i):
                c.dma("sync", hb[i % 2][:], hTb_v[:, :, i * T:(i + 1) * T], reads=[self.B_hTb], writes=[hb[i % 2]])
                if ple:
                    c.dma("sync", ptb[i % 2][:], pTb_v[L, :, :, i * T:(i + 1) * T], reads=[self.B_pTb],
                          writes=[ptb[i % 2]])

            def compute(i):
                hbi = hb[i % 2]
                tok = slice(i * T, (i + 1) * T)
                for cc in range(NFC):
                    wt = w13sb[cc % 3]
                    c.dma("sync", wt[:], w13v[cc], reads=[self.B_wb[L]], writes=[wt])
                    g_, u_ = pg[cc % 2], pu[cc % 2]
                    for kc in range(8):
                        c.op("tensor", lambda e, kc=kc: e.matmul(g_[:], lhsT=wt[:, kc * 128:(kc + 1) * 128],
                                                                 rhs=hbi[:, kc, :], start=(kc == 0), stop=(kc == 7)),
                             reads=[wt, hbi], writes=[g_])
                    for kc in range(8):
                        c.op("tensor", lambda e, kc=kc: e.matmul(u_[:], lhsT=wt[:, (8 + kc) * 128:(9 + kc) * 128],
                                                                 rhs=hbi[:, kc, :], start=(kc == 0), stop=(kc == 7)),
                             reads=[wt, hbi], writes=[u_])
                    sg = sgt[cc % 2]
                    c.op("scalar", lambda e: e.activation(out=sg[:], in_=g_[:], func=AF.Silu), reads=[g_], writes=[sg])
                    a_ = act[cc]
                    c.op("vector", lambda e: e.scalar_tensor_tensor(out=a_[:], in0=u_[:], scalar=0.5, in1=sg[:],
                                                                    op0=ALU.mult, op1=ALU.mult),
                         reads=[u_, sg], writes=[a_])
                for dc in range(8):
                    hr = h32c[dc % 3]
                    c.dma("sync", hr[:], src32[dc * 128:(dc + 1) * 128, tok], reads=[B_src32], writes=[hr])
                    p_ = pd[dc % 2]
                    for cc in range(NFC):
                        c.op("tensor", lambda e, cc=cc: e.matmul(
                            p_[:], lhsT=w2sb[:, cc * D + dc * 128: cc * D + (dc + 1) * 128], rhs=act[cc][:],
                            start=(cc == 0), stop=(cc == NFC - 1)), reads=[w2sb, act[cc]], writes=[p_])
                    yd = y[dc]
                    c.op("vector", lambda e: e.scalar_tensor_tensor(out=yd[:], in0=hr[:], scalar=float(ALPHA), in1=p_[:],
                                                                    op0=ALU.mult, op1=ALU.add),
                         reads=[hr, p_], writes=[yd])
                    self.ln_accum(c, o, dc)
                self.ln_finish(c, o, lcol, tok, store=not ple)
                if ple:
                    pti = ptb[i % 2]
                    for dc in range(8):
                        g_, u_ = pg[dc % 2], pu[dc % 2]
                        for kc in range(8):
                            c.op("tensor", lambda e, kc=kc: e.matmul(
                                g_[:], lhsT=wgsb[:, kc * D + dc * 128: kc * D + (dc + 1) * 128], rhs=ob[kc][:],
                                start=(kc == 0), stop=(kc == 7)), reads=[wgsb, ob[kc]], writes=[g_])
                        for kc in range(2):
                            c.op("tensor", lambda e, kc=kc: e.matmul(
                                u_[:], lhsT=wpsb[:, kc * D + dc * 128: kc * D + (dc + 1) * 128], rhs=pti[:, kc, :],
                                start=(kc == 0), stop=(kc == 1)), reads=[wpsb, pti], writes=[u_])
                        sg, e_ = sig[dc % 2], et[dc % 2]
                        c.op("scalar", lambda e: e.activation(out=sg[:], in_=g_[:], func=AF.Sigmoid), reads=[g_], writes=[sg])
                        c.op("vector", lambda e: e.tensor_tensor(out=e_[:], in0=u_[:], in1=sg[:], op=ALU.mult),
                             reads=[u_, sg], writes=[e_])
                        yd = y[dc]
                        a1 = t1[dc % 2]
                        c.op("gpsimd", lambda e: e.tensor_tensor(out=a1[:], in0=yd[:], in1=e_[:], op=ALU.add),
                             reads=[yd, e_], writes=[a1])
                        a2 = ybf[dc % 2]
                        c.op("scalar", lambda e: e.activation(out=a2[:], in_=a1[:], func=AF.Copy), reads=[a1], writes=[a2])
                        c.dma("gpsimd", self.outT[dc * 128:(dc + 1) * 128, tok], a1[:], reads=[a1], writes=[self.B_out])
                        c.dma("gpsimd", self.hTb[dc * 128:(dc + 1) * 128, tok], a2[:], reads=[a2], writes=[self.B_hTb])

            for i in range(NT + 1):
                if i < NT:
                    load(i)
                if i > 0:
                    compute(i - 1)
            c.barrier()

    def spload(self, c, es, name, eng="sync"):
        f, shape = self.spv(name)
        P = shape[0]
        m = int(np.prod(shape)) // P
        t = c.sb(es, [P, m], F32, name)
        c.dma(eng, t[:], f.rearrange("(p m) -> p m", p=P), writes=[t])
        return t

    def sin_tile(self, c, out, x, P, N, shift, tmp):
        a, xs, kf, m, ki = tmp["a"], tmp["xs"], tmp["kf"], tmp["m"], tmp["ki"]
        TWO_PI = 2.0 * math.pi
        C1 = 6.28125
        C2 = TWO_PI - C1
        V = lambda t: t[0:P, 0:N]
        c.op("vector", lambda e: e.tensor_scalar(out=V(a), in0=x, scalar1=float(shift), scalar2=None, op0=ALU.add),
             reads=[tmp["xb"]], writes=[a])
        c.op("vector", lambda e: e.tensor_scalar(out=V(xs), in0=V(a), scalar1=float(1.0 / TWO_PI), scalar2=None, op0=ALU.mult),
             reads=[a], writes=[xs])
        c.op("vector", lambda e: e.tensor_copy(out=V(ki), in_=V(xs)), reads=[xs], writes=[ki])
        c.op("vector", lambda e: e.tensor_copy(out=V(kf), in_=V(ki)), reads=[ki], writes=[kf])
        c.op("vector", lambda e: e.scalar_tensor_tensor(out=V(a), in0=V(kf), scalar=float(-C1), in1=V(a), op0=ALU.mult, op1=ALU.add),
             reads=[kf, a], writes=[a])
        c.op("vector", lambda e: e.scalar_tensor_tensor(out=V(a), in0=V(kf), scalar=float(-C2), in1=V(a), op0=ALU.mult, op1=ALU.add),
             reads=[kf, a], writes=[a])
        c.op("vector", lambda e: e.tensor_scalar(out=V(m), in0=V(a), scalar1=float(math.pi), scalar2=float(-TWO_PI),
                                                 op0=ALU.is_gt, op1=ALU.mult), reads=[a], writes=[m])
        c.op("vector", lambda e: e.tensor_tensor(out=V(a), in0=V(a), in1=V(m), op=ALU.add), reads=[a, m], writes=[a])
        c.op("vector", lambda e: e.tensor_scalar(out=V(m), in0=V(a), scalar1=float(-math.pi), scalar2=float(TWO_PI),
                                                 op0=ALU.is_lt, op1=ALU.mult), reads=[a], writes=[m])
        c.op("vector", lambda e: e.tensor_tensor(out=V(a), in0=V(a), in1=V(m), op=ALU.add), reads=[a, m], writes=[a])
        c.op("vector", lambda e: e.tensor_scalar(out=V(a), in0=V(a), scalar1=-3.1415925, scalar2=3.1415925,
                                                 op0=ALU.max, op1=ALU.min), reads=[a], writes=[a])
        c.op("scalar", lambda e: e.activation(out=out, in_=V(a), func=AF.Sin), reads=[a], writes=[tmp["ob"]])

    def even_in(self, c, L):
        j = L // 2
        with contextlib.ExitStack() as es:
            winf, _ = self.wv(L, "win")
            wsb = c.sb(es, [128, 8 * 2048], BF16, "winsb")
            wv_ = winf.rearrange("(p m) -> p m", p=128)
            for q in range(4):
                c.dma("sync", wsb[:, q * 4096:(q + 1) * 4096], wv_[:, q * 4096:(q + 1) * 4096], reads=[self.B_wb[L]], writes=[wsb])
            cw = self.spload(c, es, "convw%d" % j)
            hb = [c.sb(es, [128, 8, T], BF16, "hb") for _ in range(2)]
            pp = [c.ps(es, [128, T], F32, "pp") for _ in range(6)]
            hcs = [c.sb(es, [128, T], F32, "hcs") for _ in range(2)]
            ucx = [c.sb(es, [128, T + 2], F32, "ucx") for _ in range(4)]
            vt = [c.sb(es, [128, T], F32, "vt") for _ in range(2)]
            yab = [c.sb(es, [128, T], BF16, "yab") for _ in range(2)]
            u32 = [c.sb(es, [128, T], F32, "u32") for _ in range(2)]
            ubf = [c.sb(es, [128, T], BF16, "ubf") for _ in range(2)]
            for t_ in ucx:
                c.op("vector", lambda e: e.memset(t_[:], 0.0), writes=[t_])
            hTb_v = self.hTb.rearrange("(kc p) t -> p kc t", p=128)
            npp = [0]

            def proj(hbi, oc):
                p_ = pp[npp[0] % 6]
                npp[0] += 1
                for kc in range(8):
                    c.op("tensor", lambda e, kc=kc: e.matmul(p_[:], lhsT=wsb[:, kc * 2048 + oc * 128: kc * 2048 + (oc + 1) * 128],
                                                             rhs=hbi[:, kc, :], start=(kc == 0), stop=(kc == 7)),
                         reads=[wsb, hbi], writes=[p_])
                return p_

            def load(i):
                c.dma("sync", hb[i % 2][:], hTb_v[:, :, i * T:(i + 1) * T], reads=[self.B_hTb], writes=[hb[i % 2]])

            def compute(i):
                hbi = hb[i % 2]
                tok = slice(i * T, (i + 1) * T)
                for ch in range(4):
                    p_h = proj(hbi, ch)
                    p_c = proj(hbi, 8 + ch)
                    p_b = proj(hbi, 4 + ch)
                    hs = hcs[ch % 2]
                    c.op("scalar", lambda e: e.activation(out=hs[:], in_=p_h[:], func=AF.Copy), reads=[p_h], writes=[hs])
                    ux = ucx[ch]
                    c.op("vector", lambda e: e.tensor_tensor(out=ux[:, 2:T + 2], in0=hs[:], in1=p_c[:], op=ALU.mult),
                         reads=[hs, p_c], writes=[ux])
                    v_ = vt[ch % 2]
                    c.op("vector", lambda e: e.tensor_scalar(out=v_[:], in0=ux[:, 2:T + 2], scalar1=cw[:, ch * 3 + 2: ch * 3 + 3],
                                                             scalar2=None, op0=ALU.mult), reads=[ux, cw], writes=[v_])
                    c.op("vector", lambda e: e.scalar_tensor_tensor(out=v_[:], in0=ux[:, 1:T + 1], scalar=cw[:, ch * 3 + 1: ch * 3 + 2],
                                                                    in1=v_[:], op0=ALU.mult, op1=ALU.add), reads=[ux, cw, v_], writes=[v_])
                    c.op("vector", lambda e: e.scalar_tensor_tensor(out=v_[:], in0=ux[:, 0:T], scalar=cw[:, ch * 3: ch * 3 + 1],
                                                                    in1=v_[:], op0=ALU.mult, op1=ALU.add), reads=[ux, cw, v_], writes=[v_])
                    ya_ = yab[ch % 2]
                    c.op("vector", lambda e: e.tensor_tensor(out=ya_[:], in0=v_[:], in1=p_b[:], op=ALU.mult),
                         reads=[v_, p_b], writes=[ya_])
                    c.op("vector", lambda e: e.tensor_copy(out=ux[:, 0:2], in_=ux[:, T:T + 2]), reads=[ux], writes=[ux])
                    c.dma("gpsimd", self.yaTb[ch * 128:(ch + 1) * 128, tok], ya_[:], reads=[ya_], writes=[self.B_ya])
                for ch in range(4):
                    p_u = proj(hbi, 12 + ch)
                    a_, b_ = u32[ch % 2], ubf[ch % 2]
                    c.op("scalar", lambda e: e.activation(out=a_[:], in_=p_u[:], func=AF.Copy), reads=[p_u], writes=[a_])
                    c.op("scalar", lambda e: e.activation(out=b_[:], in_=p_u[:], func=AF.Copy), reads=[p_u], writes=[b_])
                    c.dma("gpsimd", self.uT32[ch * 128:(ch + 1) * 128, tok], a_[:], reads=[a_], writes=[self.B_u])
                    c.dma("gpsimd", self.uTb[ch * 128:(ch + 1) * 128, tok], b_[:], reads=[b_], writes=[self.B_u])

            for i in range(NT + 1):
                if i < NT:
                    load(i)
                if i > 0:
                    compute(i - 1)
            c.barrier()

    def even_s5(self, c, L):
        j = L // 2
        NJ = T + 1
        GK = math.sqrt(2.0 / math.pi)
        with contextlib.ExitStack() as es:
            sgn = self.spload(c, es, "sgn")
            ident = self.spload(c, es, "ident")
            swapP = self.spload(c, es, "swapP")
            dC = self.spload(c, es, "d_C%d" % j)
            magA = c.sb(es, [128, 32], F32, "magA")
            thA = c.sb(es, [128, 32], F32, "thA")
            LB1 = c.sb(es, [16, 32 * 128], BF16, "LB1")
            LB2 = c.sb(es, [16, 32 * 128], BF16, "LB2")
            LC1 = c.sb(es, [128, 512], BF16, "LC1")
            LC2 = c.sb(es, [128, 512], BF16, "LC2")
            Jf = c.sb(es, [128, NJ], F32, "Jf")
            with contextlib.ExitStack() as es2:
                lreA = self.spload(c, es2, "lamre_A%d" % j)
                limA = self.spload(c, es2, "lamim_A%d" % j)
                ldtA = self.spload(c, es2, "logdt_A%d" % j)
                dtA = c.sb(es2, [128, 32], F32, "dtA")
                c.op("scalar", lambda e: e.activation(out=dtA[:], in_=ldtA[:], func=AF.Exp), reads=[ldtA], writes=[dtA])
                c.op("vector", lambda e: e.tensor_scalar(out=lreA[:], in0=lreA[:], scalar1=-1e-4, scalar2=None, op0=ALU.min),
                     reads=[lreA], writes=[lreA])
                c.op("vector", lambda e: e.tensor_tensor(out=lreA[:], in0=lreA[:], in1=dtA[:], op=ALU.mult), reads=[lreA, dtA], writes=[lreA])
                c.op("scalar", lambda e: e.activation(out=magA[:], in_=lreA[:], func=AF.Exp), reads=[lreA], writes=[magA])
                c.op("vector", lambda e: e.tensor_tensor(out=thA[:], in0=limA[:], in1=dtA[:], op=ALU.mult), reads=[limA, dtA], writes=[thA])
                Ji = c.sb(es2, [128, NJ], I32, "Ji")
                c.op("gpsimd", lambda e: e.iota(Ji[:], pattern=[[1, NJ]], base=0, channel_multiplier=0), writes=[Ji])
                c.op("vector", lambda e: e.tensor_copy(out=Jf[:], in_=Ji[:]), reads=[Ji], writes=[Jf])
                lre = self.spload(c, es2, "lamre_C%d" % j)
                lim = self.spload(c, es2, "lamim_C%d" % j)
                ldt = self.spload(c, es2, "logdt_C%d" % j)
                br = self.spload(c, es2, "bre_C%d" % j)
                bi = self.spload(c, es2, "bim_C%d" % j)
                N = 512
                mk = lambda nm, dt=F32: c.sb(es2, [16, N], dt, nm)
                dt_, mag, ang, cs, sn = mk("dt"), mk("mag"), mk("ang"), mk("cs"), mk("sn")
                tmp = dict(a=mk("ta"), xs=mk("txs"), kf=mk("tkf"), m=mk("tm"), ki=mk("tki", I32))
                nr, ni, den, w1_, w2_, fre, fim = mk("nr"), mk("ni"), mk("den"), mk("w1"), mk("w2"), mk("fre"), mk("fim")
                bbre, bbim, lrq = mk("bbre"), mk("bbim"), mk("lrq")
                tt = lambda o_, a_, b_, op: c.op("vector", lambda e: e.tensor_tensor(out=o_[:], in0=a_[:], in1=b_[:], op=op),
                                                 reads=[a_, b_], writes=[o_])
                for gq in range(4):
                    sl = slice(gq * N, (gq + 1) * N)
                    c.op("scalar", lambda e: e.activation(out=dt_[:], in_=ldt[:, sl], func=AF.Exp), reads=[ldt], writes=[dt_])
                    c.op("vector", lambda e: e.tensor_scalar(out=lrq[:], in0=lre[:, sl], scalar1=-1e-4, scalar2=None, op0=ALU.min),
                         reads=[lre], writes=[lrq])
                    tt(mag, lrq, dt_, ALU.mult)
                    c.op("scalar", lambda e: e.activation(out=mag[:], in_=mag[:], func=AF.Exp), reads=[mag], writes=[mag])
                    c.op("vector", lambda e: e.tensor_tensor(out=ang[:], in0=lim[:, sl], in1=dt_[:], op=ALU.mult), reads=[lim, dt_], writes=[ang])
                    tmp["xb"] = ang
                    tmp["ob"] = cs
                    self.sin_tile(c, cs[:], ang[:], 16, N, math.pi / 2, tmp)
                    tmp["ob"] = sn
                    self.sin_tile(c, sn[:], ang[:], 16, N, 0.0, tmp)
                    tt(nr, mag, cs, ALU.mult)
                    c.op("vector", lambda e: e.tensor_scalar(out=nr[:], in0=nr[:], scalar1=-1.0, scalar2=None, op0=ALU.add), reads=[nr], writes=[nr])
                    tt(ni, mag, sn, ALU.mult)
                    tt(den, lrq, lrq, ALU.mult)
                    c.op("vector", lambda e: e.tensor_tensor(out=w1_[:], in0=lim[:, sl], in1=lim[:, sl], op=ALU.mult), reads=[lim], writes=[w1_])
                    tt(den, den, w1_, ALU.add)
                    c.op("vector", lambda e: e.reciprocal(out=den[:], in_=den[:]), reads=[den], writes=[den])
                    tt(w1_, nr, lrq, ALU.mult)
                    c.op("vector", lambda e: e.tensor_tensor(out=w2_[:], in0=ni[:], in1=lim[:, sl], op=ALU.mult), reads=[ni, lim], writes=[w2_])
                    tt(fre, w1_, w2_, ALU.add)
                    tt(fre, fre, den, ALU.mult)
                    tt(w1_, ni, lrq, ALU.mult)
                    c.op("vector", lambda e: e.tensor_tensor(out=w2_[:], in0=nr[:], in1=lim[:, sl], op=ALU.mult), reads=[nr, lim], writes=[w2_])
                    tt(fim, w1_, w2_, ALU.subtract)
                    tt(fim, fim, den, ALU.mult)
                    c.op("vector", lambda e: e.tensor_tensor(out=w1_[:], in0=fre[:], in1=br[:, sl], op=ALU.mult), reads=[fre, br], writes=[w1_])
                    c.op("vector", lambda e: e.tensor_tensor(out=w2_[:], in0=fim[:], in1=bi[:, sl], op=ALU.mult), reads=[fim, bi], writes=[w2_])
                    tt(bbre, w1_, w2_, ALU.subtract)
                    c.op("vector", lambda e: e.tensor_tensor(out=w1_[:], in0=fre[:], in1=bi[:, sl], op=ALU.mult), reads=[fre, bi], writes=[w1_])
                    c.op("vector", lambda e: e.tensor_tensor(out=w2_[:], in0=fim[:], in1=br[:, sl], op=ALU.mult), reads=[fim, br], writes=[w2_])
                    tt(bbim, w1_, w2_, ALU.add)
                    v3 = lambda t_: t_[:].rearrange("c (g p) -> c g p", g=8)
                    l3 = lambda t_, h: t_[:].rearrange("c (g m) -> c g m", g=32)[:, gq * 8:(gq + 1) * 8, h * 64:(h + 1) * 64]
                    c.op("vector", lambda e: e.tensor_copy(out=l3(LB1, 0), in_=v3(bbre)), reads=[bbre], writes=[LB1])
                    c.op("vector", lambda e: e.tensor_copy(out=l3(LB1, 1), in_=v3(bbim)), reads=[bbim], writes=[LB1])
                    c.op("vector", lambda e: e.tensor_copy(out=l3(LB2, 0), in_=v3(bbim)), reads=[bbim], writes=[LB2])
                    c.op("vector", lambda e: e.tensor_scalar(out=l3(LB2, 1), in0=v3(bbre), scalar1=-1.0, scalar2=None, op0=ALU.mult),
                         reads=[bbre], writes=[LB2])
                CA = self.spload(c, es2, "CA%d" % j)
                CB = self.spload(c, es2, "CB%d" % j)
                c.op("vector", lambda e: e.tensor_copy(out=LC1[0:64, :], in_=CA[0:64, :]), reads=[CA], writes=[LC1])
                c.op("vector", lambda e: e.tensor_scalar(out=LC1[64:128, :], in0=CA[64:128, :], scalar1=-1.0, scalar2=None, op0=ALU.mult),
                     reads=[CA], writes=[LC1])
                c.op("vector", lambda e: e.tensor_scalar(out=LC2[:], in0=CB[:], scalar1=-1.0, scalar2=None, op0=ALU.mult),
                     reads=[CB], writes=[LC2])
                c.barrier()
            NG = 2
            COS = [c.sb(es, [128, NJ], F32, "COS") for _ in range(2 * NG)]
            SIN = [c.sb(es, [128, NJ], F32, "SIN") for _ in range(2 * NG)]
            ANG = [c.sb(es, [128, NJ], F32, "ANG") for _ in range(2)]
            mkA = lambda nm, dt=F32: c.sb(es, [128, NJ], dt, nm)
            tmpA = dict(a=mkA("ta"), xs=mkA("txs"), kf=mkA("tkf"), m=mkA("tm"), ki=mkA("tki", I32))
            MAGT = [c.sb(es, [128, T], F32, "MAGT") for _ in range(2 * NG)]
            onesF = c.sb(es, [128, T], F32, "onesF")
            c.op("vector", lambda e: e.memset(onesF[:], 1.0), writes=[onesF])
            Rm = [c.sb(es, [128, 128], F32, "Rm") for _ in range(2 * NG)]
            ss = [c.sb(es, [128, 1], F32, "ss") for _ in range(2 * NG)]
            B2 = lambda shape, dt, nm, k=2: [[c.sb(es, shape, dt, nm) for _ in range(k)] for _ in range(NG)]
            ub = B2([16, T], BF16, "ub", 3)
            u32 = B2([16, T], F32, "u32", 3)
            P1 = [c.ps(es, [128, T], F32, "P1") for _ in range(NG)]
            P2 = [c.ps(es, [128, T], F32, "P2") for _ in range(NG)]
            py = [c.ps(es, [128, T], F32, "py") for _ in range(NG)]
            pc = [c.ps(es, [128, T], F32, "pc") for _ in range(NG)]
            m1 = B2([128, T], F32, "m1")
            m2 = B2([128, T], F32, "m2")
            vv = B2([128, T], F32, "vv")
            zz = B2([128, T], F32, "zz")
            q1 = B2([128, T], BF16, "q1")
            q2 = B2([128, T], BF16, "q2")
            init = B2([128, 1], F32, "init")
            yy = B2([16, T], F32, "yy")
            g1 = B2([16, T], F32, "g1")
            g2 = B2([16, T], F32, "g2")
            zo = B2([16, T], F32, "zo")
            zb = B2([16, T], BF16, "zb")

            def tables(g):
                k = g % (2 * NG)
                cs_, sn_, an_ = COS[k], SIN[k], ANG[g % 2]
                c.op("vector", lambda e: e.tensor_scalar(out=an_[:], in0=Jf[:], scalar1=thA[:, g:g + 1], scalar2=None, op0=ALU.mult),
                     reads=[Jf, thA], writes=[an_])
                tmpA["xb"] = an_
                tmpA["ob"] = cs_
                self.sin_tile(c, cs_[:], an_[:], 128, NJ, math.pi / 2, tmpA)
                tmpA["ob"] = sn_
                self.sin_tile(c, sn_[:], an_[:], 128, NJ, 0.0, tmpA)
                mg = MAGT[k]
                c.op("vector", lambda e: e.tensor_scalar(out=mg[:], in0=onesF[:], scalar1=magA[:, g:g + 1], scalar2=None, op0=ALU.mult),
                     reads=[onesF, magA], writes=[mg])
                R_, s_ = Rm[k], ss[k]
                c.op("vector", lambda e: e.tensor_tensor(out=s_[:], in0=sn_[:, T:T + 1], in1=sgn[:], op=ALU.mult), reads=[sn_, sgn], writes=[s_])
                c.op("vector", lambda e: e.tensor_scalar(out=R_[:], in0=ident[:], scalar1=cs_[:, T:T + 1], scalar2=None, op0=ALU.mult),
                     reads=[ident, cs_], writes=[R_])
                c.op("vector", lambda e: e.scalar_tensor_tensor(out=R_[:], in0=swapP[:], scalar=s_[:, 0:1], in1=R_[:], op0=ALU.mult, op1=ALU.add),
                     reads=[swapP, s_, R_], writes=[R_])

            def stageA(g, i):
                gi, k = g % NG, g % (2 * NG)
                tok = slice(i * T, (i + 1) * T)
                ub_, u32_ = ub[gi][i % 3], u32[gi][i % 3]
                c.dma("sync", ub_[:], self.uTb[g * 16:(g + 1) * 16, tok], reads=[self.B_u], writes=[ub_])
                c.dma("sync", u32_[:], self.uT32[g * 16:(g + 1) * 16, tok], reads=[self.B_u], writes=[u32_])
                p1, p2 = P1[gi], P2[gi]
                c.op("tensor", lambda e: e.matmul(p1[:], lhsT=LB1[:, g * 128:(g + 1) * 128], rhs=ub_[:], start=True, stop=True),
                     reads=[LB1, ub_], writes=[p1])
                c.op("tensor", lambda e: e.matmul(p2[:], lhsT=LB2[:, g * 128:(g + 1) * 128], rhs=ub_[:], start=True, stop=True),
                     reads=[LB2, ub_], writes=[p2])
                a_, b_, v_ = m1[gi][i % 2], m2[gi][i % 2], vv[gi][i % 2]
                cs_, sn_ = COS[k], SIN[k]
                c.op("vector", lambda e: e.tensor_tensor(out=a_[:], in0=cs_[:, 0:T], in1=p1[:], op=ALU.mult), reads=[cs_, p1], writes=[a_])
                c.op("vector", lambda e: e.tensor_tensor(out=b_[:], in0=sn_[:, 0:T], in1=p2[:], op=ALU.mult), reads=[sn_, p2], writes=[b_])
                c.op("gpsimd", lambda e: e.tensor_tensor(out=v_[:], in0=a_[:], in1=b_[:], op=ALU.add), reads=[a_, b_], writes=[v_])

            def stageB1(g, i):
                gi, k = g % NG, g % (2 * NG)
                v_, z_ = vv[gi][i % 2], zz[gi][i % 2]
                mg, R_, cs_, sn_ = MAGT[k], Rm[k], COS[k], SIN[k]
                if i == 0:
                    c.op("vector", lambda e: e.tensor_tensor_scan(out=z_[:], data0=mg[:], data1=v_[:], initial=0.0,
                                                                  op0=ALU.mult, op1=ALU.add), reads=[mg, v_], writes=[z_])
                else:
                    ini = init[gi][i % 2]
                    c.op("vector", lambda e: e.tensor_tensor_scan(out=z_[:], data0=mg[:], data1=v_[:], initial=ini[:, 0:1],
                                                                  op0=ALU.mult, op1=ALU.add), reads=[mg, v_, ini], writes=[z_])
                if i < NT - 1:
                    pc_ = pc[gi]
                    nini = init[gi][(i + 1) % 2]
                    c.op("tensor", lambda e: e.matmul(pc_[:, 0:2], lhsT=R_[:], rhs=z_[:, T - 2:T], start=True, stop=True),
                         reads=[R_, z_], writes=[pc_])
                    c.op("scalar", lambda e: e.activation(out=nini[:], in_=pc_[:, 1:2], func=AF.Copy), reads=[pc_], writes=[nini])
                qa, qb = q1[gi][i % 2], q2[gi][i % 2]
                c.op("gpsimd", lambda e: e.tensor_tensor(out=qa[:], in0=cs_[:, 0:T], in1=z_[:], op=ALU.mult), reads=[cs_, z_], writes=[qa])
                c.op("gpsimd", lambda e: e.tensor_tensor(out=qb[:], in0=sn_[:, 0:T], in1=z_[:], op=ALU.mult), reads=[sn_, z_], writes=[qb])

            def stageB2(g, i):
                gi = g % NG
                tok = slice(i * T, (i + 1) * T)
                qa, qb = q1[gi][i % 2], q2[gi][i % 2]
                u32_ = u32[gi][i % 3]
                py_ = py[gi]
                c.op("tensor", lambda e: e.matmul(py_[0:16, :], lhsT=LC1[:, g * 16:(g + 1) * 16], rhs=qa[:], start=True, stop=False),
                     reads=[LC1, qa], writes=[py_])
                c.op("tensor", lambda e: e.matmul(py_[0:16, :], lhsT=LC2[:, g * 16:(g + 1) * 16], rhs=qb[:], start=False, stop=True),
                     reads=[LC2, qb], writes=[py_])
                y_, ga, gb_, zo_, zb_ = yy[gi][i % 2], g1[gi][i % 2], g2[gi][i % 2], zo[gi][i % 2], zb[gi][i % 2]
                c.op("vector", lambda e: e.scalar_tensor_tensor(out=y_[:], in0=u32_[:], scalar=dC[:, g:g + 1], in1=py_[0:16, :],
                                                                op0=ALU.mult, op1=ALU.add), reads=[u32_, dC, py_], writes=[y_])
                c.op("scalar", lambda e: e.activation(out=ga[:], in_=y_[:], func=AF.Square), reads=[y_], writes=[ga])
                c.op("vector", lambda e: e.tensor_scalar(out=ga[:], in0=ga[:], scalar1=0.044715, scalar2=1.0, op0=ALU.mult, op1=ALU.add),
                     reads=[ga], writes=[ga])
                c.op("vector", lambda e: e.tensor_tensor(out=gb_[:], in0=ga[:], in1=y_[:], op=ALU.mult), reads=[ga, y_], writes=[gb_])
                c.op("scalar", lambda e: e.activation(out=gb_[:], in_=gb_[:], func=AF.Sigmoid, scale=float(2.0 * GK)), reads=[gb_], writes=[gb_])
                c.op("vector", lambda e: e.tensor_tensor(out=zo_[:], in0=gb_[:], in1=y_[:], op=ALU.mult), reads=[gb_, y_], writes=[zo_])
                c.op("scalar", lambda e: e.activation(out=zb_[:], in_=zo_[:], func=AF.Copy), reads=[zo_], writes=[zb_])
                c.dma("sync", self.zT32[g * 16:(g + 1) * 16, tok], zo_[:], reads=[zo_], writes=[self.B_z])
                c.dma("sync", self.zTb[g * 16:(g + 1) * 16, tok], zb_[:], reads=[zb_], writes=[self.B_z])

            for g in range(NG):
                tables(g)
            for gp in range(32 // NG):
                gs = [gp * NG + k for k in range(NG)]
                for g in gs:
                    stageA(g, 0)
                for i in range(NT):
                    if i + 1 < NT:
                        for g in gs:
                            stageA(g, i + 1)
                    for g in gs:
                        stageB1(g, i)
                    if i == 2 and gp + 1 < 32 // NG:
                        for g in gs:
                            tables(g + NG)
                    for g in gs:
                        stageB2(g, i)
            c.barrier()

    def even_out(self, c, L):
        j = L // 2
        with contextlib.ExitStack() as es:
            woutf, _ = self.wv(L, "wout")
            wgluf, _ = self.wv(L, "wglu")
            wout = c.sb(es, [128, 8 * D], BF16, "wout")
            wglu = c.sb(es, [128, 4 * 512], BF16, "wglu")
            c.dma("sync", wout[:], woutf.rearrange("(p m) -> p m", p=128), reads=[self.B_wb[L]], writes=[wout])
            c.dma("sync", wglu[:], wgluf.rearrange("(p m) -> p m", p=128), reads=[self.B_wb[L]], writes=[wglu])
            bglu = self.spload(c, es, "bglu%d" % j)
            o = self.ln_setup(c, es)
            zb = [c.sb(es, [128, 4, T], BF16, "zb") for _ in range(2)]
            z32 = [c.sb(es, [128, 4, T], F32, "z32") for _ in range(2)]
            yab = [c.sb(es, [128, 4, T], BF16, "yab") for _ in range(2)]
            ybb = [c.sb(es, [128, T], BF16, "ybb") for _ in range(4)]
            sig = [c.sb(es, [128, T], F32, "sig") for _ in range(2)]
            h32c = [c.sb(es, [128, T], F32, "h32c") for _ in range(3)]
            pg = [c.ps(es, [128, T], F32, "pg") for _ in range(2)]
            pd = [c.ps(es, [128, T], F32, "pd") for _ in range(2)]
            lcol = (L * 3 + 1) * 8
            v4 = lambda ap: ap.rearrange("(kc p) t -> p kc t", p=128)

            def load(i):
                tok = slice(i * T, (i + 1) * T)
                c.dma("sync", zb[i % 2][:], v4(self.zTb)[:, :, tok], reads=[self.B_z], writes=[zb[i % 2]])
                c.dma("sync", z32[i % 2][:], v4(self.zT32)[:, :, tok], reads=[self.B_z], writes=[z32[i % 2]])
                c.dma("sync", yab[i % 2][:], v4(self.yaTb)[:, :, tok], reads=[self.B_ya], writes=[yab[i % 2]])

            def compute(i):
                tok = slice(i * T, (i + 1) * T)
                zbi, z32i, yai = zb[i % 2], z32[i % 2], yab[i % 2]
                for oc in range(4):
                    g_ = pg[oc % 2]
                    for kc in range(4):
                        c.op("tensor", lambda e, kc=kc: e.matmul(g_[:], lhsT=wglu[:, kc * 512 + oc * 128: kc * 512 + (oc + 1) * 128],
                                                                 rhs=zbi[:, kc, :], start=(kc == 0), stop=(kc == 3)),
                             reads=[wglu, zbi], writes=[g_])
                    sg = sig[oc % 2]
                    c.op("scalar", lambda e: e.activation(out=sg[:], in_=g_[:], func=AF.Sigmoid, bias=bglu[:, oc:oc + 1]),
                         reads=[g_, bglu], writes=[sg])
                    yb_ = ybb[oc]
                    c.op("vector", lambda e: e.tensor_tensor(out=yb_[:], in0=z32i[:, oc, :], in1=sg[:], op=ALU.mult),
                         reads=[z32i, sg], writes=[yb_])
                for dc in range(8):
                    hr = h32c[dc % 3]
                    c.dma("sync", hr[:], self.outT[dc * 128:(dc + 1) * 128, tok], reads=[self.B_out], writes=[hr])
                    p_ = pd[dc % 2]
                    for kc in range(8):
                        rhs_b = yai if kc < 4 else ybb[kc - 4]
                        rhs = yai[:, kc, :] if kc < 4 else ybb[kc - 4][:]
                        c.op("tensor", lambda e, kc=kc, rhs=rhs: e.matmul(
                            p_[:], lhsT=wout[:, kc * D + dc * 128: kc * D + (dc + 1) * 128], rhs=rhs,
                            start=(kc == 0), stop=(kc == 7)), reads=[wout, rhs_b], writes=[p_])
                    yd = o.y[dc]
                    c.op("vector", lambda e: e.scalar_tensor_tensor(out=yd[:], in0=hr[:], scalar=float(ALPHA), in1=p_[:],
                                                                    op0=ALU.mult, op1=ALU.add), reads=[hr, p_], writes=[yd])
                    self.ln_accum(c, o, dc)
                self.ln_finish(c, o, lcol, tok, store=True)

            for i in range(NT + 1):
                if i < NT:
                    load(i)
                if i > 0:
                    compute(i - 1)
            c.barrier()

    def rope_tables(self, c):
        CH = 512
        with contextlib.ExitStack() as es:
            rc = self.spload(c, es, "ropec")
            posi = [c.sb(es, [128, CH], I32, "posi") for _ in range(2)]
            posf = [c.sb(es, [128, CH], F32, "posf") for _ in range(2)]
            ang = [c.sb(es, [128, CH], F32, "ang") for _ in range(2)]
            outt = [c.sb(es, [128, CH], F32, "outt") for _ in range(4)]
            mk = lambda nm, dt=F32: c.sb(es, [128, CH], dt, nm)
            tmp = dict(a=mk("ta"), xs=mk("txs"), kf=mk("tkf"), m=mk("tm"), ki=mk("tki", I32))
            n = 0
            for i in range(S // CH):
                sl = slice(i * CH, (i + 1) * CH)
                pi_, pf_ = posi[i % 2], posf[i % 2]
                c.dma("sync", pi_[:], self.posb[:, sl], writes=[pi_])
                c.op("vector", lambda e: e.tensor_copy(out=pf_[:], in_=pi_[:]), reads=[pi_], writes=[pf_])
                for ty in range(2):
                    an_ = ang[ty]
                    c.op("vector", lambda e: e.tensor_scalar(out=an_[:], in0=pf_[:], scalar1=rc[:, 2 * ty:2 * ty + 1], scalar2=None,
                                                             op0=ALU.mult), reads=[pf_, rc], writes=[an_])
                    tmp["xb"] = an_
                    oc_ = outt[n % 4]
                    n += 1
                    tmp["ob"] = oc_
                    self.sin_tile(c, oc_[:], an_[:], 128, CH, math.pi / 2, tmp)
                    c.dma("gpsimd", self.tabs[2 * ty, :, sl], oc_[:], reads=[oc_], writes=[self.B_tabs])
                    os_ = outt[n % 4]
                    n += 1
                    tmp["ob"] = os_
                    self.sin_tile(c, os_[:], an_[:], 128, CH, 0.0, tmp)
                    c.op("vector", lambda e: e.tensor_scalar(out=os_[:], in0=os_[:], scalar1=rc[:, 2 * ty + 1:2 * ty + 2], scalar2=None,
                                                             op0=ALU.mult), reads=[os_, rc], writes=[os_])
                    c.dma("gpsimd", self.tabs[2 * ty + 1, :, sl], os_[:], reads=[os_], writes=[self.B_tabs])
            c.barrier()

    def odd_in(self, c, L):
        with contextlib.ExitStack() as es:
            cinf, _ = self.wv(L, "cin")
            cswf, _ = self.wv(L, "cinsw")
            NM, NS = 2176, 1920
            wm = c.sb(es, [128, 8 * NM], BF16, "wm")
            ws = c.sb(es, [128, 8 * NS], BF16, "ws")
            wmv = cinf.rearrange("(p m) -> p m", p=128)
            wsv = cswf.rearrange("(p m) -> p m", p=128)
            for q in range(4):
                c.dma("sync", wm[:, q * 2 * NM:(q + 1) * 2 * NM], wmv[:, q * 2 * NM:(q + 1) * 2 * NM], reads=[self.B_wb[L]], writes=[wm])
                c.dma("sync", ws[:, q * 2 * NS:(q + 1) * 2 * NS], wsv[:, q * 2 * NS:(q + 1) * 2 * NS], reads=[self.B_wb[L]], writes=[ws])
            hb = [c.sb(es, [128, 8, T], BF16, "hb") for _ in range(2)]
            tb = [c.sb(es, [128, 4, T], F32, "tb") for _ in range(2)]
            pm = [c.ps(es, [128, T], F32, "pm") for _ in range(3)]
            psw = [c.ps(es, [128, T], F32, "psw") for _ in range(3)]
            pvw = [c.ps(es, [128, 512], F32, "pvw") for _ in range(2)]
            ta = [c.sb(es, [128, T], F32, "ta") for _ in range(2)]
            tb2 = [c.sb(es, [128, T], F32, "tb2") for _ in range(2)]
            ro = [c.sb(es, [128, T], BF16, "ro") for _ in range(3)]
            vo = [c.sb(es, [128, 256], BF16, "vo") for _ in range(2)]
            wo = [c.sb(es, [128, 8], F32, "wo") for _ in range(2)]
            hTb_v = self.hTb.rearrange("(kc p) t -> p kc t", p=128)
            tabs_v = self.tabs.rearrange("f p t -> p f t")
            chunks = []
            for h in range(8):
                chunks.append((self.qT[h * 128:(h + 1) * 128], h * 128, h, 0, 128))
            for g in range(2):
                chunks.append((self.kT[g * 128:(g + 1) * 128], 1024 + g * 128, 8 + g, 0, 128))
            for q in range(4):
                chunks.append((self.qiT[q * 128:(q + 1) * 128], 1536 + q * 128, 10 + q, 1, 128))
            chunks.append((None, 2048, 14, 1, 64))
            cnt = [0]

            def load(i):
                tok = slice(i * T, (i + 1) * T)
                c.dma("sync", hb[i % 2][:], hTb_v[:, :, tok], reads=[self.B_hTb], writes=[hb[i % 2]])
                c.dma("sync", tb[i % 2][:], tabs_v[:, :, tok], reads=[self.B_tabs], writes=[tb[i % 2]])

            def compute(i):
                tok = slice(i * T, (i + 1) * T)
                hbi, tbi = hb[i % 2], tb[i % 2]
                for (dst, mo, sc, ty, rows) in chunks:
                    k = cnt[0]
                    cnt[0] += 1
                    p_m, p_s = pm[k % 3], psw[k % 3]
                    for kc in range(8):
                        c.op("tensor", lambda e, kc=kc: e.matmul(p_m[0:rows, :], lhsT=wm[:, kc * NM + mo: kc * NM + mo + rows],
                                                                 rhs=hbi[:, kc, :], start=(kc == 0), stop=(kc == 7)),
                             reads=[wm, hbi], writes=[p_m])
                    for kc in range(8):
                        c.op("tensor", lambda e, kc=kc: e.matmul(p_s[0:rows, :], lhsT=ws[:, kc * NS + sc * 128: kc * NS + sc * 128 + rows],
                                                                 rhs=hbi[:, kc, :], start=(kc == 0), stop=(kc == 7)),
                             reads=[ws, hbi], writes=[p_s])
                    a_, b_, r_ = ta[k % 2], tb2[k % 2], ro[k % 3]
                    c.op("vector", lambda e: e.tensor_tensor(out=a_[0:rows, :], in0=tbi[0:rows, 2 * ty, :], in1=p_m[0:rows, :], op=ALU.mult),
                         reads=[tbi, p_m], writes=[a_])
                    c.op("vector", lambda e: e.tensor_tensor(out=b_[0:rows, :], in0=tbi[0:rows, 2 * ty + 1, :], in1=p_s[0:rows, :], op=ALU.mult),
                         reads=[tbi, p_s], writes=[b_])
                    c.op("gpsimd", lambda e: e.tensor_tensor(out=r_[0:rows, :], in0=a_[0:rows, :], in1=b_[0:rows, :], op=ALU.add),
                         reads=[a_, b_], writes=[r_])
                    if dst is not None:
                        c.dma("gpsimd", dst[:, tok], r_[:], reads=[r_], writes=[self.B_qkv])
                    else:
                        c.dma("gpsimd", self.kiT[0:64, tok], r_[0:64, :], reads=[r_], writes=[self.B_qkv])
                        c.dma("gpsimd", self.kiT[64:128, tok], r_[0:64, :], reads=[r_], writes=[self.B_qkv])
                for sub in range(4):
                    p_vw = pvw[sub % 2]
                    for kc in range(8):
                        c.op("tensor", lambda e, kc=kc: e.matmul(p_vw[:, 0:256], lhsT=hbi[:, kc, sub * 128:(sub + 1) * 128],
                                                                 rhs=wm[:, kc * NM + 1280: kc * NM + 1536], start=(kc == 0), stop=(kc == 7)),
                             reads=[wm, hbi], writes=[p_vw])
                    for kc in range(8):
                        c.op("tensor", lambda e, kc=kc: e.matmul(p_vw[:, 256:264], lhsT=hbi[:, kc, sub * 128:(sub + 1) * 128],
                                                                 rhs=wm[:, kc * NM + 2112: kc * NM + 2120], start=(kc == 0), stop=(kc == 7)),
                             reads=[wm, hbi], writes=[p_vw])
                    v_, w_ = vo[sub % 2], wo[sub % 2]
                    c.op("scalar", lambda e: e.activation(out=v_[:], in_=p_vw[:, 0:256], func=AF.Copy), reads=[p_vw], writes=[v_])
                    c.op("scalar", lambda e: e.activation(out=w_[:], in_=p_vw[:, 256:264], func=AF.Copy, scale=float(512 ** -0.5)), reads=[p_vw], writes=[w_])
                    t0 = i * T + sub * 128
                    c.dma("gpsimd", self.vtok[t0:t0 + 128, :], v_[:], reads=[v_], writes=[self.B_qkv])
                    c.dma("gpsimd", self.witok[t0:t0 + 128, :], w_[:], reads=[w_], writes=[self.B_qkv])

            for i in range(NT + 1):
                if i < NT:
                    load(i)
                if i > 0:
                    compute(i - 1)
            c.barrier()

    def odd_attn(self, c, L, nqs=None):
        QS = 256
        NQS = S // QS
        NIT = 16
        with contextlib.ExitStack() as es:
            KT = c.sb(es, [128, 2, S], BF16, "KT")
            V = c.sb(es, [128, 64, 256], BF16, "V")
            kT_v = self.kT.rearrange("(g p) t -> p g t", p=128)
            v_v = self.vtok.rearrange("(kb p) c -> p kb c", p=128)
            for q in range(4):
                c.dma("sync", KT[:, :, q * 2048:(q + 1) * 2048], kT_v[:, :, q * 2048:(q + 1) * 2048], reads=[self.B_qkv], writes=[KT])
                c.dma("sync", V[:, q * 16:(q + 1) * 16, :], v_v[:, q * 16:(q + 1) * 16, :], reads=[self.B_qkv], writes=[V])
            identf = self.spload(c, es, "ident")
            Wneg = self.spload(c, es, "Wneg")
            Wpos = self.spload(c, es, "Wpos")
            identb = c.sb(es, [128, 128], BF16, "identb")
            onesb = c.sb(es, [128, 128], BF16, "onesb")
            c.op("vector", lambda e: e.tensor_copy(out=identb[:], in_=identf[:]), reads=[identf], writes=[identb])
            c.op("vector", lambda e: e.memset(onesb[:], 1.0), writes=[onesb])
            row = c.sb(es, [128, S], F32, "row")
            msk = c.sb(es, [128, S], BF16, "msk")
            maskT = c.sb(es, [128, 64, QS], BF16, "maskT")
            kib = [c.sb(es, [128, 512], BF16, "kib") for _ in range(3)]
            qib = [c.sb(es, [128, 4, 128], BF16, "qib") for _ in range(2)]
            wib = [c.sb(es, [128, 8], F32, "wib") for _ in range(2)]
            dg = [c.sb(es, [128, 128], BF16, "dg") for _ in range(8)]
            rl = [c.sb(es, [128, 512], BF16, "rl") for _ in range(3)]
            tmpd = c.sb(es, [128, 512], F32, "tmpd")
            st8 = c.sb(es, [128, 8], F32, "st8")
            sc = {k: c.sb(es, [128, 1], F32, k) for k in ["lo", "hi", "mid", "cnt", "ge", "d1", "d2", "dmin", "omin"]}
            qTb = [c.sb(es, [128, 8, QS], BF16, "qTb") for _ in range(2)]
            ex = [c.sb(es, [128, QS], BF16, "ex") for _ in range(3)]
            pb = [c.sb(es, [128, QS], BF16, "pb") for _ in range(3)]
            oev = [c.sb(es, [128, QS], F32, "oev") for _ in range(2)]
            lev = [c.sb(es, [1, QS], F32, "lev") for _ in range(2)]
            plg = [c.ps(es, [128, 512], F32, "plg") for _ in range(2)]
            psc = [c.ps(es, [128, 512], F32, "psc") for _ in range(2)]
            ptm = c.ps(es, [128, 1024], BF16, "ptm")
            plb = [c.ps(es, [128, 512], F32, "plb") for _ in range(2)]
            pst, po, pl = plg, psc, plb
            qiT_v = self.qiT.rearrange("(q p) t -> p q t", p=128)
            qT_v = self.qT.rearrange("(h p) t -> p h t", p=128)
            nk = [0]
            nh = [0]
            for Qs in range(NQS if nqs is None else nqs):
                q0 = Qs * QS
                qt = qTb[Qs % 2]
                c.dma("sync", qt[:], qT_v[:, :, q0:q0 + QS], reads=[self.B_qkv], writes=[qt])
                c.op("gpsimd", lambda e: e.memset(maskT[:, 2 * Qs + 1, 0:128], 0.0), writes=[maskT])
                for b in range(2):
                    i = 2 * Qs + b
                    t0 = i * 128
                    nkb = i // 4 + 1
                    n = nkb * 512
                    qi_, wi_ = qib[i % 2], wib[i % 2]
                    c.dma("sync", qi_[:], qiT_v[:, :, t0:t0 + 128], reads=[self.B_qkv], writes=[qi_])
                    c.dma("sync", wi_[:], self.witok[t0:t0 + 128, :], reads=[self.B_qkv], writes=[wi_])
                    for h in range(8):
                        c.op("vector", lambda e, h=h: e.tensor_scalar(out=dg[h][:], in0=identb[:], scalar1=wi_[:, h:h + 1], scalar2=None,
                                                                      op0=ALU.mult), reads=[identb, wi_], writes=[dg[h]])
                    kis, pss = {}, []

                    def kiload(kb):
                        if kb >= nkb:
                            return
                        ki_ = kib[kb % 3]
                        c.dma("sync", ki_[:], self.kiT[:, kb * 512:(kb + 1) * 512], reads=[self.B_qkv], writes=[ki_])
                        kis[kb] = ki_

                    for kb in range(nkb):
                        pss.append(psc[nk[0] % 2])
                        nk[0] += 1
                    kiload(0)
                    kiload(1)
                    units = [(kb, h) for kb in range(nkb) for h in range(8)]

                    def logit(u):
                        kb, h = units[u]
                        pl_ = plg[u % 2]
                        r0 = (h % 2) * 64
                        c.op("tensor", lambda e: e.matmul(pl_[:], lhsT=qi_[r0:r0 + 64, h // 2, :], rhs=kis[kb][r0:r0 + 64, :],
                                                          start=True, stop=True), reads=[qi_, kis[kb]], writes=[pl_])

                    if LA_IDX:
                        logit(0)
                    for u, (kb, h) in enumerate(units):
                        if h == 6:
                            kiload(kb + 2)
                        if LA_IDX:
                            if u + 1 < len(units):
                                logit(u + 1)
                        else:
                            logit(u)
                        pl_, r_, ps_ = plg[u % 2], rl[u % 3], pss[kb]
                        c.op("scalar", lambda e: e.activation(out=r_[:], in_=pl_[:], func=AF.Relu), reads=[pl_], writes=[r_])
                        c.op("tensor", lambda e: e.matmul(ps_[:], lhsT=dg[h][:], rhs=r_[:], start=(h == 0), stop=(h == 7)),
                             reads=[dg[h], r_], writes=[ps_])
                        if h != 7:
                            continue
                        ksl = slice(kb * 512, (kb + 1) * 512)
                        if kb == nkb - 1:
                            v = i % 4
                            wsl = slice((3 - v) * 128, (3 - v) * 128 + 512)
                            c.op("vector", lambda e: e.tensor_tensor(out=row[:, ksl], in0=Wneg[:, wsl], in1=ps_[:], op=ALU.add),
                                 reads=[Wneg, ps_], writes=[row])
                            c.op("vector", lambda e: e.tensor_tensor(out=tmpd[:], in0=Wpos[:, wsl], in1=ps_[:], op=ALU.add),
                                 reads=[Wpos, ps_], writes=[tmpd])
                            c.op("vector", lambda e: e.tensor_reduce(out=sc["dmin"][:], in_=tmpd[:], axis=AX.X, op=ALU.min),
                                 reads=[tmpd], writes=[sc["dmin"]])
                        else:
                            c.op("vector", lambda e: e.tensor_copy(out=row[:, ksl], in_=ps_[:]), reads=[ps_], writes=[row])
                    c.op("vector", lambda e: e.max(out=st8[:], in_=row[:, 0:n]), reads=[row], writes=[st8])
                    c.op("vector", lambda e: e.tensor_copy(out=sc["hi"][:], in_=st8[:, 0:1]), reads=[st8], writes=[sc["hi"]])
                    if nkb > 1:
                        c.op("vector", lambda e: e.tensor_reduce(out=sc["omin"][:], in_=row[:, 0:n - 512], axis=AX.X, op=ALU.min),
                             reads=[row], writes=[sc["omin"]])
                        c.op("vector", lambda e: e.tensor_tensor(out=sc["lo"][:], in0=sc["dmin"][:], in1=sc["omin"][:], op=ALU.min),
                             reads=[sc["dmin"], sc["omin"]], writes=[sc["lo"]])
                    else:
                        c.op("vector", lambda e: e.tensor_copy(out=sc["lo"][:], in_=sc["dmin"][:]), reads=[sc["dmin"]], writes=[sc["lo"]])
                    if i >= 2:
                        for it in range(NIT):
                            lo, hi, mid, cnt, ge, d1, d2 = (sc[k] for k in ["lo", "hi", "mid", "cnt", "ge", "d1", "d2"])
                            c.op("vector", lambda e: e.tensor_scalar(out=mid[:], in0=lo[:], scalar1=hi[:, 0:1], scalar2=0.5,
                                                                     op0=ALU.add, op1=ALU.mult), reads=[lo, hi], writes=[mid])
                            c.op("vector", lambda e: e.tensor_scalar(out=msk[:, 0:n], in0=row[:, 0:n], scalar1=mid[:, 0:1], scalar2=None,
                                                                     op0=ALU.is_ge, op1=ALU.add, accum_out=cnt[:]),
                                 reads=[row, mid], writes=[msk, cnt])
                            c.op("vector", lambda e: e.tensor_scalar(out=ge[:], in0=cnt[:], scalar1=255.5, scalar2=None, op0=ALU.is_ge),
                                 reads=[cnt], writes=[ge])
                            c.op("vector", lambda e: e.tensor_tensor(out=d1[:], in0=mid[:], in1=lo[:], op=ALU.subtract), reads=[mid, lo], writes=[d1])
                            c.op("vector", lambda e: e.tensor_tensor(out=d2[:], in0=hi[:], in1=mid[:], op=ALU.subtract), reads=[hi, mid], writes=[d2])
                            c.op("vector", lambda e: e.scalar_tensor_tensor(out=lo[:], in0=d1[:], scalar=ge[:, 0:1], in1=lo[:],
                                                                            op0=ALU.mult, op1=ALU.add), reads=[d1, ge, lo], writes=[lo])
                            c.op("vector", lambda e: e.scalar_tensor_tensor(out=hi[:], in0=d2[:], scalar=ge[:, 0:1], in1=mid[:],
                                                                            op0=ALU.mult, op1=ALU.add), reads=[d2, ge, mid], writes=[hi])
                    nv = (i + 1) * 128
                    c.op("vector", lambda e: e.tensor_scalar(out=msk[:, 0:nv], in0=row[:, 0:nv], scalar1=sc["lo"][:, 0:1], scalar2=None,
                                                             op0=ALU.is_ge), reads=[row, sc["lo"]], writes=[msk])
                    for k4 in range(0, i + 1, 4):
                        m4 = min(4, i + 1 - k4)
                        for kk in range(m4):
                            c.op("tensor", lambda e, kk=kk: e.transpose(out=ptm[:, kk * 128:(kk + 1) * 128],
                                                                        in_=msk[:, (k4 + kk) * 128:(k4 + kk + 1) * 128], identity=identb[:]),
                                 reads=[msk, identb], writes=[ptm])
                        c.op("scalar", lambda e: e.activation(
                            out=maskT[:, k4:k4 + m4, b * 128:(b + 1) * 128],
                            in_=ptm[:, 0:m4 * 128].rearrange("p (k q) -> p k q", k=m4), func=AF.Copy), reads=[ptm], writes=[maskT])
                nkk = 2 * Qs + 2
                aunits = [(h, kk) for h in range(8) for kk in range(nkk)]

                def smm(u):
                    h, kk = aunits[u]
                    st_ = pst[u % 2]
                    c.op("tensor", lambda e: e.matmul(st_[:, 0:QS], lhsT=KT[:, h // 4, kk * 128:(kk + 1) * 128], rhs=qt[:, h, :], start=True, stop=True),
                         reads=[KT, qt], writes=[st_])

                if LA_ATT:
                    smm(0)
                for u, (h, kk) in enumerate(aunits):
                    if LA_ATT:
                        if u + 1 < len(aunits):
                            smm(u + 1)
                    else:
                        smm(u)
                    g = h // 4
                    po_, pl_ = po[h % 2], pl[h % 2]
                    st_, e_, p_ = pst[u % 2], ex[u % 3], pb[u % 3]
                    c.op("scalar", lambda e: e.activation(out=e_[:], in_=st_[:, 0:QS], func=AF.Exp, scale=float(128 ** -0.5)), reads=[st_], writes=[e_])
                    meng = "gpsimd" if u % 2 == 0 else "vector"
                    c.op(meng, lambda e: e.tensor_tensor(out=p_[:], in0=e_[:], in1=maskT[:, kk, :], op=ALU.mult),
                         reads=[e_, maskT], writes=[p_])
                    c.op("tensor", lambda e: e.matmul(po_[:, 0:QS], lhsT=V[:, kk, g * 128:(g + 1) * 128], rhs=p_[:], start=(kk == 0), stop=(kk == nkk - 1)),
                         reads=[V, p_], writes=[po_])
                    c.op("tensor", lambda e: e.matmul(pl_[:, 0:QS], lhsT=onesb[:], rhs=p_[:], start=(kk == 0), stop=(kk == nkk - 1)),
                         reads=[onesb, p_], writes=[pl_])
                    if kk != nkk - 1:
                        continue
                    o_, l_ = oev[h % 2], lev[h % 2]
                    c.op("scalar", lambda e: e.activation(out=o_[:], in_=po_[:, 0:QS], func=AF.Copy), reads=[po_], writes=[o_])
                    c.op("scalar", lambda e: e.activation(out=l_[:], in_=pl_[0:1, 0:QS], func=AF.Copy), reads=[pl_], writes=[l_])
                    c.dma("sync", self.attT[h * 128:(h + 1) * 128, q0:q0 + QS], o_[:], reads=[o_], writes=[self.B_att])
                    c.dma("sync", self.lden[h:h + 1, q0:q0 + QS], l_[:], reads=[l_], writes=[self.B_att])
            c.barrier()

    def odd_out(self, c, L):
        with contextlib.ExitStack() as es:
            coutf, _ = self.wv(L, "cout")
            wout = c.sb(es, [128, 8 * D], BF16, "wout")
            c.dma("sync", wout[:], coutf.rearrange("(p m) -> p m", p=128), reads=[self.B_wb[L]], writes=[wout])
            o = self.ln_setup(c, es)
            at = [c.sb(es, [128, 8, T], F32, "at") for _ in range(2)]
            ld = [c.sb(es, [128, 8, T], F32, "ld") for _ in range(2)]
            ab = [c.sb(es, [128, T], BF16, "ab") for _ in range(8)]
            h32c = [c.sb(es, [128, T], F32, "h32c") for _ in range(3)]
            pd = [c.ps(es, [128, T], F32, "pd") for _ in range(2)]
            lcol = (L * 3 + 1) * 8
            at_v = self.attT.rearrange("(kc p) t -> p kc t", p=128)

            def load(i):
                tok = slice(i * T, (i + 1) * T)
                c.dma("sync", at[i % 2][:], at_v[:, :, tok], reads=[self.B_att], writes=[at[i % 2]])
                for h in range(8):
                    c.dma("sync", ld[i % 2][:, h, :], self.lden[h:h + 1, tok].partition_broadcast(128), reads=[self.B_att], writes=[ld[i % 2]])

            def compute(i):
                tok = slice(i * T, (i + 1) * T)
                ati, ldi = at[i % 2], ld[i % 2]
                c.op("vector", lambda e: e.reciprocal(out=ldi[:], in_=ldi[:]), reads=[ldi], writes=[ldi])
                for h in range(8):
                    eng = "vector" if h % 2 == 0 else "gpsimd"
                    c.op(eng, lambda e: e.tensor_tensor(out=ab[h][:], in0=ati[:, h, :], in1=ldi[:, h, :], op=ALU.mult),
                         reads=[ati, ldi], writes=[ab[h]])
                for dc in range(8):
                    hr = h32c[dc % 3]
                    c.dma("sync", hr[:], self.outT[dc * 128:(dc + 1) * 128, tok], reads=[self.B_out], writes=[hr])
                    p_ = pd[dc % 2]
                    for kc in range(8):
                        c.op("tensor", lambda e, kc=kc: e.matmul(p_[:], lhsT=wout[:, kc * D + dc * 128: kc * D + (dc + 1) * 128], rhs=ab[kc][:],
                                                                 start=(kc == 0), stop=(kc == 7)), reads=[wout, ab[kc]], writes=[p_])
                    yd = o.y[dc]
                    c.op("vector", lambda e: e.scalar_tensor_tensor(out=yd[:], in0=hr[:], scalar=float(ALPHA), in1=p_[:],
                                                                    op0=ALU.mult, op1=ALU.add), reads=[hr, p_], writes=[yd])
                    self.ln_accum(c, o, dc)
                self.ln_finish(c, o, lcol, tok, store=True)

            for i in range(NT + 1):
                if i < NT:
                    load(i)
                if i > 0:
                    compute(i - 1)
            c.barrier()

    def build(self):
        nc = self.nc
        with contextlib.ExitStack() as es:
            c = Ctx(nc, es)
            self.cast_weights(c)
            self.rope_tables(c)
            stages = []
            if self.stop_after == ("pro", 0):
                return nc
            for L in range(DEPTH):
                stages.append(("f0", L))
                stages.append(("mix", L))
                stages.append(("f1", L))
            first = True
            if self.start_at is not None:
                stages = stages[stages.index(self.start_at):]
                first = False
                for r0 in range(0, D, 128):
                    c.dma("sync", self.outT[r0:r0 + 128, :], self.xT[r0:r0 + 128, :], writes=[self.B_out])
                c.barrier()
            for kind, L in stages:
                if kind == "f0":
                    src, bsrc = (self.xT, Buf(self.xT, "xT")) if first else (self.outT, self.B_out)
                    self.ffn(c, L, 0, src, bsrc, ple=False)
                    first = False
                elif kind == "f1":
                    self.ffn(c, L, 1, self.outT, self.B_out, ple=True)
                elif kind == "mix" and L % 2 == 0:
                    self.even_in(c, L)
                    if self.stop_after == ("ein", L):
                        break
                    self.even_s5(c, L)
                    if self.stop_after == ("es5", L):
                        break
                    self.even_out(c, L)
                elif kind == "mix":
                    self.odd_in(c, L)
                    if self.stop_after == ("oin", L):
                        break
                    self.odd_attn(c, L, nqs=self.nqs)
                    if self.stop_after == ("oat", L):
                        break
                    self.odd_out(c, L)
                if self.stop_after == (kind, L):
                    break
            c.barrier()
        return nc


def _prep_inputs(inputs, b):
    m = {}
    m["xT"] = np.ascontiguousarray(np.asarray(inputs["x"][b], np.float32).T)
    m["pT"] = np.ascontiguousarray(np.asarray(inputs["p"][:, b], np.float32).transpose(0, 2, 1))
    m["posb"] = np.ascontiguousarray(np.broadcast_to(np.asarray(inputs["positions"][b], np.int32)[None, :], (128, S)))
    return m


def _shared_inputs(inputs):
    m = {}
    for L in range(DEPTH):
        m["wl%d" % L] = _layer_blob_layout(L, inputs).finish()
    m["sp"] = _small_params(inputs).finish()
    return m


def kernel(**inputs):
    prog = Prog()
    nc = prog.build()
    shared = _shared_inputs(inputs)
    in_maps = []
    for core in range(NCORES):
        m = dict(shared)
        m.update(_prep_inputs(inputs, core % 4))
        in_maps.append(m)
    res = run_bass_kernel_spmd(nc, in_maps, core_ids=list(range(NCORES)))
    out = np.stack([np.ascontiguousarray(res.results[b]["outT"].T) for b in range(4)], axis=0)
    return out.astype(np.float32)
```

```python
import contextlib
import math
import numpy as np
import concourse.bass as bass
import concourse.mybir as mybir
from concourse.bass_utils import run_bass_kernel_spmd

F32 = mybir.dt.float32
BF16 = mybir.dt.bfloat16
I32 = mybir.dt.int32
AF = mybir.ActivationFunctionType
ALU = mybir.AluOpType
AX = mybir.AxisListType

S = 8192
D = 1024
FF = 2816
NFC = FF // 128
T = 512
NT = S // T
DEPTH = 4
ALPHA = (2 * DEPTH) ** 0.25
EPS = 1e-5
NCORES = 8
import os
LA_IDX = int(os.environ.get('LA_IDX', '1'))
LA_ATT = int(os.environ.get('LA_ATT', '1'))


class _Eng:
    def __init__(self, name, h, sem):
        self.name, self.h, self.sem = name, h, sem
        self.count = 0
        self.seen = {}


class Buf:
    def __init__(self, t, name):
        self.t = t
        self.name = name
        self.w = None
        self.r = {}

    def __getitem__(self, k):
        return self.t[k]


class Ctx:
    def __init__(self, nc, es):
        self.nc, self.es = nc, es
        self.e = {}
        for name in ["tensor", "vector", "scalar", "gpsimd"]:
            sem = es.enter_context(nc.semaphore("s_" + name))
            self.e[name] = _Eng(name, getattr(nc, name), sem)
        self.dq = {}
        for name, n in [("sync", 16), ("gpsimd", 8)]:
            sems = [es.enter_context(nc.semaphore("d_%s%d" % (name, i))) for i in range(n)]
            self.dq[name] = dict(h=getattr(nc, name), sems=sems, cnt=[0] * n, i=0, seen={})
        self.nbuf = 0

    def sb(self, es, shape, dt, name=None):
        self.nbuf += 1
        name = (name or "b") + "_%d" % self.nbuf
        return Buf(es.enter_context(self.nc.sbuf_tensor(name, list(shape), dt)), name)

    def ps(self, es, shape, dt, name=None):
        self.nbuf += 1
        name = (name or "p") + "_%d" % self.nbuf
        return Buf(es.enter_context(self.nc.psum_tensor(name, list(shape), dt)), name)

    def _need(self, seen, h, dep, me, skip_same):
        if dep is None:
            return
        src, val = dep
        if src[0] == "E":
            if src[1] == me and skip_same:
                return
            sem = self.e[src[1]].sem
        else:
            sem = self.dq[src[1]]["sems"][src[2]]
        if seen.get(src, -1) >= val:
            return
        h.wait_ge(sem, val)
        seen[src] = val

    def op(self, eng, fn, reads=(), writes=()):
        E = self.e[eng]
        same_ok = eng == "tensor"
        for b in reads:
            self._need(E.seen, E.h, b.w, eng, same_ok)
        for b in writes:
            self._need(E.seen, E.h, b.w, eng, same_ok)
            for d in list(b.r.items()):
                self._need(E.seen, E.h, d, eng, True)
        ins = fn(E.h)
        E.count += 1
        ins.then_inc(E.sem, 1)
        tag = (("E", eng), E.count)
        for b in reads:
            b.r[tag[0]] = tag[1]
        for b in writes:
            b.w = tag
            b.r = {}
        return ins

    def dma(self, qn, out, in_, reads=(), writes=(), **kw):
        q = self.dq[qn]
        h = q["h"]
        seen = self.e[qn].seen if qn in self.e else q["seen"]
        me = qn if qn in self.e else None
        for b in reads:
            self._need(seen, h, b.w, me, False)
        for b in writes:
            self._need(seen, h, b.w, me, False)
            for d in list(b.r.items()):
                self._need(seen, h, d, me, False)
        i = q["i"]
        q["i"] = (i + 1) % len(q["sems"])
        src = ("D", qn, i)
        if q["cnt"][i] > 0 and seen.get(src, -1) < q["cnt"][i]:
            h.wait_ge(q["sems"][i], q["cnt"][i])
            seen[src] = q["cnt"][i]
        ins = h.dma_start(out=out, in_=in_, **kw)
        q["cnt"][i] += 16
        ins.then_inc(q["sems"][i], 16)
        tag = (src, q["cnt"][i])
        for b in reads:
            b.r[tag[0]] = tag[1]
        for b in writes:
            b.w = tag
            b.r = {}
        return ins

    def barrier(self):
        deps = [(("E", n), E.count) for n, E in self.e.items() if E.count > 0]
        for qn, q in self.dq.items():
            for i, v in enumerate(q["cnt"]):
                if v > 0:
                    deps.append((("D", qn, i), v))
        for n, E in self.e.items():
            for d in deps:
                self._need(E.seen, E.h, d, n, True)
        q = self.dq["sync"]
        for d in deps:
            self._need(q["seen"], q["h"], d, None, False)


def _kmaj(w):
    K, N = w.shape
    return np.ascontiguousarray(w.reshape(K // 128, 128, N).transpose(1, 0, 2))


def _swap_index():
    idx = []
    swA = [d + 16 if d < 16 else (d - 16 if d < 32 else d) for d in range(128)]
    swI = [d + 8 if d < 8 else (d - 8 if d < 16 else d) for d in range(64)]
    for h in range(8):
        idx += [h * 128 + v for v in swA]
    for g in range(2):
        idx += [1024 + g * 128 + v for v in swA]
    for h in range(8):
        idx += [1536 + h * 64 + v for v in swI]
    idx += [2048 + v for v in swI] + [2048 + v for v in swI]
    return np.array(idx, np.int64)


_SW_IDX = _swap_index()
BIG = 3.0e38


class Blob:
    def __init__(self):
        self.parts, self.off, self.n = [], {}, 0

    def add(self, name, arr):
        a = np.ascontiguousarray(arr, dtype=np.float32).reshape(-1)
        self.off[name] = (self.n, tuple(arr.shape))
        self.parts.append(a)
        self.n += a.size
        pad = (-self.n) % 128
        if pad:
            self.parts.append(np.zeros(pad, np.float32))
            self.n += pad

    def finish(self, row=2048 * 128):
        pad = (-self.n) % row
        if pad:
            self.parts.append(np.zeros(pad, np.float32))
            self.n += pad
        return np.concatenate(self.parts).reshape(-1, 2048)


def _layer_blob_layout(L, inputs=None):
    b = Blob()

    def get(name, idx, shape):
        if inputs is None:
            return np.zeros(shape, np.float32)
        a = inputs[name]
        for i in idx:
            a = a[i]
        return np.asarray(a, np.float32)

    for s in range(2):
        w1 = get("ffn_w1", (L, s), (D, FF))
        w3 = get("ffn_w3", (L, s), (D, FF))
        w2 = get("ffn_w2", (L, s), (FF, D))
        a1 = w1.reshape(8, 128, NFC, 128).transpose(2, 1, 0, 3)
        a3 = w3.reshape(8, 128, NFC, 128).transpose(2, 1, 0, 3)
        b.add("w13_%d" % s, np.stack([a1, a3], axis=2))
        b.add("w2_%d" % s, _kmaj(w2))
    b.add("wg", _kmaj(get("ple_w_gate", (L,), (D, D))))
    b.add("wp", _kmaj(get("ple_w_proj", (L,), (256, D))))
    j = L // 2
    if L % 2 == 0:
        b.add("win", _kmaj(get("ab_w_in", (j,), (D, 2048))))
        b.add("wout", _kmaj(get("ab_w_out", (j,), (D, D))))
        b.add("wglu", _kmaj(get("s5_w_glu", (j,), (512, 512))))
    else:
        wc = get("c_w_in", (j,), (D, 2120))
        wpad = np.zeros((D, 2176), np.float32)
        wpad[:, :2120] = wc
        b.add("cin", _kmaj(wpad))
        b.add("cinsw", _kmaj(np.ascontiguousarray(wc[:, _SW_IDX])))
        b.add("cout", _kmaj(get("c_w_out", (j,), (D, D))))
    return b


def _small_params(inputs=None):
    b = Blob()

    def get(name, shape):
        if inputs is None:
            return np.zeros(shape, np.float32)
        return np.asarray(inputs[name], np.float32)

    g = get("ln_g", (DEPTH, 3, D))
    bb = get("ln_b", (DEPTH, 3, D))
    b.add("ln_g", g.reshape(DEPTH, 3, 8, 128).transpose(3, 0, 1, 2))
    b.add("ln_b", bb.reshape(DEPTH, 3, 8, 128).transpose(3, 0, 1, 2))
    sgn = np.ones((128, 1), np.float32)
    sgn[64:] = -1.0
    b.add("sgn", sgn)
    b.add("ident", np.eye(128, dtype=np.float32))
    sw = np.zeros((128, 128), np.float32)
    for k in range(64):
        sw[k, k + 64] = 1.0
        sw[k + 64, k] = 1.0
    b.add("swapP", sw)
    inv_a = (500000.0 ** (-np.arange(0, 32, 2, dtype=np.float32) / np.float32(32))).astype(np.float32)
    inv_i = (500000.0 ** (-np.arange(0, 16, 2, dtype=np.float32) / np.float32(16))).astype(np.float32)
    rc = np.zeros((128, 4), np.float32)
    for r in range(128):
        if r < 32:
            rc[r, 0] = inv_a[r % 16]
            rc[r, 1] = -1.0 if r < 16 else 1.0
        dd = r % 64
        if dd < 16:
            rc[r, 2] = inv_i[dd % 8]
            rc[r, 3] = -1.0 if dd < 8 else 1.0
    b.add("ropec", rc)
    b.add("pow2", np.broadcast_to((2.0 ** -(np.arange(16, dtype=np.float32) + 1.0))[None, :], (128, 16)))
    xx = np.arange(896)[None, :]
    qq = np.arange(128)[:, None]
    b.add("Wneg", np.where(xx <= qq + 384, 0.0, -BIG).astype(np.float32))
    b.add("Wpos", np.where(xx <= qq + 384, 0.0, BIG).astype(np.float32))
    for j in range(2):
        lre = get("s5_lam_re", (2, 32, 64))[j]
        lim = get("s5_lam_im", (2, 32, 64))[j]
        ldt = get("s5_log_dt", (2, 32))[j]
        b.add("lamre_A%d" % j, np.concatenate([lre.T, lre.T], 0))
        b.add("lamim_A%d" % j, np.concatenate([lim.T, lim.T], 0))
        b.add("logdt_A%d" % j, np.broadcast_to(ldt[None, :], (128, 32)))
        b.add("lamre_C%d" % j, np.broadcast_to(lre.reshape(1, 2048), (16, 2048)))
        b.add("lamim_C%d" % j, np.broadcast_to(lim.reshape(1, 2048), (16, 2048)))
        b.add("logdt_C%d" % j, np.broadcast_to(np.repeat(ldt, 64)[None, :], (16, 2048)))
        b.add("bre_C%d" % j, get("s5_b_re", (2, 32, 64, 16))[j].transpose(2, 0, 1))
        b.add("bim_C%d" % j, get("s5_b_im", (2, 32, 64, 16))[j].transpose(2, 0, 1))
        cre = get("s5_c_re", (2, 32, 16, 64))[j].transpose(2, 0, 1)
        cim = get("s5_c_im", (2, 32, 16, 64))[j].transpose(2, 0, 1)
        b.add("CA%d" % j, np.concatenate([cre, cim], 0))
        b.add("CB%d" % j, np.concatenate([cim, cre], 0))
        b.add("d_C%d" % j, get("s5_d", (2, 512))[j].reshape(32, 16).T)
        b.add("convw%d" % j, get("conv_w", (2, 3, 512))[j].reshape(3, 4, 128).transpose(2, 1, 0))
        b.add("bglu%d" % j, get("s5_b_glu", (2, 512))[j].reshape(4, 128).T)
    return b


class Prog:
    def __init__(self, stop_after=None, debug=False):
        self.stop_after = stop_after
        self.nqs = None
        self.start_at = None
        dk = "ExternalOutput" if debug else "Internal"
        nc = self.nc = bass.Bass("TRN2", target_bir_lowering=False)
        self.xT = nc.dram_tensor("xT", [D, S], F32, kind="ExternalInput").ap()
        self.pT = nc.dram_tensor("pT", [DEPTH, 256, S], F32, kind="ExternalInput").ap()
        self.lay = [_layer_blob_layout(L) for L in range(DEPTH)]
        self.wl, self.wb = [], []
        for L in range(DEPTH):
            rows = self.lay[L].finish().shape[0]
            self.wl.append(nc.dram_tensor("wl%d" % L, [rows, 2048], F32, kind="ExternalInput").ap())
            self.wb.append(nc.dram_tensor("wb%d" % L, [rows, 2048], BF16, kind="Internal").ap())
        self.splay = _small_params()
        sp_rows = self.splay.finish().shape[0]
        self.sp = nc.dram_tensor("sp", [sp_rows, 2048], F32, kind="ExternalInput").ap()
        self.outT = nc.dram_tensor("outT", [D, S], F32, kind="ExternalOutput").ap()
        self.hTb = nc.dram_tensor("hTb", [D, S], BF16, kind="Internal").ap()
        self.pTb = nc.dram_tensor("pTb", [DEPTH, 256, S], BF16, kind="Internal").ap()
        self.yaTb = nc.dram_tensor("yaTb", [512, S], BF16, kind=dk).ap()
        self.uT32 = nc.dram_tensor("uT32", [512, S], F32, kind=dk).ap()
        self.uTb = nc.dram_tensor("uTb", [512, S], BF16, kind="Internal").ap()
        self.zT32 = nc.dram_tensor("zT32", [512, S], F32, kind=dk).ap()
        self.zTb = nc.dram_tensor("zTb", [512, S], BF16, kind="Internal").ap()
        self.posb = nc.dram_tensor("posb", [128, S], I32, kind="ExternalInput").ap()
        self.tabs = nc.dram_tensor("tabs", [4, 128, S], F32, kind="Internal").ap()
        self.qT = nc.dram_tensor("qT", [1024, S], BF16, kind="Internal").ap()
        self.kT = nc.dram_tensor("kT", [256, S], BF16, kind="Internal").ap()
        self.vtok = nc.dram_tensor("vtok", [S, 256], BF16, kind="Internal").ap()
        self.qiT = nc.dram_tensor("qiT", [512, S], BF16, kind="Internal").ap()
        self.kiT = nc.dram_tensor("kiT", [128, S], BF16, kind="Internal").ap()
        self.witok = nc.dram_tensor("witok", [S, 8], F32, kind="Internal").ap()
        self.attT = nc.dram_tensor("attT", [1024, S], F32, kind=dk).ap()
        self.lden = nc.dram_tensor("lden", [8, S], F32, kind=dk).ap()
        self.B_tabs = Buf(self.tabs, "tabs")
        self.B_qkv = Buf(self.qT, "qkv")
        self.B_att = Buf(self.attT, "att")
        self.B_ya = Buf(self.yaTb, "yaTb")
        self.B_u = Buf(self.uT32, "u")
        self.B_z = Buf(self.zT32, "z")
        self.B_out = Buf(self.outT, "outT")
        self.B_hTb = Buf(self.hTb, "hTb")
        self.B_wb = [Buf(self.wb[L], "wb%d" % L) for L in range(DEPTH)]
        self.B_pTb = Buf(self.pTb, "pTb")

    def wv(self, L, name):
        off, shape = self.lay[L].off[name]
        n = int(np.prod(shape))
        flat = self.wb[L].rearrange("r c -> (r c)")[off:off + n]
        return flat, shape

    def spv(self, name):
        off, shape = self.splay.off[name]
        n = int(np.prod(shape))
        return self.sp.rearrange("r c -> (r c)")[off:off + n], shape

    def cast_weights(self, c):
        for L in range(DEPTH):
            rows = self.wl[L].shape[0]
            for r0 in range(0, rows, 128):
                c.dma("gpsimd", self.wb[L][r0:r0 + 128, :], self.wl[L][r0:r0 + 128, :],
                      writes=[self.B_wb[L]])
        for r0 in range(0, D, 128):
            for c0 in range(0, S, 2048):
                c.dma("gpsimd", self.hTb[r0:r0 + 128, c0:c0 + 2048], self.xT[r0:r0 + 128, c0:c0 + 2048],
                      writes=[self.B_hTb])
        for L in range(DEPTH):
            for r0 in range(0, 256, 128):
                for c0 in range(0, S, 2048):
                    c.dma("gpsimd", self.pTb[L, r0:r0 + 128, c0:c0 + 2048],
                          self.pT[L, r0:r0 + 128, c0:c0 + 2048], writes=[self.B_pTb])
        c.barrier()

    def ln_setup(self, c, es):
        o = type("LN", (), {})()
        o.y = [c.sb(es, [128, T], F32, "y") for _ in range(8)]
        o.ybf = [c.sb(es, [128, T], BF16, "ybf") for _ in range(2)]
        o.ysq = [c.sb(es, [128, T], BF16, "ysq") for _ in range(2)]
        o.ob = [c.sb(es, [128, T], BF16, "ob") for _ in range(8)]
        o.t1 = [c.sb(es, [128, T], F32, "t1") for _ in range(2)]
        o.t2 = [c.sb(es, [128, T], F32, "t2") for _ in range(2)]
        o.mean = c.sb(es, [128, T], F32, "mean")
        o.msq = c.sb(es, [128, T], F32, "msq")
        o.var = c.sb(es, [128, T], F32, "var")
        o.rstd = c.sb(es, [128, T], F32, "rstd")
        o.ones = c.sb(es, [128, 128], BF16, "ones")
        o.lng = c.sb(es, [128, DEPTH * 3 * 8], F32, "lng")
        o.lnb = c.sb(es, [128, DEPTH * 3 * 8], F32, "lnb")
        gf, _ = self.spv("ln_g")
        bf_, _ = self.spv("ln_b")
        c.dma("sync", o.lng[:], gf.rearrange("(p m) -> p m", p=128), writes=[o.lng])
        c.dma("sync", o.lnb[:], bf_.rearrange("(p m) -> p m", p=128), writes=[o.lnb])
        c.op("vector", lambda e: e.memset(o.ones[:], 1.0), writes=[o.ones])
        o.pss = c.ps(es, [128, T], F32, "pss")
        o.psq = c.ps(es, [128, T], F32, "psq")
        return o

    def ln_accum(self, c, o, dc):
        yd = o.y[dc]
        yb_, yq_ = o.ybf[dc % 2], o.ysq[dc % 2]
        c.op("scalar", lambda e: e.activation(out=yb_[:], in_=yd[:], func=AF.Copy), reads=[yd], writes=[yb_])
        c.op("scalar", lambda e: e.activation(out=yq_[:], in_=yd[:], func=AF.Square), reads=[yd], writes=[yq_])

    def ln_pe(self, c, o, dc):
        yb_, yq_ = o.ybf[dc % 2], o.ysq[dc % 2]
        c.op("tensor", lambda e: e.matmul(o.pss[:], lhsT=o.ones[:], rhs=yb_[:], start=(dc == 0), stop=(dc == 7)),
             reads=[o.ones, yb_], writes=[o.pss])
        c.op("tensor", lambda e: e.matmul(o.psq[:], lhsT=o.ones[:], rhs=yq_[:], start=(dc == 0), stop=(dc == 7)),
             reads=[o.ones, yq_], writes=[o.psq])

    def ln_finish(self, c, o, lcol, tok, store):
        mean_sb, msq, var, rstd = o.mean, o.msq, o.var, o.rstd
        c.op("scalar", lambda e: e.activation(out=mean_sb[:], in_=o.pss[:], func=AF.Copy, scale=1.0 / D),
             reads=[o.pss], writes=[mean_sb])
        c.op("scalar", lambda e: e.activation(out=msq[:], in_=mean_sb[:], func=AF.Square), reads=[mean_sb], writes=[msq])
        c.op("vector", lambda e: e.scalar_tensor_tensor(out=var[:], in0=o.psq[:], scalar=1.0 / D, in1=msq[:],
                                                        op0=ALU.mult, op1=ALU.subtract),
             reads=[o.psq, msq], writes=[var])
        c.op("vector", lambda e: e.tensor_scalar(out=var[:], in0=var[:], scalar1=float(EPS), scalar2=None, op0=ALU.add),
             reads=[var], writes=[var])
        c.op("scalar", lambda e: e.activation(out=msq[:], in_=var[:], func=AF.Sqrt), reads=[var], writes=[msq])
        c.op("vector", lambda e: e.reciprocal(out=rstd[:], in_=msq[:]), reads=[msq], writes=[rstd])
        for dc in range(8):
            yd = o.y[dc]
            a1, a2 = o.t1[dc % 2], o.t2[dc % 2]
            c.op("gpsimd", lambda e: e.tensor_tensor(out=a1[:], in0=yd[:], in1=mean_sb[:], op=ALU.subtract),
                 reads=[yd, mean_sb], writes=[a1])
            c.op("gpsimd", lambda e: e.tensor_tensor(out=a2[:], in0=a1[:], in1=rstd[:], op=ALU.mult),
                 reads=[a1, rstd], writes=[a2])
            gcol, bcol = o.lng[:, lcol + dc:lcol + dc + 1], o.lnb[:, lcol + dc:lcol + dc + 1]
            c.op("scalar", lambda e: e.activation(out=yd[:], in_=a2[:], func=AF.Identity, bias=bcol, scale=gcol),
                 reads=[a2, o.lng, o.lnb], writes=[yd])
            o_ = o.ob[dc]
            c.op("scalar", lambda e: e.activation(out=o_[:], in_=a2[:], func=AF.Identity, bias=bcol, scale=gcol),
                 reads=[a2, o.lng, o.lnb], writes=[o_])
            if store:
                c.dma("gpsimd", self.outT[dc * 128:(dc + 1) * 128, tok], yd[:], reads=[yd], writes=[self.B_out])
                c.dma("gpsimd", self.hTb[dc * 128:(dc + 1) * 128, tok], o_[:], reads=[o_], writes=[self.B_hTb])

    def ffn(self, c, L, s, src32, B_src32, ple):
        j_ln = 0 if s == 0 else 2
        with contextlib.ExitStack() as es:
            w13f, _ = self.wv(L, "w13_%d" % s)
            w13v = w13f.rearrange("(c p m) -> c p m", c=NFC, p=128)
            w2f, _ = self.wv(L, "w2_%d" % s)
            w2v = w2f.rearrange("(p m) -> p m", p=128)
            w2sb = c.sb(es, [128, NFC * D], BF16, "w2sb")
            for q in range(4):
                m0, m1 = q * (NFC * D // 4), (q + 1) * (NFC * D // 4)
                c.dma("sync", w2sb[:, m0:m1], w2v[:, m0:m1], reads=[self.B_wb[L]], writes=[w2sb])
            w13sb = [c.sb(es, [128, 2 * 8 * 128], BF16, "w13sb") for _ in range(3)]
            hb = [c.sb(es, [128, 8, T], BF16, "hb") for _ in range(2)]
            act = [c.sb(es, [128, T], BF16, "act") for _ in range(NFC)]
            sgt = [c.sb(es, [128, T], F32, "sgt") for _ in range(2)]
            h32c = [c.sb(es, [128, T], F32, "h32c") for _ in range(3)]
            o = self.ln_setup(c, es)
            y, ob, t1, ybf = o.y, o.ob, o.t1, o.ybf
            pg = [c.ps(es, [128, T], F32, "pg") for _ in range(2)]
            pu = [c.ps(es, [128, T], F32, "pu") for _ in range(2)]
            pd = [c.ps(es, [128, T], F32, "pd") for _ in range(2)]
            if ple:
                wgf, _ = self.wv(L, "wg")
                wpf, _ = self.wv(L, "wp")
                wgsb = c.sb(es, [128, 8 * D], BF16, "wgsb")
                wpsb = c.sb(es, [128, 2 * D], BF16, "wpsb")
                c.dma("sync", wgsb[:], wgf.rearrange("(p m) -> p m", p=128), reads=[self.B_wb[L]], writes=[wgsb])
                c.dma("sync", wpsb[:], wpf.rearrange("(p m) -> p m", p=128), reads=[self.B_wb[L]], writes=[wpsb])
                ptb = [c.sb(es, [128, 2, T], BF16, "ptb") for _ in range(2)]
                sig = [c.sb(es, [128, T], F32, "sig") for _ in range(2)]
                et = [c.sb(es, [128, T], F32, "et") for _ in range(2)]
            lcol = (L * 3 + j_ln) * 8
            hTb_v = self.hTb.rearrange("(kc p) t -> p kc t", p=128)
            pTb_v = self.pTb.rearrange("l (kc p) t -> l p kc t", p=128)

            def load(i):
                c.dma("sync", hb[i % 2][:], hTb_v[:, :, i * T:(i + 1) * T], reads=[self.B_hTb], writes=[hb[i % 2]])
                if ple:
                    c.dma("sync", ptb[i % 2][:], pTb_v[L, :, :, i * T:(i + 1) * T], reads=[self.B_pTb],
                          writes=[ptb[i % 2]])

            def compute(i):
                hbi = hb[i % 2]
                tok = slice(i * T, (i + 1) * T)
                for cc in range(NFC):
                    wt = w13sb[cc % 3]
                    c.dma("sync", wt[:], w13v[cc], reads=[self.B_wb[L]], writes=[wt])
                    g_, u_ = pg[cc % 2], pu[cc % 2]
                    for kc in range(8):
                        c.op("tensor", lambda e, kc=kc: e.matmul(g_[:], lhsT=wt[:, kc * 128:(kc + 1) * 128],
                                                                 rhs=hbi[:, kc, :], start=(kc == 0), stop=(kc == 7)),
                             reads=[wt, hbi], writes=[g_])
                    for kc in range(8):
                        c.op("tensor", lambda e, kc=kc: e.matmul(u_[:], lhsT=wt[:, (8 + kc) * 128:(9 + kc) * 128],
                                                                 rhs=hbi[:, kc, :], start=(kc == 0), stop=(kc == 7)),
                             reads=[wt, hbi], writes=[u_])
                    sg = sgt[cc % 2]
                    c.op("scalar", lambda e: e.activation(out=sg[:], in_=g_[:], func=AF.Silu), reads=[g_], writes=[sg])
                    a_ = act[cc]
                    c.op("vector", lambda e: e.scalar_tensor_tensor(out=a_[:], in0=u_[:], scalar=0.5, in1=sg[:],
                                                                    op0=ALU.mult, op1=ALU.mult),
                         reads=[u_, sg], writes=[a_])
                for dc in range(8):
                    hr = h32c[dc % 3]
                    c.dma("sync", hr[:], src32[dc * 128:(dc + 1) * 128, tok], reads=[B_src32], writes=[hr])
                    p_ = pd[dc % 2]
                    for cc in range(NFC):
                        c.op("tensor", lambda e, cc=cc: e.matmul(
                            p_[:], lhsT=w2sb[:, cc * D + dc * 128: cc * D + (dc + 1) * 128], rhs=act[cc][:],
                            start=(cc == 0), stop=(cc == NFC - 1)), reads=[w2sb, act[cc]], writes=[p_])
                    yd = y[dc]
                    c.op("vector", lambda e: e.scalar_tensor_tensor(out=yd[:], in0=hr[:], scalar=float(ALPHA), in1=p_[:],
                                                                    op0=ALU.mult, op1=ALU.add),
                         reads=[hr, p_], writes=[yd])
                    self.ln_accum(c, o, dc)
                    if dc > 0:
                        self.ln_pe(c, o, dc - 1)
                self.ln_pe(c, o, 7)
                self.ln_finish(c, o, lcol, tok, store=not ple)
                if ple:
                    pti = ptb[i % 2]
                    for dc in range(8):
                        g_, u_ = pg[dc % 2], pu[dc % 2]
                        for kc in range(8):
                            c.op("tensor", lambda e, kc=kc: e.matmul(
                                g_[:], lhsT=wgsb[:, kc * D + dc * 128: kc * D + (dc + 1) * 128], rhs=ob[kc][:],
                                start=(kc == 0), stop=(kc == 7)), reads=[wgsb, ob[kc]], writes=[g_])
                        for kc in range(2):
                            c.op("tensor", lambda e, kc=kc: e.matmul(
                                u_[:], lhsT=wpsb[:, kc * D + dc * 128: kc * D + (dc + 1) * 128], rhs=pti[:, kc, :],
                                start=(kc == 0), stop=(kc == 1)), reads=[wpsb, pti], writes=[u_])
                        sg, e_ = sig[dc % 2], et[dc % 2]
                        c.op("scalar", lambda e: e.activation(out=sg[:], in_=g_[:], func=AF.Sigmoid), reads=[g_], writes=[sg])
                        c.op("vector", lambda e: e.tensor_tensor(out=e_[:], in0=u_[:], in1=sg[:], op=ALU.mult),
                             reads=[u_, sg], writes=[e_])
                        yd = y[dc]
                        a1 = t1[dc % 2]
                        c.op("gpsimd", lambda e: e.tensor_tensor(out=a1[:], in0=yd[:], in1=e_[:], op=ALU.add),
                             reads=[yd, e_], writes=[a1])
                        a2 = ybf[dc % 2]
                        c.op("scalar", lambda e: e.activation(out=a2[:], in_=a1[:], func=AF.Copy), reads=[a1], writes=[a2])
                        c.dma("gpsimd", self.outT[dc * 128:(dc + 1) * 128, tok], a1[:], reads=[a1], writes=[self.B_out])
                        c.dma("gpsimd", self.hTb[dc * 128:(dc + 1) * 128, tok], a2[:], reads=[a2], writes=[self.B_hTb])

            for i in range(NT + 1):
                if i < NT:
                    load(i)
                if i > 0:
                    compute(i - 1)
            c.barrier()

    def spload(self, c, es, name, eng="sync"):
        f, shape = self.spv(name)
        P = shape[0]
        m = int(np.prod(shape)) // P
        t = c.sb(es, [P, m], F32, name)
        c.dma(eng, t[:], f.rearrange("(p m) -> p m", p=P), writes=[t])
        return t

    def sin_tile(self, c, out, x, P, N, shift, tmp):
        a, xs, kf, m, ki = tmp["a"], tmp["xs"], tmp["kf"], tmp["m"], tmp["ki"]
        TWO_PI = 2.0 * math.pi
        C1 = 6.28125
        C2 = TWO_PI - C1
        V = lambda t: t[0:P, 0:N]
        c.op("vector", lambda e: e.tensor_scalar(out=V(a), in0=x, scalar1=float(shift), scalar2=None, op0=ALU.add),
             reads=[tmp["xb"]], writes=[a])
        c.op("vector", lambda e: e.tensor_scalar(out=V(xs), in0=V(a), scalar1=float(1.0 / TWO_PI), scalar2=None, op0=ALU.mult),
             reads=[a], writes=[xs])
        c.op("vector", lambda e: e.tensor_copy(out=V(ki), in_=V(xs)), reads=[xs], writes=[ki])
        c.op("vector", lambda e: e.tensor_copy(out=V(kf), in_=V(ki)), reads=[ki], writes=[kf])
        c.op("vector", lambda e: e.scalar_tensor_tensor(out=V(a), in0=V(kf), scalar=float(-C1), in1=V(a), op0=ALU.mult, op1=ALU.add),
             reads=[kf, a], writes=[a])
        c.op("vector", lambda e: e.scalar_tensor_tensor(out=V(a), in0=V(kf), scalar=float(-C2), in1=V(a), op0=ALU.mult, op1=ALU.add),
             reads=[kf, a], writes=[a])
        c.op("vector", lambda e: e.tensor_scalar(out=V(m), in0=V(a), scalar1=float(math.pi), scalar2=float(-TWO_PI),
                                                 op0=ALU.is_gt, op1=ALU.mult), reads=[a], writes=[m])
        c.op("vector", lambda e: e.tensor_tensor(out=V(a), in0=V(a), in1=V(m), op=ALU.add), reads=[a, m], writes=[a])
        c.op("vector", lambda e: e.tensor_scalar(out=V(m), in0=V(a), scalar1=float(-math.pi), scalar2=float(TWO_PI),
                                                 op0=ALU.is_lt, op1=ALU.mult), reads=[a], writes=[m])
        c.op("vector", lambda e: e.tensor_tensor(out=V(a), in0=V(a), in1=V(m), op=ALU.add), reads=[a, m], writes=[a])
        c.op("vector", lambda e: e.tensor_scalar(out=V(a), in0=V(a), scalar1=-3.1415925, scalar2=3.1415925,
                                                 op0=ALU.max, op1=ALU.min), reads=[a], writes=[a])
        c.op("scalar", lambda e: e.activation(out=out, in_=V(a), func=AF.Sin), reads=[a], writes=[tmp["ob"]])

    def even_in(self, c, L):
        j = L // 2
        with contextlib.ExitStack() as es:
            winf, _ = self.wv(L, "win")
            wsb = c.sb(es, [128, 8 * 2048], BF16, "winsb")
            wv_ = winf.rearrange("(p m) -> p m", p=128)
            for q in range(4):
                c.dma("sync", wsb[:, q * 4096:(q + 1) * 4096], wv_[:, q * 4096:(q + 1) * 4096], reads=[self.B_wb[L]], writes=[wsb])
            cw = self.spload(c, es, "convw%d" % j)
            hb = [c.sb(es, [128, 8, T], BF16, "hb") for _ in range(2)]
            pp = [c.ps(es, [128, T], F32, "pp") for _ in range(6)]
            hcs = [c.sb(es, [128, T], F32, "hcs") for _ in range(2)]
            ucx = [c.sb(es, [128, T + 2], F32, "ucx") for _ in range(4)]
            vt = [c.sb(es, [128, T], F32, "vt") for _ in range(2)]
            yab = [c.sb(es, [128, T], BF16, "yab") for _ in range(2)]
            u32 = [c.sb(es, [128, T], F32, "u32") for _ in range(2)]
            ubf = [c.sb(es, [128, T], BF16, "ubf") for _ in range(2)]
            for t_ in ucx:
                c.op("vector", lambda e: e.memset(t_[:], 0.0), writes=[t_])
            hTb_v = self.hTb.rearrange("(kc p) t -> p kc t", p=128)
            npp = [0]

            def proj(hbi, oc):
                p_ = pp[npp[0] % 6]
                npp[0] += 1
                for kc in range(8):
                    c.op("tensor", lambda e, kc=kc: e.matmul(p_[:], lhsT=wsb[:, kc * 2048 + oc * 128: kc * 2048 + (oc + 1) * 128],
                                                             rhs=hbi[:, kc, :], start=(kc == 0), stop=(kc == 7)),
                         reads=[wsb, hbi], writes=[p_])
                return p_

            def load(i):
                c.dma("sync", hb[i % 2][:], hTb_v[:, :, i * T:(i + 1) * T], reads=[self.B_hTb], writes=[hb[i % 2]])

            def compute(i):
                hbi = hb[i % 2]
                tok = slice(i * T, (i + 1) * T)
                for ch in range(4):
                    p_h = proj(hbi, ch)
                    p_c = proj(hbi, 8 + ch)
                    p_b = proj(hbi, 4 + ch)
                    hs = hcs[ch % 2]
                    c.op("scalar", lambda e: e.activation(out=hs[:], in_=p_h[:], func=AF.Copy), reads=[p_h], writes=[hs])
                    ux = ucx[ch]
                    c.op("vector", lambda e: e.tensor_tensor(out=ux[:, 2:T + 2], in0=hs[:], in1=p_c[:], op=ALU.mult),
                         reads=[hs, p_c], writes=[ux])
                    v_ = vt[ch % 2]
                    c.op("vector", lambda e: e.tensor_scalar(out=v_[:], in0=ux[:, 2:T + 2], scalar1=cw[:, ch * 3 + 2: ch * 3 + 3],
                                                             scalar2=None, op0=ALU.mult), reads=[ux, cw], writes=[v_])
                    c.op("vector", lambda e: e.scalar_tensor_tensor(out=v_[:], in0=ux[:, 1:T + 1], scalar=cw[:, ch * 3 + 1: ch * 3 + 2],
                                                                    in1=v_[:], op0=ALU.mult, op1=ALU.add), reads=[ux, cw, v_], writes=[v_])
                    c.op("vector", lambda e: e.scalar_tensor_tensor(out=v_[:], in0=ux[:, 0:T], scalar=cw[:, ch * 3: ch * 3 + 1],
                                                                    in1=v_[:], op0=ALU.mult, op1=ALU.add), reads=[ux, cw, v_], writes=[v_])
                    ya_ = yab[ch % 2]
                    c.op("vector", lambda e: e.tensor_tensor(out=ya_[:], in0=v_[:], in1=p_b[:], op=ALU.mult),
                         reads=[v_, p_b], writes=[ya_])
                    c.op("vector", lambda e: e.tensor_copy(out=ux[:, 0:2], in_=ux[:, T:T + 2]), reads=[ux], writes=[ux])
                    c.dma("gpsimd", self.yaTb[ch * 128:(ch + 1) * 128, tok], ya_[:], reads=[ya_], writes=[self.B_ya])
                for ch in range(4):
                    p_u = proj(hbi, 12 + ch)
                    a_, b_ = u32[ch % 2], ubf[ch % 2]
                    c.op("scalar", lambda e: e.activation(out=a_[:], in_=p_u[:], func=AF.Copy), reads=[p_u], writes=[a_])
                    c.op("scalar", lambda e: e.activation(out=b_[:], in_=p_u[:], func=AF.Copy), reads=[p_u], writes=[b_])
                    c.dma("gpsimd", self.uT32[ch * 128:(ch + 1) * 128, tok], a_[:], reads=[a_], writes=[self.B_u])
                    c.dma("gpsimd", self.uTb[ch * 128:(ch + 1) * 128, tok], b_[:], reads=[b_], writes=[self.B_u])

            for i in range(NT + 1):
                if i < NT:
                    load(i)
                if i > 0:
                    compute(i - 1)
            c.barrier()

    def even_s5(self, c, L):
        j = L // 2
        NJ = T + 1
        GK = math.sqrt(2.0 / math.pi)
        with contextlib.ExitStack() as es:
            sgn = self.spload(c, es, "sgn")
            ident = self.spload(c, es, "ident")
            swapP = self.spload(c, es, "swapP")
            dC = self.spload(c, es, "d_C%d" % j)
            magA = c.sb(es, [128, 32], F32, "magA")
            thA = c.sb(es, [128, 32], F32, "thA")
            LB1 = c.sb(es, [16, 32 * 128], BF16, "LB1")
            LB2 = c.sb(es, [16, 32 * 128], BF16, "LB2")
            LC1 = c.sb(es, [128, 512], BF16, "LC1")
            LC2 = c.sb(es, [128, 512], BF16, "LC2")
            Jf = c.sb(es, [128, NJ], F32, "Jf")
            with contextlib.ExitStack() as es2:
                lreA = self.spload(c, es2, "lamre_A%d" % j)
                limA = self.spload(c, es2, "lamim_A%d" % j)
                ldtA = self.spload(c, es2, "logdt_A%d" % j)
                dtA = c.sb(es2, [128, 32], F32, "dtA")
                c.op("scalar", lambda e: e.activation(out=dtA[:], in_=ldtA[:], func=AF.Exp), reads=[ldtA], writes=[dtA])
                c.op("vector", lambda e: e.tensor_scalar(out=lreA[:], in0=lreA[:], scalar1=-1e-4, scalar2=None, op0=ALU.min),
                     reads=[lreA], writes=[lreA])
                c.op("vector", lambda e: e.tensor_tensor(out=lreA[:], in0=lreA[:], in1=dtA[:], op=ALU.mult), reads=[lreA, dtA], writes=[lreA])
                c.op("scalar", lambda e: e.activation(out=magA[:], in_=lreA[:], func=AF.Exp), reads=[lreA], writes=[magA])
                c.op("vector", lambda e: e.tensor_tensor(out=thA[:], in0=limA[:], in1=dtA[:], op=ALU.mult), reads=[limA, dtA], writes=[thA])
                Ji = c.sb(es2, [128, NJ], I32, "Ji")
                c.op("gpsimd", lambda e: e.iota(Ji[:], pattern=[[1, NJ]], base=0, channel_multiplier=0), writes=[Ji])
                c.op("vector", lambda e: e.tensor_copy(out=Jf[:], in_=Ji[:]), reads=[Ji], writes=[Jf])
                lre = self.spload(c, es2, "lamre_C%d" % j)
                lim = self.spload(c, es2, "lamim_C%d" % j)
                ldt = self.spload(c, es2, "logdt_C%d" % j)
                br = self.spload(c, es2, "bre_C%d" % j)
                bi = self.spload(c, es2, "bim_C%d" % j)
                N = 512
                mk = lambda nm, dt=F32: c.sb(es2, [16, N], dt, nm)
                dt_, mag, ang, cs, sn = mk("dt"), mk("mag"), mk("ang"), mk("cs"), mk("sn")
                tmp = dict(a=mk("ta"), xs=mk("txs"), kf=mk("tkf"), m=mk("tm"), ki=mk("tki", I32))
                nr, ni, den, w1_, w2_, fre, fim = mk("nr"), mk("ni"), mk("den"), mk("w1"), mk("w2"), mk("fre"), mk("fim")
                bbre, bbim, lrq = mk("bbre"), mk("bbim"), mk("lrq")
                tt = lambda o_, a_, b_, op: c.op("vector", lambda e: e.tensor_tensor(out=o_[:], in0=a_[:], in1=b_[:], op=op),
                                                 reads=[a_, b_], writes=[o_])
                for gq in range(4):
                    sl = slice(gq * N, (gq + 1) * N)
                    c.op("scalar", lambda e: e.activation(out=dt_[:], in_=ldt[:, sl], func=AF.Exp), reads=[ldt], writes=[dt_])
                    c.op("vector", lambda e: e.tensor_scalar(out=lrq[:], in0=lre[:, sl], scalar1=-1e-4, scalar2=None, op0=ALU.min),
                         reads=[lre], writes=[lrq])
                    tt(mag, lrq, dt_, ALU.mult)
                    c.op("scalar", lambda e: e.activation(out=mag[:], in_=mag[:], func=AF.Exp), reads=[mag], writes=[mag])
                    c.op("vector", lambda e: e.tensor_tensor(out=ang[:], in0=lim[:, sl], in1=dt_[:], op=ALU.mult), reads=[lim, dt_], writes=[ang])
                    tmp["xb"] = ang
                    tmp["ob"] = cs
                    self.sin_tile(c, cs[:], ang[:], 16, N, math.pi / 2, tmp)
                    tmp["ob"] = sn
                    self.sin_tile(c, sn[:], ang[:], 16, N, 0.0, tmp)
                    tt(nr, mag, cs, ALU.mult)
                    c.op("vector", lambda e: e.tensor_scalar(out=nr[:], in0=nr[:], scalar1=-1.0, scalar2=None, op0=ALU.add), reads=[nr], writes=[nr])
                    tt(ni, mag, sn, ALU.mult)
                    tt(den, lrq, lrq, ALU.mult)
                    c.op("vector", lambda e: e.tensor_tensor(out=w1_[:], in0=lim[:, sl], in1=lim[:, sl], op=ALU.mult), reads=[lim], writes=[w1_])
                    tt(den, den, w1_, ALU.add)
                    c.op("vector", lambda e: e.reciprocal(out=den[:], in_=den[:]), reads=[den], writes=[den])
                    tt(w1_, nr, lrq, ALU.mult)
                    c.op("vector", lambda e: e.tensor_tensor(out=w2_[:], in0=ni[:], in1=lim[:, sl], op=ALU.mult), reads=[ni, lim], writes=[w2_])
                    tt(fre, w1_, w2_, ALU.add)
                    tt(fre, fre, den, ALU.mult)
                    tt(w1_, ni, lrq, ALU.mult)
                    c.op("vector", lambda e: e.tensor_tensor(out=w2_[:], in0=nr[:], in1=lim[:, sl], op=ALU.mult), reads=[nr, lim], writes=[w2_])
                    tt(fim, w1_, w2_, ALU.subtract)
                    tt(fim, fim, den, ALU.mult)
                    c.op("vector", lambda e: e.tensor_tensor(out=w1_[:], in0=fre[:], in1=br[:, sl], op=ALU.mult), reads=[fre, br], writes=[w1_])
                    c.op("vector", lambda e: e.tensor_tensor(out=w2_[:], in0=fim[:], in1=bi[:, sl], op=ALU.mult), reads=[fim, bi], writes=[w2_])
                    tt(bbre, w1_, w2_, ALU.subtract)
                    c.op("vector", lambda e: e.tensor_tensor(out=w1_[:], in0=fre[:], in1=bi[:, sl], op=ALU.mult), reads=[fre, bi], writes=[w1_])
                    c.op("vector", lambda e: e.tensor_tensor(out=w2_[:], in0=fim[:], in1=br[:, sl], op=ALU.mult), reads=[fim, br], writes=[w2_])
                    tt(bbim, w1_, w2_, ALU.add)
                    v3 = lambda t_: t_[:].rearrange("c (g p) -> c g p", g=8)
                    l3 = lambda t_, h: t_[:].rearrange("c (g m) -> c g m", g=32)[:, gq * 8:(gq + 1) * 8, h * 64:(h + 1) * 64]
                    c.op("vector", lambda e: e.tensor_copy(out=l3(LB1, 0), in_=v3(bbre)), reads=[bbre], writes=[LB1])
                    c.op("vector", lambda e: e.tensor_copy(out=l3(LB1, 1), in_=v3(bbim)), reads=[bbim], writes=[LB1])
                    c.op("vector", lambda e: e.tensor_copy(out=l3(LB2, 0), in_=v3(bbim)), reads=[bbim], writes=[LB2])
                    c.op("vector", lambda e: e.tensor_scalar(out=l3(LB2, 1), in0=v3(bbre), scalar1=-1.0, scalar2=None, op0=ALU.mult),
                         reads=[bbre], writes=[LB2])
                CA = self.spload(c, es2, "CA%d" % j)
                CB = self.spload(c, es2, "CB%d" % j)
                c.op("vector", lambda e: e.tensor_copy(out=LC1[0:64, :], in_=CA[0:64, :]), reads=[CA], writes=[LC1])
                c.op("vector", lambda e: e.tensor_scalar(out=LC1[64:128, :], in0=CA[64:128, :], scalar1=-1.0, scalar2=None, op0=ALU.mult),
                     reads=[CA], writes=[LC1])
                c.op("vector", lambda e: e.tensor_scalar(out=LC2[:], in0=CB[:], scalar1=-1.0, scalar2=None, op0=ALU.mult),
                     reads=[CB], writes=[LC2])
                c.barrier()
            NG = 2
            COS = [c.sb(es, [128, NJ], F32, "COS") for _ in range(2 * NG)]
            SIN = [c.sb(es, [128, NJ], F32, "SIN") for _ in range(2 * NG)]
            ANG = [c.sb(es, [128, NJ], F32, "ANG") for _ in range(2)]
            mkA = lambda nm, dt=F32: c.sb(es, [128, NJ], dt, nm)
            tmpA = dict(a=mkA("ta"), xs=mkA("txs"), kf=mkA("tkf"), m=mkA("tm"), ki=mkA("tki", I32))
            MAGT = [c.sb(es, [128, T], F32, "MAGT") for _ in range(2 * NG)]
            onesF = c.sb(es, [128, T], F32, "onesF")
            c.op("vector", lambda e: e.memset(onesF[:], 1.0), writes=[onesF])
            Rm = [c.sb(es, [128, 128], F32, "Rm") for _ in range(2 * NG)]
            ss = [c.sb(es, [128, 1], F32, "ss") for _ in range(2 * NG)]
            B2 = lambda shape, dt, nm, k=2: [[c.sb(es, shape, dt, nm) for _ in range(k)] for _ in range(NG)]
            ub = B2([16, T], BF16, "ub", 3)
            u32 = B2([16, T], F32, "u32", 3)
            P1 = [c.ps(es, [128, T], F32, "P1") for _ in range(NG)]
            P2 = [c.ps(es, [128, T], F32, "P2") for _ in range(NG)]
            py = [c.ps(es, [128, T], F32, "py") for _ in range(NG)]
            pc = [c.ps(es, [128, T], F32, "pc") for _ in range(NG)]
            m1 = B2([128, T], F32, "m1")
            m2 = B2([128, T], F32, "m2")
            vv = B2([128, T], F32, "vv")
            zz = B2([128, T], F32, "zz")
            q1 = B2([128, T], BF16, "q1")
            q2 = B2([128, T], BF16, "q2")
            init = B2([128, 1], F32, "init")
            yy = B2([16, T], F32, "yy")
            g1 = B2([16, T], F32, "g1")
            g2 = B2([16, T], F32, "g2")
            zo = B2([16, T], F32, "zo")
            zb = B2([16, T], BF16, "zb")

            def tables(g):
                k = g % (2 * NG)
                cs_, sn_, an_ = COS[k], SIN[k], ANG[g % 2]
                c.op("vector", lambda e: e.tensor_scalar(out=an_[:], in0=Jf[:], scalar1=thA[:, g:g + 1], scalar2=None, op0=ALU.mult),
                     reads=[Jf, thA], writes=[an_])
                tmpA["xb"] = an_
                tmpA["ob"] = cs_
                self.sin_tile(c, cs_[:], an_[:], 128, NJ, math.pi / 2, tmpA)
                tmpA["ob"] = sn_
                self.sin_tile(c, sn_[:], an_[:], 128, NJ, 0.0, tmpA)
                mg = MAGT[k]
                c.op("vector", lambda e: e.tensor_scalar(out=mg[:], in0=onesF[:], scalar1=magA[:, g:g + 1], scalar2=None, op0=ALU.mult),
                     reads=[onesF, magA], writes=[mg])
                R_, s_ = Rm[k], ss[k]
                c.op("vector", lambda e: e.tensor_tensor(out=s_[:], in0=sn_[:, T:T + 1], in1=sgn[:], op=ALU.mult), reads=[sn_, sgn], writes=[s_])
                c.op("vector", lambda e: e.tensor_scalar(out=R_[:], in0=ident[:], scalar1=cs_[:, T:T + 1], scalar2=None, op0=ALU.mult),
                     reads=[ident, cs_], writes=[R_])
                c.op("vector", lambda e: e.scalar_tensor_tensor(out=R_[:], in0=swapP[:], scalar=s_[:, 0:1], in1=R_[:], op0=ALU.mult, op1=ALU.add),
                     reads=[swapP, s_, R_], writes=[R_])

            def stageA(g, i):
                gi, k = g % NG, g % (2 * NG)
                tok = slice(i * T, (i + 1) * T)
                ub_, u32_ = ub[gi][i % 3], u32[gi][i % 3]
                c.dma("sync", ub_[:], self.uTb[g * 16:(g + 1) * 16, tok], reads=[self.B_u], writes=[ub_])
                c.dma("sync", u32_[:], self.uT32[g * 16:(g + 1) * 16, tok], reads=[self.B_u], writes=[u32_])
                p1, p2 = P1[gi], P2[gi]
                c.op("tensor", lambda e: e.matmul(p1[:], lhsT=LB1[:, g * 128:(g + 1) * 128], rhs=ub_[:], start=True, stop=True),
                     reads=[LB1, ub_], writes=[p1])
                c.op("tensor", lambda e: e.matmul(p2[:], lhsT=LB2[:, g * 128:(g + 1) * 128], rhs=ub_[:], start=True, stop=True),
                     reads=[LB2, ub_], writes=[p2])
                a_, b_, v_ = m1[gi][i % 2], m2[gi][i % 2], vv[gi][i % 2]
                cs_, sn_ = COS[k], SIN[k]
                c.op("vector", lambda e: e.tensor_tensor(out=a_[:], in0=cs_[:, 0:T], in1=p1[:], op=ALU.mult), reads=[cs_, p1], writes=[a_])
                c.op("vector", lambda e: e.tensor_tensor(out=b_[:], in0=sn_[:, 0:T], in1=p2[:], op=ALU.mult), reads=[sn_, p2], writes=[b_])
                c.op("gpsimd", lambda e: e.tensor_tensor(out=v_[:], in0=a_[:], in1=b_[:], op=ALU.add), reads=[a_, b_], writes=[v_])

            def stageB1(g, i):
                gi, k = g % NG, g % (2 * NG)
                v_, z_ = vv[gi][i % 2], zz[gi][i % 2]
                mg, R_, cs_, sn_ = MAGT[k], Rm[k], COS[k], SIN[k]
                if i == 0:
                    c.op("vector", lambda e: e.tensor_tensor_scan(out=z_[:], data0=mg[:], data1=v_[:], initial=0.0,
                                                                  op0=ALU.mult, op1=ALU.add), reads=[mg, v_], writes=[z_])
                else:
                    ini = init[gi][i % 2]
                    c.op("vector", lambda e: e.tensor_tensor_scan(out=z_[:], data0=mg[:], data1=v_[:], initial=ini[:, 0:1],
                                                                  op0=ALU.mult, op1=ALU.add), reads=[mg, v_, ini], writes=[z_])
                if i < NT - 1:
                    pc_ = pc[gi]
                    nini = init[gi][(i + 1) % 2]
                    c.op("tensor", lambda e: e.matmul(pc_[:, 0:2], lhsT=R_[:], rhs=z_[:, T - 2:T], start=True, stop=True),
                         reads=[R_, z_], writes=[pc_])
                    c.op("scalar", lambda e: e.activation(out=nini[:], in_=pc_[:, 1:2], func=AF.Copy), reads=[pc_], writes=[nini])
                qa, qb = q1[gi][i % 2], q2[gi][i % 2]
                c.op("gpsimd", lambda e: e.tensor_tensor(out=qa[:], in0=cs_[:, 0:T], in1=z_[:], op=ALU.mult), reads=[cs_, z_], writes=[qa])
                c.op("gpsimd", lambda e: e.tensor_tensor(out=qb[:], in0=sn_[:, 0:T], in1=z_[:], op=ALU.mult), reads=[sn_, z_], writes=[qb])

            def stageB2(g, i):
                gi = g % NG
                tok = slice(i * T, (i + 1) * T)
                qa, qb = q1[gi][i % 2], q2[gi][i % 2]
                u32_ = u32[gi][i % 3]
                py_ = py[gi]
                c.op("tensor", lambda e: e.matmul(py_[0:16, :], lhsT=LC1[:, g * 16:(g + 1) * 16], rhs=qa[:], start=True, stop=False),
                     reads=[LC1, qa], writes=[py_])
                c.op("tensor", lambda e: e.matmul(py_[0:16, :], lhsT=LC2[:, g * 16:(g + 1) * 16], rhs=qb[:], start=False, stop=True),
                     reads=[LC2, qb], writes=[py_])
                y_, ga, gb_, zo_, zb_ = yy[gi][i % 2], g1[gi][i % 2], g2[gi][i % 2], zo[gi][i % 2], zb[gi][i % 2]
                c.op("vector", lambda e: e.scalar_tensor_tensor(out=y_[:], in0=u32_[:], scalar=dC[:, g:g + 1], in1=py_[0:16, :],
                                                                op0=ALU.mult, op1=ALU.add), reads=[u32_, dC, py_], writes=[y_])
                c.op("scalar", lambda e: e.activation(out=ga[:], in_=y_[:], func=AF.Square), reads=[y_], writes=[ga])
                c.op("vector", lambda e: e.tensor_scalar(out=ga[:], in0=ga[:], scalar1=0.044715, scalar2=1.0, op0=ALU.mult, op1=ALU.add),
                     reads=[ga], writes=[ga])
                c.op("vector", lambda e: e.tensor_tensor(out=gb_[:], in0=ga[:], in1=y_[:], op=ALU.mult), reads=[ga, y_], writes=[gb_])
                c.op("scalar", lambda e: e.activation(out=gb_[:], in_=gb_[:], func=AF.Sigmoid, scale=float(2.0 * GK)), reads=[gb_], writes=[gb_])
                c.op("vector", lambda e: e.tensor_tensor(out=zo_[:], in0=gb_[:], in1=y_[:], op=ALU.mult), reads=[gb_, y_], writes=[zo_])
                c.op("scalar", lambda e: e.activation(out=zb_[:], in_=zo_[:], func=AF.Copy), reads=[zo_], writes=[zb_])
                c.dma("sync", self.zT32[g * 16:(g + 1) * 16, tok], zo_[:], reads=[zo_], writes=[self.B_z])
                c.dma("sync", self.zTb[g * 16:(g + 1) * 16, tok], zb_[:], reads=[zb_], writes=[self.B_z])

            for g in range(NG):
                tables(g)
            for gp in range(32 // NG):
                gs = [gp * NG + k for k in range(NG)]
                for g in gs:
                    stageA(g, 0)
                for i in range(NT):
                    if i + 1 < NT:
                        for g in gs:
                            stageA(g, i + 1)
                    for g in gs:
                        stageB1(g, i)
                    if i == 2 and gp + 1 < 32 // NG:
                        for g in gs:
                            tables(g + NG)
                    for g in gs:
                        stageB2(g, i)
            c.barrier()

    def even_out(self, c, L):
        j = L // 2
        with contextlib.ExitStack() as es:
            woutf, _ = self.wv(L, "wout")
            wgluf, _ = self.wv(L, "wglu")
            wout = c.sb(es, [128, 8 * D], BF16, "wout")
            wglu = c.sb(es, [128, 4 * 512], BF16, "wglu")
            c.dma("sync", wout[:], woutf.rearrange("(p m) -> p m", p=128), reads=[self.B_wb[L]], writes=[wout])
            c.dma("sync", wglu[:], wgluf.rearrange("(p m) -> p m", p=128), reads=[self.B_wb[L]], writes=[wglu])
            bglu = self.spload(c, es, "bglu%d" % j)
            o = self.ln_setup(c, es)
            zb = [c.sb(es, [128, 4, T], BF16, "zb") for _ in range(2)]
            z32 = [c.sb(es, [128, 4, T], F32, "z32") for _ in range(2)]
            yab = [c.sb(es, [128, 4, T], BF16, "yab") for _ in range(2)]
            ybb = [c.sb(es, [128, T], BF16, "ybb") for _ in range(4)]
            sig = [c.sb(es, [128, T], F32, "sig") for _ in range(2)]
            h32c = [c.sb(es, [128, T], F32, "h32c") for _ in range(3)]
            pg = [c.ps(es, [128, T], F32, "pg") for _ in range(2)]
            pd = [c.ps(es, [128, T], F32, "pd") for _ in range(2)]
            lcol = (L * 3 + 1) * 8
            v4 = lambda ap: ap.rearrange("(kc p) t -> p kc t", p=128)

            def load(i):
                tok = slice(i * T, (i + 1) * T)
                c.dma("sync", zb[i % 2][:], v4(self.zTb)[:, :, tok], reads=[self.B_z], writes=[zb[i % 2]])
                c.dma("sync", z32[i % 2][:], v4(self.zT32)[:, :, tok], reads=[self.B_z], writes=[z32[i % 2]])
                c.dma("sync", yab[i % 2][:], v4(self.yaTb)[:, :, tok], reads=[self.B_ya], writes=[yab[i % 2]])

            def compute(i):
                tok = slice(i * T, (i + 1) * T)
                zbi, z32i, yai = zb[i % 2], z32[i % 2], yab[i % 2]
                for oc in range(4):
                    g_ = pg[oc % 2]
                    for kc in range(4):
                        c.op("tensor", lambda e, kc=kc: e.matmul(g_[:], lhsT=wglu[:, kc * 512 + oc * 128: kc * 512 + (oc + 1) * 128],
                                                                 rhs=zbi[:, kc, :], start=(kc == 0), stop=(kc == 3)),
                             reads=[wglu, zbi], writes=[g_])
                    sg = sig[oc % 2]
                    c.op("scalar", lambda e: e.activation(out=sg[:], in_=g_[:], func=AF.Sigmoid, bias=bglu[:, oc:oc + 1]),
                         reads=[g_, bglu], writes=[sg])
                    yb_ = ybb[oc]
                    c.op("vector", lambda e: e.tensor_tensor(out=yb_[:], in0=z32i[:, oc, :], in1=sg[:], op=ALU.mult),
                         reads=[z32i, sg], writes=[yb_])
                for dc in range(8):
                    hr = h32c[dc % 3]
                    c.dma("sync", hr[:], self.outT[dc * 128:(dc + 1) * 128, tok], reads=[self.B_out], writes=[hr])
                    p_ = pd[dc % 2]
                    for kc in range(8):
                        rhs_b = yai if kc < 4 else ybb[kc - 4]
                        rhs = yai[:, kc, :] if kc < 4 else ybb[kc - 4][:]
                        c.op("tensor", lambda e, kc=kc, rhs=rhs: e.matmul(
                            p_[:], lhsT=wout[:, kc * D + dc * 128: kc * D + (dc + 1) * 128], rhs=rhs,
                            start=(kc == 0), stop=(kc == 7)), reads=[wout, rhs_b], writes=[p_])
                    yd = o.y[dc]
                    c.op("vector", lambda e: e.scalar_tensor_tensor(out=yd[:], in0=hr[:], scalar=float(ALPHA), in1=p_[:],
                                                                    op0=ALU.mult, op1=ALU.add), reads=[hr, p_], writes=[yd])
                    self.ln_accum(c, o, dc)
                    if dc > 0:
                        self.ln_pe(c, o, dc - 1)
                self.ln_pe(c, o, 7)
                self.ln_finish(c, o, lcol, tok, store=True)

            for i in range(NT + 1):
                if i < NT:
                    load(i)
                if i > 0:
                    compute(i - 1)
            c.barrier()

    def rope_tables(self, c):
        CH = 512
        with contextlib.ExitStack() as es:
            rc = self.spload(c, es, "ropec")
            posi = [c.sb(es, [128, CH], I32, "posi") for _ in range(2)]
            posf = [c.sb(es, [128, CH], F32, "posf") for _ in range(2)]
            ang = [c.sb(es, [128, CH], F32, "ang") for _ in range(2)]
            outt = [c.sb(es, [128, CH], F32, "outt") for _ in range(4)]
            mk = lambda nm, dt=F32: c.sb(es, [128, CH], dt, nm)
            tmp = dict(a=mk("ta"), xs=mk("txs"), kf=mk("tkf"), m=mk("tm"), ki=mk("tki", I32))
            n = 0
            for i in range(S // CH):
                sl = slice(i * CH, (i + 1) * CH)
                pi_, pf_ = posi[i % 2], posf[i % 2]
                c.dma("sync", pi_[:], self.posb[:, sl], writes=[pi_])
                c.op("vector", lambda e: e.tensor_copy(out=pf_[:], in_=pi_[:]), reads=[pi_], writes=[pf_])
                for ty in range(2):
                    an_ = ang[ty]
                    c.op("vector", lambda e: e.tensor_scalar(out=an_[:], in0=pf_[:], scalar1=rc[:, 2 * ty:2 * ty + 1], scalar2=None,
                                                             op0=ALU.mult), reads=[pf_, rc], writes=[an_])
                    tmp["xb"] = an_
                    oc_ = outt[n % 4]
                    n += 1
                    tmp["ob"] = oc_
                    self.sin_tile(c, oc_[:], an_[:], 128, CH, math.pi / 2, tmp)
                    c.dma("gpsimd", self.tabs[2 * ty, :, sl], oc_[:], reads=[oc_], writes=[self.B_tabs])
                    os_ = outt[n % 4]
                    n += 1
                    tmp["ob"] = os_
                    self.sin_tile(c, os_[:], an_[:], 128, CH, 0.0, tmp)
                    c.op("vector", lambda e: e.tensor_scalar(out=os_[:], in0=os_[:], scalar1=rc[:, 2 * ty + 1:2 * ty + 2], scalar2=None,
                                                             op0=ALU.mult), reads=[os_, rc], writes=[os_])
                    c.dma("gpsimd", self.tabs[2 * ty + 1, :, sl], os_[:], reads=[os_], writes=[self.B_tabs])
            c.barrier()

    def odd_in(self, c, L):
        with contextlib.ExitStack() as es:
            cinf, _ = self.wv(L, "cin")
            cswf, _ = self.wv(L, "cinsw")
            NM, NS = 2176, 1920
            wm = c.sb(es, [128, 8 * NM], BF16, "wm")
            ws = c.sb(es, [128, 8 * NS], BF16, "ws")
            wmv = cinf.rearrange("(p m) -> p m", p=128)
            wsv = cswf.rearrange("(p m) -> p m", p=128)
            for q in range(4):
                c.dma("sync", wm[:, q * 2 * NM:(q + 1) * 2 * NM], wmv[:, q * 2 * NM:(q + 1) * 2 * NM], reads=[self.B_wb[L]], writes=[wm])
                c.dma("sync", ws[:, q * 2 * NS:(q + 1) * 2 * NS], wsv[:, q * 2 * NS:(q + 1) * 2 * NS], reads=[self.B_wb[L]], writes=[ws])
            hb = [c.sb(es, [128, 8, T], BF16, "hb") for _ in range(2)]
            tb = [c.sb(es, [128, 4, T], F32, "tb") for _ in range(2)]
            pm = [c.ps(es, [128, T], F32, "pm") for _ in range(3)]
            psw = [c.ps(es, [128, T], F32, "psw") for _ in range(3)]
            pvw = [c.ps(es, [128, 512], F32, "pvw") for _ in range(2)]
            ta = [c.sb(es, [128, T], F32, "ta") for _ in range(2)]
            tb2 = [c.sb(es, [128, T], F32, "tb2") for _ in range(2)]
            ro = [c.sb(es, [128, T], BF16, "ro") for _ in range(3)]
            vo = [c.sb(es, [128, 256], BF16, "vo") for _ in range(2)]
            wo = [c.sb(es, [128, 8], F32, "wo") for _ in range(2)]
            hTb_v = self.hTb.rearrange("(kc p) t -> p kc t", p=128)
            tabs_v = self.tabs.rearrange("f p t -> p f t")
            chunks = []
            for h in range(8):
                chunks.append((self.qT[h * 128:(h + 1) * 128], h * 128, h, 0, 128))
            for g in range(2):
                chunks.append((self.kT[g * 128:(g + 1) * 128], 1024 + g * 128, 8 + g, 0, 128))
            for q in range(4):
                chunks.append((self.qiT[q * 128:(q + 1) * 128], 1536 + q * 128, 10 + q, 1, 128))
            chunks.append((None, 2048, 14, 1, 64))
            cnt = [0]

            def load(i):
                tok = slice(i * T, (i + 1) * T)
                c.dma("sync", hb[i % 2][:], hTb_v[:, :, tok], reads=[self.B_hTb], writes=[hb[i % 2]])
                c.dma("sync", tb[i % 2][:], tabs_v[:, :, tok], reads=[self.B_tabs], writes=[tb[i % 2]])

            def compute(i):
                tok = slice(i * T, (i + 1) * T)
                hbi, tbi = hb[i % 2], tb[i % 2]
                for (dst, mo, sc, ty, rows) in chunks:
                    k = cnt[0]
                    cnt[0] += 1
                    p_m, p_s = pm[k % 3], psw[k % 3]
                    for kc in range(8):
                        c.op("tensor", lambda e, kc=kc: e.matmul(p_m[0:rows, :], lhsT=wm[:, kc * NM + mo: kc * NM + mo + rows],
                                                                 rhs=hbi[:, kc, :], start=(kc == 0), stop=(kc == 7)),
                             reads=[wm, hbi], writes=[p_m])
                    for kc in range(8):
                        c.op("tensor", lambda e, kc=kc: e.matmul(p_s[0:rows, :], lhsT=ws[:, kc * NS + sc * 128: kc * NS + sc * 128 + rows],
                                                                 rhs=hbi[:, kc, :], start=(kc == 0), stop=(kc == 7)),
                             reads=[ws, hbi], writes=[p_s])
                    a_, b_, r_ = ta[k % 2], tb2[k % 2], ro[k % 3]
                    c.op("vector", lambda e: e.tensor_tensor(out=a_[0:rows, :], in0=tbi[0:rows, 2 * ty, :], in1=p_m[0:rows, :], op=ALU.mult),
                         reads=[tbi, p_m], writes=[a_])
                    c.op("vector", lambda e: e.tensor_tensor(out=b_[0:rows, :], in0=tbi[0:rows, 2 * ty + 1, :], in1=p_s[0:rows, :], op=ALU.mult),
                         reads=[tbi, p_s], writes=[b_])
                    c.op("gpsimd", lambda e: e.tensor_tensor(out=r_[0:rows, :], in0=a_[0:rows, :], in1=b_[0:rows, :], op=ALU.add),
                         reads=[a_, b_], writes=[r_])
                    if dst is not None:
                        c.dma("gpsimd", dst[:, tok], r_[:], reads=[r_], writes=[self.B_qkv])
                    else:
                        c.dma("gpsimd", self.kiT[0:64, tok], r_[0:64, :], reads=[r_], writes=[self.B_qkv])
                        c.dma("gpsimd", self.kiT[64:128, tok], r_[0:64, :], reads=[r_], writes=[self.B_qkv])
                for sub in range(4):
                    p_vw = pvw[sub % 2]
                    for kc in range(8):
                        c.op("tensor", lambda e, kc=kc: e.matmul(p_vw[:, 0:256], lhsT=hbi[:, kc, sub * 128:(sub + 1) * 128],
                                                                 rhs=wm[:, kc * NM + 1280: kc * NM + 1536], start=(kc == 0), stop=(kc == 7)),
                             reads=[wm, hbi], writes=[p_vw])
                    for kc in range(8):
                        c.op("tensor", lambda e, kc=kc: e.matmul(p_vw[:, 256:264], lhsT=hbi[:, kc, sub * 128:(sub + 1) * 128],
                                                                 rhs=wm[:, kc * NM + 2112: kc * NM + 2120], start=(kc == 0), stop=(kc == 7)),
                             reads=[wm, hbi], writes=[p_vw])
                    v_, w_ = vo[sub % 2], wo[sub % 2]
                    c.op("scalar", lambda e: e.activation(out=v_[:], in_=p_vw[:, 0:256], func=AF.Copy), reads=[p_vw], writes=[v_])
                    c.op("scalar", lambda e: e.activation(out=w_[:], in_=p_vw[:, 256:264], func=AF.Copy, scale=float(512 ** -0.5)), reads=[p_vw], writes=[w_])
                    t0 = i * T + sub * 128
                    c.dma("gpsimd", self.vtok[t0:t0 + 128, :], v_[:], reads=[v_], writes=[self.B_qkv])
                    c.dma("gpsimd", self.witok[t0:t0 + 128, :], w_[:], reads=[w_], writes=[self.B_qkv])

            for i in range(NT + 1):
                if i < NT:
                    load(i)
                if i > 0:
                    compute(i - 1)
            c.barrier()

    def odd_attn(self, c, L, nqs=None):
        QS = 256
        NQS = S // QS
        NIT = 16
        with contextlib.ExitStack() as es:
            KT = c.sb(es, [128, 2, S], BF16, "KT")
            V = c.sb(es, [128, 64, 256], BF16, "V")
            kT_v = self.kT.rearrange("(g p) t -> p g t", p=128)
            v_v = self.vtok.rearrange("(kb p) c -> p kb c", p=128)
            for q in range(4):
                c.dma("sync", KT[:, :, q * 2048:(q + 1) * 2048], kT_v[:, :, q * 2048:(q + 1) * 2048], reads=[self.B_qkv], writes=[KT])
                c.dma("sync", V[:, q * 16:(q + 1) * 16, :], v_v[:, q * 16:(q + 1) * 16, :], reads=[self.B_qkv], writes=[V])
            identf = self.spload(c, es, "ident")
            Wneg = self.spload(c, es, "Wneg")
            Wpos = self.spload(c, es, "Wpos")
            identb = c.sb(es, [128, 128], BF16, "identb")
            onesb = c.sb(es, [128, 128], BF16, "onesb")
            c.op("vector", lambda e: e.tensor_copy(out=identb[:], in_=identf[:]), reads=[identf], writes=[identb])
            c.op("vector", lambda e: e.memset(onesb[:], 1.0), writes=[onesb])
            row = c.sb(es, [128, S], F32, "row")
            msk = c.sb(es, [128, S], BF16, "msk")
            maskT = c.sb(es, [128, 64, QS], BF16, "maskT")
            kib = [c.sb(es, [128, 512], BF16, "kib") for _ in range(3)]
            qib = [c.sb(es, [128, 4, 128], BF16, "qib") for _ in range(2)]
            wib = [c.sb(es, [128, 8], F32, "wib") for _ in range(2)]
            dg = [c.sb(es, [128, 128], BF16, "dg") for _ in range(8)]
            rl = [c.sb(es, [128, 512], BF16, "rl") for _ in range(3)]
            tmpd = c.sb(es, [128, 512], F32, "tmpd")
            st8 = c.sb(es, [128, 8], F32, "st8")
            pow2 = self.spload(c, es, "pow2")
            Wt = c.sb(es, [128, NIT], F32, "Wt")
            NW = c.sb(es, [128, NIT], F32, "NW")
            sc = {k: c.sb(es, [128, 1], F32, k) for k in ["lo", "hi", "mid", "cnt", "ge", "d1", "d2", "dmin", "omin"]}
            qTb = [c.sb(es, [128, 8, QS], BF16, "qTb") for _ in range(2)]
            ex = [c.sb(es, [128, QS], BF16, "ex") for _ in range(3)]
            pb = [c.sb(es, [128, QS], BF16, "pb") for _ in range(3)]
            oev = [c.sb(es, [128, QS], F32, "oev") for _ in range(2)]
            lev = [c.sb(es, [1, QS], F32, "lev") for _ in range(2)]
            plg = [c.ps(es, [128, 512], F32, "plg") for _ in range(2)]
            psc = [c.ps(es, [128, 512], F32, "psc") for _ in range(2)]
            ptm = c.ps(es, [128, 1024], BF16, "ptm")
            plb = [c.ps(es, [128, 512], F32, "plb") for _ in range(2)]
            pst, po, pl = plg, psc, plb
            qiT_v = self.qiT.rearrange("(q p) t -> p q t", p=128)
            qT_v = self.qT.rearrange("(h p) t -> p h t", p=128)
            nk = [0]
            nh = [0]
            for Qs in range(NQS if nqs is None else nqs):
                q0 = Qs * QS
                qt = qTb[Qs % 2]
                c.dma("sync", qt[:], qT_v[:, :, q0:q0 + QS], reads=[self.B_qkv], writes=[qt])
                c.op("gpsimd", lambda e: e.memset(maskT[:, 2 * Qs + 1, 0:128], 0.0), writes=[maskT])
                for b in range(2):
                    i = 2 * Qs + b
                    t0 = i * 128
                    nkb = i // 4 + 1
                    n = nkb * 512
                    qi_, wi_ = qib[i % 2], wib[i % 2]
                    c.dma("sync", qi_[:], qiT_v[:, :, t0:t0 + 128], reads=[self.B_qkv], writes=[qi_])
                    c.dma("sync", wi_[:], self.witok[t0:t0 + 128, :], reads=[self.B_qkv], writes=[wi_])
                    for h in range(8):
                        c.op("vector", lambda e, h=h: e.tensor_scalar(out=dg[h][:], in0=identb[:], scalar1=wi_[:, h:h + 1], scalar2=None,
                                                                      op0=ALU.mult), reads=[identb, wi_], writes=[dg[h]])
                    kis, pss = {}, []

                    def kiload(kb):
                        if kb >= nkb:
                            return
                        ki_ = kib[kb % 3]
                        c.dma("sync", ki_[:], self.kiT[:, kb * 512:(kb + 1) * 512], reads=[self.B_qkv], writes=[ki_])
                        kis[kb] = ki_

                    for kb in range(nkb):
                        pss.append(psc[nk[0] % 2])
                        nk[0] += 1
                    kiload(0)
                    kiload(1)
                    units = [(kb, h) for kb in range(nkb) for h in range(8)]

                    def logit(u):
                        kb, h = units[u]
                        pl_ = plg[u % 2]
                        r0 = (h % 2) * 64
                        c.op("tensor", lambda e: e.matmul(pl_[:], lhsT=qi_[r0:r0 + 64, h // 2, :], rhs=kis[kb][r0:r0 + 64, :],
                                                          start=True, stop=True), reads=[qi_, kis[kb]], writes=[pl_])

                    if LA_IDX:
                        logit(0)
                    for u, (kb, h) in enumerate(units):
                        if h == 6:
                            kiload(kb + 2)
                        if LA_IDX:
                            if u + 1 < len(units):
                                logit(u + 1)
                        else:
                            logit(u)
                        pl_, r_, ps_ = plg[u % 2], rl[u % 3], pss[kb]
                        c.op("scalar", lambda e: e.activation(out=r_[:], in_=pl_[:], func=AF.Relu), reads=[pl_], writes=[r_])
                        c.op("tensor", lambda e: e.matmul(ps_[:], lhsT=dg[h][:], rhs=r_[:], start=(h == 0), stop=(h == 7)),
                             reads=[dg[h], r_], writes=[ps_])
                        if h != 7:
                            continue
                        ksl = slice(kb * 512, (kb + 1) * 512)
                        if kb == nkb - 1:
                            v = i % 4
                            wsl = slice((3 - v) * 128, (3 - v) * 128 + 512)
                            c.op("vector", lambda e: e.tensor_tensor(out=row[:, ksl], in0=Wneg[:, wsl], in1=ps_[:], op=ALU.add),
                                 reads=[Wneg, ps_], writes=[row])
                            c.op("vector", lambda e: e.tensor_tensor(out=tmpd[:], in0=Wpos[:, wsl], in1=ps_[:], op=ALU.add),
                                 reads=[Wpos, ps_], writes=[tmpd])
                            c.op("vector", lambda e: e.tensor_reduce(out=sc["dmin"][:], in_=tmpd[:], axis=AX.X, op=ALU.min),
                                 reads=[tmpd], writes=[sc["dmin"]])
                        else:
                            c.op("vector", lambda e: e.tensor_copy(out=row[:, ksl], in_=ps_[:]), reads=[ps_], writes=[row])
                    c.op("vector", lambda e: e.max(out=st8[:], in_=row[:, 0:n]), reads=[row], writes=[st8])
                    c.op("vector", lambda e: e.tensor_copy(out=sc["hi"][:], in_=st8[:, 0:1]), reads=[st8], writes=[sc["hi"]])
                    if nkb > 1:
                        c.op("vector", lambda e: e.tensor_reduce(out=sc["omin"][:], in_=row[:, 0:n - 512], axis=AX.X, op=ALU.min),
                             reads=[row], writes=[sc["omin"]])
                        c.op("vector", lambda e: e.tensor_tensor(out=sc["lo"][:], in0=sc["dmin"][:], in1=sc["omin"][:], op=ALU.min),
                             reads=[sc["dmin"], sc["omin"]], writes=[sc["lo"]])
                    else:
                        c.op("vector", lambda e: e.tensor_copy(out=sc["lo"][:], in_=sc["dmin"][:]), reads=[sc["dmin"]], writes=[sc["lo"]])
                    if i >= 2:
                        lo, hi, mid, cnt, tq = (sc[k] for k in ["lo", "hi", "mid", "cnt", "ge"])
                        c.op("vector", lambda e: e.tensor_tensor(out=sc["d1"][:], in0=hi[:], in1=lo[:], op=ALU.subtract), reads=[hi, lo], writes=[sc["d1"]])
                        c.op("vector", lambda e: e.tensor_scalar(out=Wt[:], in0=pow2[:], scalar1=sc["d1"][:, 0:1], scalar2=None, op0=ALU.mult),
                             reads=[pow2, sc["d1"]], writes=[Wt])
                        c.op("vector", lambda e: e.tensor_scalar(out=NW[:], in0=Wt[:], scalar1=-0.5, scalar2=None, op0=ALU.mult), reads=[Wt], writes=[NW])
                        c.op("vector", lambda e: e.tensor_tensor(out=mid[:], in0=lo[:], in1=Wt[:, 0:1], op=ALU.add), reads=[lo, Wt], writes=[mid])
                        for it in range(NIT):
                            c.op("vector", lambda e: e.tensor_scalar(out=msk[:, 0:n], in0=row[:, 0:n], scalar1=mid[:, 0:1], scalar2=None,
                                                                     op0=ALU.is_ge, op1=ALU.add, accum_out=cnt[:]),
                                 reads=[row, mid], writes=[msk, cnt])
                            c.op("vector", lambda e, it=it: e.tensor_scalar(out=tq[:], in0=cnt[:], scalar1=255.5, scalar2=Wt[:, it:it + 1],
                                                                            op0=ALU.is_ge, op1=ALU.mult), reads=[cnt, Wt], writes=[tq])
                            c.op("vector", lambda e, it=it: e.scalar_tensor_tensor(out=mid[:], in0=tq[:], scalar=NW[:, it:it + 1], in1=mid[:],
                                                                                   op0=ALU.add, op1=ALU.add), reads=[tq, NW, mid], writes=[mid])
                        c.op("vector", lambda e: e.tensor_tensor(out=lo[:], in0=mid[:], in1=NW[:, NIT - 1:NIT], op=ALU.add), reads=[mid, NW], writes=[lo])
                    nv = (i + 1) * 128
                    c.op("vector", lambda e: e.tensor_scalar(out=msk[:, 0:nv], in0=row[:, 0:nv], scalar1=sc["lo"][:, 0:1], scalar2=None,
                                                             op0=ALU.is_ge), reads=[row, sc["lo"]], writes=[msk])
                    for k4 in range(0, i + 1, 4):
                        m4 = min(4, i + 1 - k4)
                        for kk in range(m4):
                            c.op("tensor", lambda e, kk=kk: e.transpose(out=ptm[:, kk * 128:(kk + 1) * 128],
                                                                        in_=msk[:, (k4 + kk) * 128:(k4 + kk + 1) * 128], identity=identb[:]),
                                 reads=[msk, identb], writes=[ptm])
                        c.op("scalar", lambda e: e.activation(
                            out=maskT[:, k4:k4 + m4, b * 128:(b + 1) * 128],
                            in_=ptm[:, 0:m4 * 128].rearrange("p (k q) -> p k q", k=m4), func=AF.Copy), reads=[ptm], writes=[maskT])
                nkk = 2 * Qs + 2
                aunits = [(h, kk) for h in range(8) for kk in range(nkk)]

                def smm(u):
                    h, kk = aunits[u]
                    st_ = pst[u % 2]
                    c.op("tensor", lambda e: e.matmul(st_[:, 0:QS], lhsT=KT[:, h // 4, kk * 128:(kk + 1) * 128], rhs=qt[:, h, :], start=True, stop=True),
                         reads=[KT, qt], writes=[st_])

                if LA_ATT:
                    smm(0)
                for u, (h, kk) in enumerate(aunits):
                    if LA_ATT:
                        if u + 1 < len(aunits):
                            smm(u + 1)
                    else:
                        smm(u)
                    g = h // 4
                    po_, pl_ = po[h % 2], pl[h % 2]
                    st_, e_, p_ = pst[u % 2], ex[u % 3], pb[u % 3]
                    c.op("scalar", lambda e: e.activation(out=e_[:], in_=st_[:, 0:QS], func=AF.Exp, scale=float(128 ** -0.5)), reads=[st_], writes=[e_])
                    meng = "gpsimd" if u % 2 == 0 else "vector"
                    c.op(meng, lambda e: e.tensor_tensor(out=p_[:], in0=e_[:], in1=maskT[:, kk, :], op=ALU.mult),
                         reads=[e_, maskT], writes=[p_])
                    c.op("tensor", lambda e: e.matmul(po_[:, 0:QS], lhsT=V[:, kk, g * 128:(g + 1) * 128], rhs=p_[:], start=(kk == 0), stop=(kk == nkk - 1)),
                         reads=[V, p_], writes=[po_])
                    c.op("tensor", lambda e: e.matmul(pl_[:, 0:QS], lhsT=onesb[:], rhs=p_[:], start=(kk == 0), stop=(kk == nkk - 1)),
                         reads=[onesb, p_], writes=[pl_])
                    if kk != nkk - 1:
                        continue
                    o_, l_ = oev[h % 2], lev[h % 2]
                    c.op("scalar", lambda e: e.activation(out=o_[:], in_=po_[:, 0:QS], func=AF.Copy), reads=[po_], writes=[o_])
                    c.op("scalar", lambda e: e.activation(out=l_[:], in_=pl_[0:1, 0:QS], func=AF.Copy), reads=[pl_], writes=[l_])
                    c.dma("sync", self.attT[h * 128:(h + 1) * 128, q0:q0 + QS], o_[:], reads=[o_], writes=[self.B_att])
                    c.dma("sync", self.lden[h:h + 1, q0:q0 + QS], l_[:], reads=[l_], writes=[self.B_att])
            c.barrier()

    def odd_out(self, c, L):
        with contextlib.ExitStack() as es:
            coutf, _ = self.wv(L, "cout")
            wout = c.sb(es, [128, 8 * D], BF16, "wout")
            c.dma("sync", wout[:], coutf.rearrange("(p m) -> p m", p=128), reads=[self.B_wb[L]], writes=[wout])
            o = self.ln_setup(c, es)
            at = [c.sb(es, [128, 8, T], F32, "at") for _ in range(2)]
            ld = [c.sb(es, [128, 8, T], F32, "ld") for _ in range(2)]
            ab = [c.sb(es, [128, T], BF16, "ab") for _ in range(8)]
            h32c = [c.sb(es, [128, T], F32, "h32c") for _ in range(3)]
            pd = [c.ps(es, [128, T], F32, "pd") for _ in range(2)]
            lcol = (L * 3 + 1) * 8
            at_v = self.attT.rearrange("(kc p) t -> p kc t", p=128)

            def load(i):
                tok = slice(i * T, (i + 1) * T)
                c.dma("sync", at[i % 2][:], at_v[:, :, tok], reads=[self.B_att], writes=[at[i % 2]])
                for h in range(8):
                    c.dma("sync", ld[i % 2][:, h, :], self.lden[h:h + 1, tok].partition_broadcast(128), reads=[self.B_att], writes=[ld[i % 2]])

            def compute(i):
                tok = slice(i * T, (i + 1) * T)
                ati, ldi = at[i % 2], ld[i % 2]
                c.op("vector", lambda e: e.reciprocal(out=ldi[:], in_=ldi[:]), reads=[ldi], writes=[ldi])
                for h in range(8):
                    eng = "vector" if h % 2 == 0 else "gpsimd"
                    c.op(eng, lambda e: e.tensor_tensor(out=ab[h][:], in0=ati[:, h, :], in1=ldi[:, h, :], op=ALU.mult),
                         reads=[ati, ldi], writes=[ab[h]])
                for dc in range(8):
                    hr = h32c[dc % 3]
                    c.dma("sync", hr[:], self.outT[dc * 128:(dc + 1) * 128, tok], reads=[self.B_out], writes=[hr])
                    p_ = pd[dc % 2]
                    for kc in range(8):
                        c.op("tensor", lambda e, kc=kc: e.matmul(p_[:], lhsT=wout[:, kc * D + dc * 128: kc * D + (dc + 1) * 128], rhs=ab[kc][:],
                                                                 start=(kc == 0), stop=(kc == 7)), reads=[wout, ab[kc]], writes=[p_])
                    yd = o.y[dc]
                    c.op("vector", lambda e: e.scalar_tensor_tensor(out=yd[:], in0=hr[:], scalar=float(ALPHA), in1=p_[:],
                                                                    op0=ALU.mult, op1=ALU.add), reads=[hr, p_], writes=[yd])
                    self.ln_accum(c, o, dc)
                    if dc > 0:
                        self.ln_pe(c, o, dc - 1)
                self.ln_pe(c, o, 7)
                self.ln_finish(c, o, lcol, tok, store=True)

            for i in range(NT + 1):
                if i < NT:
                    load(i)
                if i > 0:
                    compute(i - 1)
            c.barrier()

    def build(self):
        nc = self.nc
        with contextlib.ExitStack() as es:
            c = Ctx(nc, es)
            self.cast_weights(c)
            self.rope_tables(c)
            stages = []
            if self.stop_after == ("pro", 0):
                return nc
            for L in range(DEPTH):
                stages.append(("f0", L))
                stages.append(("mix", L))
                stages.append(("f1", L))
            first = True
            if self.start_at is not None:
                stages = stages[stages.index(self.start_at):]
                first = False
                for r0 in range(0, D, 128):
                    c.dma("sync", self.outT[r0:r0 + 128, :], self.xT[r0:r0 + 128, :], writes=[self.B_out])
                c.barrier()
            for kind, L in stages:
                if kind == "f0":
                    src, bsrc = (self.xT, Buf(self.xT, "xT")) if first else (self.outT, self.B_out)
                    self.ffn(c, L, 0, src, bsrc, ple=False)
                    first = False
                elif kind == "f1":
                    self.ffn(c, L, 1, self.outT, self.B_out, ple=True)
                elif kind == "mix" and L % 2 == 0:
                    self.even_in(c, L)
                    if self.stop_after == ("ein", L):
                        break
                    self.even_s5(c, L)
                    if self.stop_after == ("es5", L):
                        break
                    self.even_out(c, L)
                elif kind == "mix":
                    self.odd_in(c, L)
                    if self.stop_after == ("oin", L):
                        break
                    self.odd_attn(c, L, nqs=self.nqs)
                    if self.stop_after == ("oat", L):
                        break
                    self.odd_out(c, L)
                if self.stop_after == (kind, L):
                    break
            c.barrier()
        return nc


def _prep_inputs(inputs, b):
    m = {}
    m["xT"] = np.ascontiguousarray(np.asarray(inputs["x"][b], np.float32).T)
    m["pT"] = np.ascontiguousarray(np.asarray(inputs["p"][:, b], np.float32).transpose(0, 2, 1))
    m["posb"] = np.ascontiguousarray(np.broadcast_to(np.asarray(inputs["positions"][b], np.int32)[None, :], (128, S)))
    return m


def _shared_inputs(inputs):
    m = {}
    for L in range(DEPTH):
        m["wl%d" % L] = _layer_blob_layout(L, inputs).finish()
    m["sp"] = _small_params(inputs).finish()
    return m


def kernel(**inputs):
    prog = Prog()
    nc = prog.build()
    shared = _shared_inputs(inputs)
    in_maps = []
    for core in range(NCORES):
        m = dict(shared)
        m.update(_prep_inputs(inputs, core % 4))
        in_maps.append(m)
    res = run_bass_kernel_spmd(nc, in_maps, core_ids=list(range(NCORES)))
    out = np.stack([np.ascontiguousarray(res.results[b]["outT"].T) for b in range(4)], axis=0)
    return out.astype(np.float32)
```
